# Optimizing a Trainium2 kernel written in Bass

```python
import math
import jax
import jax.numpy as jnp
from jax import lax
import numpy as np

D_MODEL = 1024
BATCH = 8
SEQ = 2048
DEPTH = 1
DEC_BATCH = 128
DEC_SEQ = 4
PAST_LEN = 16384
PAGE_SIZE = 128

N_META = 16
MIX_WIDTH = D_MODEL
RET_WIDTH = MIX_WIDTH // 2
RET_HEADS = 4
RET_DK = RET_WIDTH // RET_HEADS
RET_DV = RET_WIDTH // RET_HEADS
RET_CHUNK = 128
ROPE_THETA = 10000.0
POOL_WIDTH = MIX_WIDTH - RET_WIDTH
POOL_WINDOWS = (2, 4, 8, 16)
POOL_GROUP = POOL_WIDTH // len(POOL_WINDOWS)
POOL_BUF = max(POOL_WINDOWS) - 1
IN_WIDTH = 4 * RET_WIDTH + POOL_WIDTH
D_FF = 2816
ALPHA = (2.0 * DEPTH) ** 0.25
BETA = (8.0 * DEPTH) ** -0.25
LN_EPS = 1e-5
GN_EPS = 1e-5

kernel_name = "hymba_retention_pool_macaron_deepnorm"


def _layer_norm(x, g, b):
    xf = x.astype(jnp.float32)
    mu = jnp.mean(xf, axis=-1, keepdims=True)
    var = jnp.mean(jnp.square(xf - mu), axis=-1, keepdims=True)
    y = (xf - mu) * lax.rsqrt(var + LN_EPS) * g.astype(jnp.float32) + b.astype(jnp.float32)
    return y.astype(x.dtype)


def _swiglu(x, w_gate, w_up, w_down):
    return (jax.nn.silu(x @ w_gate) * (x @ w_up)) @ w_down


def _log_gamma():
    return jnp.log(1.0 - jnp.exp2(-5.0 - jnp.arange(RET_HEADS, dtype=jnp.float32)))


def _rope(x, pos):
    d = x.shape[-1]
    inv_freq = ROPE_THETA ** (-jnp.arange(0, d, 2, dtype=jnp.float32) / d)
    ang = pos.astype(jnp.float32)[:, None] * inv_freq[None, :]
    cos, sin = jnp.cos(ang), jnp.sin(ang)
    x1, x2 = x[..., : d // 2], x[..., d // 2:]
    return jnp.concatenate([x1 * cos - x2 * sin, x2 * cos + x1 * sin], axis=-1)


def _retention_chunk(q, k, v, s):
    lg = _log_gamma()
    c = q.shape[2]
    idx = jnp.arange(c, dtype=jnp.float32)
    diff = idx[:, None] - idx[None, :]
    decay = jnp.where(diff >= 0, jnp.exp(lg[:, None, None] * jnp.maximum(diff, 0.0)), 0.0)
    scores = jnp.einsum('bhid,bhjd->bhij', q, k) * decay[None]
    q_dec = q * jnp.exp(lg[:, None] * (idx + 1.0)[None, :])[None, :, :, None]
    o = jnp.einsum('bhij,bhjv->bhiv', scores, v) + jnp.einsum('bhid,bhdv->bhiv', q_dec, s)
    k_dec = k * jnp.exp(lg[:, None] * (c - 1.0 - idx)[None, :])[None, :, :, None]
    s_new = jnp.exp(lg * c)[None, :, None, None] * s + jnp.einsum('bhjd,bhjv->bhdv', k_dec, v)
    return o, s_new


def _retention(q, k, v, s0, lead):
    o_lead, s = _retention_chunk(q[:, :, :lead], k[:, :, :lead], v[:, :, :lead], s0)
    rest = q.shape[2] - lead
    if rest == 0:
        return o_lead, s
    n_chunks = rest // RET_CHUNK
    bsz, heads = q.shape[0], q.shape[1]

    def to_chunks(t):
        return t[:, :, lead:].reshape(bsz, heads, n_chunks, RET_CHUNK, t.shape[-1]).transpose(2, 0, 1, 3, 4)

    def step(state, qkv):
        o, state = _retention_chunk(qkv[0], qkv[1], qkv[2], state)
        return state, o

    s, o_rest = lax.scan(step, s, (to_chunks(q), to_chunks(k), to_chunks(v)))
    o_rest = o_rest.transpose(1, 2, 0, 3, 4).reshape(bsz, heads, rest, v.shape[-1])
    return jnp.concatenate([o_lead, o_rest], axis=2), s


def _multi_pool(p, prefix, pos, pool_w, pool_scale):
    bsz, seq_len, _ = p.shape
    xp = jnp.concatenate([prefix.astype(jnp.float32), p.astype(jnp.float32)], axis=1)
    c0 = jnp.concatenate([jnp.zeros((bsz, 1, POOL_WIDTH), jnp.float32), jnp.cumsum(xp, axis=1)], axis=1)
    end = c0[:, POOL_BUF + 1: POOL_BUF + 1 + seq_len]
    cur = xp[:, POOL_BUF:]
    outs = []
    for gi, w in enumerate(POOL_WINDOWS):
        sl = slice(gi * POOL_GROUP, (gi + 1) * POOL_GROUP)
        start = c0[:, POOL_BUF + 1 - w: POOL_BUF + 1 - w + seq_len, sl]
        cnt = jnp.minimum(pos + 1, w).astype(jnp.float32)[None, :, None]
        d = (end[..., sl] - start) / cnt - cur[..., sl]
        outs.append(jnp.einsum('blc,cd->bld', d.astype(p.dtype), pool_w[gi]))
    out = jnp.concatenate(outs, axis=-1) * pool_scale
    new_buf = xp[:, -POOL_BUF:].astype(p.dtype)
    return out, new_buf


def _mixer(h, pos, s_ret, pool_prefix, lead, w_in, pool_w, pool_scale, w_out):
    bsz, seq_len, _ = h.shape
    proj = h @ w_in
    q, k, v, g, p = jnp.split(proj, [RET_WIDTH, 2 * RET_WIDTH, 3 * RET_WIDTH, 4 * RET_WIDTH], axis=-1)

    def heads(t, d):
        return t.reshape(bsz, seq_len, RET_HEADS, d).transpose(0, 2, 1, 3).astype(jnp.float32)

    qh = _rope(heads(q, RET_DK), pos) * (RET_DK ** -0.5)
    kh = _rope(heads(k, RET_DK), pos)
    vh = heads(v, RET_DV)
    o, s_new = _retention(qh, kh, vh, s_ret.astype(jnp.float32), lead)
    mu = jnp.mean(o, axis=-1, keepdims=True)
    var = jnp.mean(jnp.square(o - mu), axis=-1, keepdims=True)
    o = (o - mu) * lax.rsqrt(var + GN_EPS)
    o = o.transpose(0, 2, 1, 3).reshape(bsz, seq_len, RET_WIDTH).astype(h.dtype)
    ret_out = jax.nn.silu(g) * o
    pool_out, new_buf = _multi_pool(p, pool_prefix, pos, pool_w, pool_scale)
    y = jnp.concatenate([ret_out, pool_out], axis=-1) @ w_out
    return y, s_new.astype(h.dtype), new_buf


def _layer(x, pos, s_ret, pool_prefix, lead,
           f1_gate, f1_up, f1_down, ln1_g, ln1_b, w_in, pool_w, pool_scale, w_out,
           ln2_g, ln2_b, f2_gate, f2_up, f2_down, ln3_g, ln3_b):
    x = _layer_norm(ALPHA * x + 0.5 * _swiglu(x, f1_gate, f1_up, f1_down), ln1_g, ln1_b)
    m, s_new, buf = _mixer(x, pos, s_ret, pool_prefix, lead, w_in, pool_w, pool_scale, w_out)
    x = _layer_norm(ALPHA * x + m, ln2_g, ln2_b)
    x = _layer_norm(ALPHA * x + 0.5 * _swiglu(x, f2_gate, f2_up, f2_down), ln3_g, ln3_b)
    return x, s_new, buf


def setup_inputs(seed: int = 0) -> dict:
    key = jax.random.key(seed)
    ks = jax.random.split(key, 24)
    f32 = jnp.float32

    def nrm(k, shape, scale=1.0):
        return jax.random.normal(k, shape, f32) * scale

    v_scale = jnp.ones((IN_WIDTH,), f32).at[2 * RET_WIDTH: 3 * RET_WIDTH].set(BETA)
    return {
        "x_prompt": nrm(ks[0], (BATCH, SEQ, D_MODEL)),
        "x_sample": nrm(ks[1], (DEC_BATCH, DEC_SEQ, D_MODEL)),
        "state_ret": nrm(ks[2], (DEPTH, DEC_BATCH, RET_HEADS, RET_DK, RET_DV)),
        "state_pool": nrm(ks[3], (DEPTH, DEC_BATCH, POOL_BUF, POOL_WIDTH)),
        "meta_tokens": nrm(ks[4], (N_META, D_MODEL)),
        "ffn1_w_gate": nrm(ks[5], (DEPTH, D_MODEL, D_FF), D_MODEL ** -0.5),
        "ffn1_w_up": nrm(ks[6], (DEPTH, D_MODEL, D_FF), D_MODEL ** -0.5),
        "ffn1_w_down": nrm(ks[7], (DEPTH, D_FF, D_MODEL), BETA * D_FF ** -0.5),
        "ln1_g": 1.0 + nrm(ks[8], (DEPTH, D_MODEL), 0.05),
        "ln1_b": nrm(ks[9], (DEPTH, D_MODEL), 0.02),
        "w_in": nrm(ks[10], (DEPTH, D_MODEL, IN_WIDTH), D_MODEL ** -0.5) * v_scale,
        "pool_w": nrm(ks[11], (DEPTH, len(POOL_WINDOWS), POOL_GROUP, POOL_GROUP), POOL_GROUP ** -0.5),
        "pool_scale": 1.0 + nrm(ks[12], (DEPTH, POOL_WIDTH), 0.1),
        "w_out": nrm(ks[13], (DEPTH, MIX_WIDTH, D_MODEL), BETA * MIX_WIDTH ** -0.5),
        "ln2_g": 1.0 + nrm(ks[14], (DEPTH, D_MODEL), 0.05),
        "ln2_b": nrm(ks[15], (DEPTH, D_MODEL), 0.02),
        "ffn2_w_gate": nrm(ks[16], (DEPTH, D_MODEL, D_FF), D_MODEL ** -0.5),
        "ffn2_w_up": nrm(ks[17], (DEPTH, D_MODEL, D_FF), D_MODEL ** -0.5),
        "ffn2_w_down": nrm(ks[18], (DEPTH, D_FF, D_MODEL), BETA * D_FF ** -0.5),
        "ln3_g": 1.0 + nrm(ks[19], (DEPTH, D_MODEL), 0.05),
        "ln3_b": nrm(ks[20], (DEPTH, D_MODEL), 0.02),
    }


def reference(x_prompt, x_sample, state_ret, state_pool, meta_tokens,
              ffn1_w_gate, ffn1_w_up, ffn1_w_down, ln1_g, ln1_b, w_in, pool_w, pool_scale, w_out,
              ln2_g, ln2_b, ffn2_w_gate, ffn2_w_up, ffn2_w_down, ln3_g, ln3_b):
    bp = x_prompt.shape[0]
    meta = jnp.broadcast_to(meta_tokens.astype(x_prompt.dtype)[None], (bp, N_META, x_prompt.shape[2]))
    hp = jnp.concatenate([meta, x_prompt], axis=1)
    hs = x_sample
    pos_p = jnp.arange(hp.shape[1], dtype=jnp.int32)
    pos_s = PAST_LEN + jnp.arange(hs.shape[1], dtype=jnp.int32)
    ret_p, pool_p, ret_s, pool_s = [], [], [], []
    for i in range(DEPTH):
        lp = (ffn1_w_gate[i], ffn1_w_up[i], ffn1_w_down[i], ln1_g[i], ln1_b[i], w_in[i], pool_w[i],
              pool_scale[i], w_out[i], ln2_g[i], ln2_b[i], ffn2_w_gate[i], ffn2_w_up[i], ffn2_w_down[i],
              ln3_g[i], ln3_b[i])
        s0 = jnp.zeros((bp, RET_HEADS, RET_DK, RET_DV), hp.dtype)
        b0 = jnp.zeros((bp, POOL_BUF, POOL_WIDTH), hp.dtype)
        hp, sr, sb = _layer(hp, pos_p, s0, b0, N_META, *lp)
        ret_p.append(sr)
        pool_p.append(sb)
        hs, sr, sb = _layer(hs, pos_s, state_ret[i], state_pool[i], hs.shape[1], *lp)
        ret_s.append(sr)
        pool_s.append(sb)
    y_prompt = hp[:, N_META:]
    return (y_prompt, hs, jnp.stack(ret_p), jnp.stack(pool_p), jnp.stack(ret_s), jnp.stack(pool_s))
```

```python
import contextlib
import numpy as np
import concourse.bass as bass
import concourse.mybir as mybir
from concourse.bass_utils import run_bass_kernel_spmd

F32 = mybir.dt.float32
BF16 = mybir.dt.bfloat16
AF = mybir.ActivationFunctionType
ALU = mybir.AluOpType

D = 1024
DFF = 2816
NFC = 22
SEQ = 2048
NMETA = 16
PAST = 16384
ALPHA = 2.0 ** 0.25
EPS = 1e-5
GAM = [1.0 - 2.0 ** (-5.0 - h) for h in range(4)]
NSLOT = 4
LOOKAHEAD = NSLOT - 2
N_SP_SEMS = 40
import os as _os
SAME_ENG_WINDOW = 10 ** 9 if _os.environ.get("KSAFE") == "1" else 2
SAME_ENG_MODE = _os.environ.get("KSAMEENG", "raw")


class Tok:
    __slots__ = ("sem", "val")

    def __init__(self, sem, val):
        self.sem = sem
        self.val = val


class Buf:
    def __init__(self, name):
        self.name = name
        self.w = None
        self.r = []


class Sched:
    def __init__(self, nc, es):
        self.nc = nc
        self.eng = {"pe": nc.tensor, "act": nc.scalar, "dve": nc.vector, "pool": nc.gpsimd, "sp": nc.sync}
        self.sem = {k: es.enter_context(nc.semaphore("sem_" + k)) for k in self.eng}
        self.cnt = {k: 0 for k in self.eng}
        self.seen = {k: {} for k in self.eng}
        self.es = es
        self.sp_sems = [es.enter_context(nc.semaphore("spd%d" % i)) for i in range(N_SP_SEMS)]
        self.sp_cnt = [0] * N_SP_SEMS
        self.sp_next = 0
        self.out_toks = []

    def wait(self, e, tok, raw=True):
        if tok is None:
            return
        if tok.sem is self.sem[e]:
            if e == "pe":
                return
            if SAME_ENG_WINDOW < 10 ** 9 and not raw:
                return
            if SAME_ENG_MODE == "window" and tok.val <= self.cnt[e] - SAME_ENG_WINDOW:
                return
        key = id(tok.sem)
        if self.seen[e].get(key, 0) >= tok.val:
            return
        self.eng[e].wait_ge(tok.sem, tok.val)
        self.seen[e][key] = tok.val

    def _deps(self, e, reads, writes):
        for b in reads:
            self.wait(e, b.w, raw=True)
        for b in writes:
            self.wait(e, b.w, raw=False)
            for t in b.r:
                self.wait(e, t, raw=False)

    def _commit(self, tok, reads, writes):
        for b in reads:
            b.r.append(tok)
        for b in writes:
            b.w = tok
            b.r = []

    def op(self, e, emit, reads=(), writes=()):
        self._deps(e, reads, writes)
        ins = emit(self.eng[e])
        self.cnt[e] += 1
        tok = Tok(self.sem[e], self.cnt[e])
        ins.then_inc(tok.sem, 1)
        self._commit(tok, reads, writes)
        return tok

    def dma_sp(self, emit, reads=(), writes=(), is_out=False):
        i = self.sp_next
        self.sp_next = (self.sp_next + 1) % N_SP_SEMS
        sem = self.sp_sems[i]
        if self.sp_cnt[i] > 0:
            self.wait("sp", Tok(sem, self.sp_cnt[i]))
        self._deps("sp", reads, writes)
        ins = emit(self.nc.sync)
        self.sp_cnt[i] += 16
        tok = Tok(sem, self.sp_cnt[i])
        ins.then_inc(sem, 16)
        self._commit(tok, reads, writes)
        if is_out:
            self.out_toks.append(tok)
        return tok

    def dma_on(self, e, sem_state, emit, reads=(), writes=()):
        self._deps(e, reads, writes)
        ins = emit(self.eng[e])
        sem_state[1] += 16
        tok = Tok(sem_state[0], sem_state[1])
        ins.then_inc(sem_state[0], 16)
        self._commit(tok, reads, writes)
        return tok


def _host_tables():
    f32 = np.float32
    lg = np.log(np.array(GAM, dtype=np.float64))
    dk = 128.0 ** -0.5
    inv_freq = (np.float32(10000.0) ** (-np.arange(0, 128, 2, dtype=f32) / f32(128))).astype(f32)

    def rope_tab(pos):
        ang = pos.astype(f32)[:, None] * inv_freq[None, :]
        ang = ang.astype(f32)
        return np.stack([np.cos(ang).astype(f32), np.sin(ang).astype(f32)], axis=1)

    rope = np.zeros((5, 128, 4, 2, 64), f32)
    for m in range(4):
        for c in range(4):
            pos = m * 512 + c * 128 + np.arange(128)
            rope[m, :, c] = rope_tab(pos)
    pos4 = np.concatenate([PAST + (np.arange(64) % 4), 2048 + np.arange(16)])
    rope[4, :80, 0] = rope_tab(pos4)

    i = np.arange(128)
    diff = i[None, :] - i[:, None]
    mask = np.zeros((128, 4, 128), f32)
    g = np.zeros((128, 4, 128), f32)
    kdec = np.zeros((128, 4), f32)
    for h in range(4):
        mask[:, h, :] = np.where(diff >= 0, dk * np.exp(lg[h] * np.maximum(diff, 0)), 0.0)
        g[:, h, :] = (dk * np.exp(lg[h] * (i + 1.0)))[None, :]
        kdec[:, h] = np.exp(lg[h] * (127.0 - i))
    blk = np.concatenate([np.arange(64) // 4, np.full(16, 16)])
    li = np.concatenate([np.arange(64) % 4, np.arange(16)])
    cb = np.concatenate([np.full(64, 4), np.full(16, 16)])
    same = blk[:, None] == blk[None, :]
    d4 = li[None, :] - li[:, None]
    mask4 = np.zeros((128, 4, 80), f32)
    g4 = np.zeros((128, 4, 80), f32)
    kdec4 = np.zeros((128, 4), f32)
    for h in range(4):
        mask4[:80, h, :] = np.where(same & (d4 >= 0), dk * np.exp(lg[h] * np.maximum(d4, 0)), 0.0)
        g4[:, h, :] = (dk * np.exp(lg[h] * (li + 1.0)))[None, :]
        kdec4[:80, h] = np.exp(lg[h] * (cb - 1.0 - li))
    bm = np.zeros((128, 17, 80), f32)
    bmt = np.zeros((128, 17), f32)
    for b in range(17):
        bm[:, b, :] = (blk == b)[None, :]
        bmt[:80, b] = (blk == b)
    invc = np.zeros((128, 4, 16), f32)
    for gi, w in enumerate((2, 4, 8, 16)):
        invc[:, gi, :] = (1.0 / np.minimum(np.arange(16) + 1, w))[None, :]
    ident = np.eye(128, dtype=f32)
    return dict(t_rope=rope, t_mask=mask, t_g=g, t_kdec=kdec, t_mask4=mask4, t_g4=g4, t_kdec4=kdec4,
                t_bm=bm, t_bmt=bmt, t_invc=invc, t_ident=ident)


def build_program():
    nc = bass.Bass("TRN2", target_bir_lowering=False)
    es = contextlib.ExitStack()

    def din(name, shape, dt=F32):
        return nc.dram_tensor(name, list(shape), dt, kind="ExternalInput").ap()

    def dout(name, shape):
        return nc.dram_tensor(name, list(shape), F32, kind="ExternalOutput").ap()

    xp = din("xp", [SEQ, D])
    xs = din("xs", [64, D])
    sret = din("sret", [16, 4, 128, 128])
    spool = din("spool", [16, 15, 512])
    meta = din("meta", [NMETA, D])
    w1g = din("w1g", [D, DFF]); w1u = din("w1u", [D, DFF]); w1d = din("w1d", [DFF, D])
    w2g = din("w2g", [D, DFF]); w2u = din("w2u", [D, DFF]); w2d = din("w2d", [DFF, D])
    win = din("win", [D, 2560]); wout = din("wout", [D, D])
    poolw = din("poolw", [4, 128, 128]); pscale = din("pscale", [128, 4])
    lng = [din("ln%dg" % i, [1, D]) for i in (1, 2, 3)]
    lnb = [din("ln%db" % i, [1, D]) for i in (1, 2, 3)]
    t_rope = din("t_rope", [5, 128, 4, 2, 64])
    t_mask = din("t_mask", [128, 4, 128]); t_g = din("t_g", [128, 4, 128]); t_kdec = din("t_kdec", [128, 4])
    t_mask4 = din("t_mask4", [128, 4, 80]); t_g4 = din("t_g4", [128, 4, 80]); t_kdec4 = din("t_kdec4", [128, 4])
    t_bm = din("t_bm", [128, 17, 80]); t_bmt = din("t_bmt", [128, 17]); t_invc = din("t_invc", [128, 4, 16])
    t_ident = din("t_ident", [128, 128])

    yp = dout("yp", [SEQ, D]); ys = dout("ys", [64, D])
    nrp = dout("nrp", [4, 128, 128]); npp = dout("npp", [15, 512])
    nrs = dout("nrs", [16, 4, 128, 128]); nps = dout("nps", [16, 15, 512])

    import os
    with es:
        S = Sched(nc, es)
        _n = [0]

        def sb(shape, dt, name):
            _n[0] += 1
            return es.enter_context(nc.sbuf_tensor("%s_%d" % (name, _n[0]), list(shape), dt))

        IDB = sb([128, 128], BF16, "idb"); IDF = sb([128, 128], F32, "idf")
        MASK = sb([128, 4, 128], F32, "mask"); GT = sb([128, 4, 128], F32, "gt"); KDEC = sb([128, 4], F32, "kdec")
        MASK4 = sb([128, 4, 80], F32, "mask4"); G4 = sb([128, 4, 80], F32, "g4"); KDEC4 = sb([128, 4], F32, "kdec4")
        BM = sb([128, 17, 80], BF16, "bm"); BMT = sb([128, 17], F32, "bmt"); INVC = sb([128, 4, 16], F32, "invc")
        PSC = sb([128, 4], F32, "psc"); POOLW = sb([128, 4, 128], BF16, "poolw")
        LNT = sb([128, 2, D], F32, "lnt"); b_LNT = Buf("lnt")
        ROPE = [sb([128, 4, 2, 64], F32, "rope") for _ in range(2)]; b_ROPE = [Buf("rope0"), Buf("rope1")]
        X = sb([128, 4, D], F32, "x"); b_X = [Buf("x%d" % c) for c in range(4)]
        XT = [sb([128, 8, 512], BF16, "xt") for _ in range(2)]
        b_XT = [[Buf("xt%d_%d" % (i, c)) for c in range(4)] for i in range(2)]
        XB = [sb([128, D], BF16, "xb") for _ in range(2)]; b_XB = [Buf("xb0"), Buf("xb1")]
        HT = sb([128, NFC, 512], BF16, "ht"); b_HT = [Buf("ht%d" % f) for f in range(NFC)]
        NTMP = 3
        TMP = [sb([128, D], F32, "tmp") for _ in range(NTMP)]; b_TMP = [Buf("tmp%d" % i) for i in range(NTMP)]
        SLOT = [sb([128, 8, 512], BF16, "slot") for _ in range(NSLOT)]; b_SLOT = [Buf("slot%d" % i) for i in range(NSLOT)]
        QR = sb([128, 4, 512], BF16, "qr"); b_QR = [Buf("qr%d" % c) for c in range(4)]
        KR = sb([128, 4, 512], BF16, "kr"); b_KR = [Buf("kr%d" % c) for c in range(4)]
        VV = sb([128, 4, 512], BF16, "vv"); b_VV = [Buf("vv%d" % c) for c in range(4)]
        SG = sb([128, 4, 512], BF16, "sg"); b_SG = [Buf("sg%d" % c) for c in range(4)]
        KD = [sb([128, 512], BF16, "kd") for _ in range(4)]; b_KD = [Buf("kd%d" % i) for i in range(4)]
        QKT = [sb([128, 8, 128], BF16, "qkt") for _ in range(4)]; b_QKT = [Buf("qkt%d" % i) for i in range(4)]
        QD = [sb([128, 4, 128], BF16, "qd") for _ in range(4)]; b_QD = [Buf("qd%d" % i) for i in range(4)]
        MM = [sb([128, 4, 128], BF16, "mm") for _ in range(4)]; b_MM = [Buf("mm%d" % i) for i in range(4)]
        ST = sb([128, 4, 128], F32, "st"); b_ST = Buf("st")
        SB = [sb([128, 4, 128], BF16, "sbb") for _ in range(4)]; b_SB = [Buf("sb%d" % i) for i in range(4)]
        RET = [sb([128, 512], BF16, "ret") for _ in range(4)]; b_RET = [Buf("ret%d" % i) for i in range(4)]
        PX = [sb([128, 528], F32, "px") for _ in range(2)]; b_PX = [Buf("px0"), Buf("px1")]
        DT = [sb([128, 512], BF16, "dt") for _ in range(4)]; b_DT = [Buf("dt%d" % i) for i in range(4)]
        CARRY = sb([128, 4, 16], F32, "carry"); b_CARRY = [Buf("carry%d" % g) for g in range(4)]
        STAT = [sb([128, 4, 6], F32, "stat") for _ in range(4)]; b_STAT = [Buf("stat%d" % i) for i in range(4)]
        MV = [sb([128, 4, 2], F32, "mv") for _ in range(4)]; b_MV = [Buf("mv%d" % i) for i in range(4)]
        RS = [sb([128, 4], F32, "rs") for _ in range(4)]; b_RS = [Buf("rs%d" % i) for i in range(4)]
        NB = [sb([128, 4], F32, "nb") for _ in range(4)]; b_NB = [Buf("nb%d" % i) for i in range(4)]
        SPT = sb([128, 4, 16, 15], F32, "spt"); b_SPT = Buf("spt")
        SF = [sb([128, 4, 128], F32, "sf") for _ in range(3)]; b_SF = [Buf("sf%d" % i) for i in range(3)]
        SBB = [sb([128, 4, 128], BF16, "sbk") for _ in range(3)]; b_SBB = [Buf("sbk%d" % i) for i in range(3)]
        SN = [sb([128, 4, 128], F32, "sn") for _ in range(2)]; b_SN = [Buf("sn0"), Buf("sn1")]
        QDB = [QD[2], QD[3]]; b_QDB = [b_QD[2], b_QD[3]]
        KDB = [KD[2], KD[3]]; b_KDB = [b_KD[2], b_KD[3]]
        QDA = QD[1]; b_QDA = b_QD[1]
        KDA = KD[1]; b_KDA = b_KD[1]
        XP4 = PX; b_XP4 = b_PX

        NPSF = 6
        PSB = [es.enter_context(nc.psum_tensor("ps%d" % i, [128, 512], F32)) for i in range(NPSF)]
        b_PS = [Buf("ps%d" % i) for i in range(NPSF)]
        PST = [es.enter_context(nc.psum_tensor("pst%d" % i, [128, 8, 128], BF16)) for i in range(2)]
        b_PST = [Buf("pst%d" % i) for i in range(2)]
        ps_next = [0, 0]
        if os.environ.get("KMEM") == "1":
            print("SBUF bytes/partition remaining after allocations:", nc.sbuf_bytes_remaining)

        def psum(exclude=None):
            i = ps_next[0]
            if exclude is not None and b_PS[i] is exclude:
                i = (i + 1) % NPSF
            ps_next[0] = (i + 1) % NPSF
            return PSB[i], b_PS[i]

        def psum_t():
            i = ps_next[1]
            ps_next[1] = (i + 1) % 2
            return PST[i], b_PST[i]

        rr = {}

        def rot(key, n):
            i = rr.get(key, 0)
            rr[key] = (i + 1) % n
            return i

        b_const = Buf("const")

        def ld(dst, src):
            S.dma_sp(lambda q: q.dma_start(out=dst, in_=src), writes=[b_const])

        ld(IDF[:], t_ident[:, :]); ld(MASK[:], t_mask[:, :, :]); ld(GT[:], t_g[:, :, :]); ld(KDEC[:], t_kdec[:, :])
        ld(MASK4[:], t_mask4[:, :, :]); ld(G4[:], t_g4[:, :, :]); ld(KDEC4[:], t_kdec4[:, :])
        ld(BMT[:], t_bmt[:, :]); ld(INVC[:], t_invc[:, :, :]); ld(PSC[:], pscale[:, :])
        b_const2 = Buf("const2")
        S.op("dve", lambda v: v.tensor_copy(out=IDB[:], in_=IDF[:]), reads=[b_const], writes=[b_const2])
        S.op("dve", lambda v: v.memset(ST[:], 0.0), writes=[b_ST])
        S.op("dve", lambda v: v.memset(SB[0][:], 0.0), writes=[b_SB[0]])
        S.op("dve", lambda v: v.memset(CARRY[:], 0.0), writes=b_CARRY)
        wsem = [[es.enter_context(nc.semaphore("wsem%d" % i)), 0] for i in range(NSLOT)]
        pwsem = [es.enter_context(nc.semaphore("pwsem")), 0]
        S.dma_on("pool", pwsem, lambda q: q.dma_start(out=POOLW[:], in_=poolw.rearrange("g c d -> c g d")),
                 writes=[b_const2])
        S.dma_on("pool", pwsem, lambda q: q.dma_start(out=BM[:], in_=t_bm[:, :, :]), writes=[b_const2])
        CONST = [b_const, b_const2]

        wlist = []

        def add_ffn(wg, wu, wd):
            for fg in range(6):
                nco = 512 if fg < 5 else 256
                wlist.append((wg, 0, 8, fg * 512, nco))
                wlist.append((wu, 0, 8, fg * 512, nco))
            for dt_ in range(2):
                for (f0, nf) in ((0, 8), (8, 8), (16, 6)):
                    wlist.append((wd, f0 * 128, nf, dt_ * 512, 512))

        def add_mixer():
            for gi in (4, 0, 1, 2, 3):
                wlist.append((win, 0, 8, gi * 512, 512))
            for dt_ in range(2):
                wlist.append((wout, 0, 8, dt_ * 512, 512))

        for _tile in range(3):
            add_ffn(w1g, w1u, w1d)
            add_mixer()
            add_ffn(w2g, w2u, w2d)
        add_ffn(w1g, w1u, w1d)
        add_mixer()
        add_mixer()
        add_ffn(w2g, w2u, w2d)
        wstate = {"emitted": 0, "next": 0}

        NW = 43
        wkey = lambda ent: (id(ent[0]), ent[1], ent[3])
        img_idx = {}
        for ent in wlist[:NW]:
            img_idx[wkey(ent)] = len(img_idx)
        assert len(img_idx) == NW and all(wkey(ent) in img_idx for ent in wlist)
        WSC = []
        for e_ in range(NW):
            (_w, _r0, nch_, _c0, nco_) = wlist[e_]
            WSC.append(nc.dram_tensor("wsc%d" % e_, [128, nch_ * nco_], BF16, kind="Internal").ap())
        b_WSC = [Buf("wsc%d" % e_) for e_ in range(NW)]
        w_uses = [0] * NW

        def w_emit(n):
            (w, r0, nch, c0, nco) = wlist[n]
            si = n % NSLOT
            e_ = img_idx[wkey(wlist[n])]
            img = WSC[e_].rearrange("p (k n) -> p k n", k=nch)
            p_ = w_uses[e_]
            w_uses[e_] += 1
            wb_pass = e_ % 3
            if p_ <= wb_pass:
                src = w[r0:r0 + nch * 128, c0:c0 + nco].rearrange("(k p) n -> p k n", p=128)
                S.dma_on("pool", wsem[si], lambda q: q.dma_start(out=SLOT[si][:, 0:nch, 0:nco], in_=src),
                         writes=[b_SLOT[si]])
                if p_ == wb_pass:
                    S.dma_sp(lambda q: q.dma_start(out=img, in_=SLOT[si][:, 0:nch, 0:nco]),
                             reads=[b_SLOT[si]], writes=[b_WSC[e_]])
            else:
                S.dma_on("pool", wsem[si], lambda q: q.dma_start(out=SLOT[si][:, 0:nch, 0:nco], in_=img),
                         reads=[b_WSC[e_]], writes=[b_SLOT[si]])

        def w_acquire(pending=0):
            n = wstate["next"]
            wstate["next"] += 1
            while wstate["emitted"] <= min(n + NSLOT - 1 - pending, len(wlist) - 1):
                w_emit(wstate["emitted"])
                wstate["emitted"] += 1
            si = n % NSLOT
            return SLOT[si], b_SLOT[si]

        class Tile:
            pass

        tiles = []
        for m in range(4):
            t = Tile(); t.idx = m; t.nt = 512; t.chunks = [(c * 128, 128) for c in range(4)]; tiles.append(t)
        t = Tile(); t.idx = 4; t.nt = 80; t.chunks = [(0, 80)]; tiles.append(t)
        for t in tiles[:4]:
            t.Xc = [X[:, c, :] for c in range(4)]; t.bX = b_X
            t.HT = HT; t.bHT = b_HT; t.XTs = XT; t.bXTs = b_XT
        X4 = sb([128, D], F32, "x4"); HT4 = sb([128, NFC, 80], BF16, "ht4")
        XT4 = [sb([128, 8, 80], BF16, "xt4") for _ in range(2)]
        t = tiles[4]
        t.Xc = [X4[:, :]]; t.bX = [Buf("x4")]
        t.HT = HT4; t.bHT = [Buf("ht4_%d" % f) for f in range(NFC)]
        t.XTs = XT4; t.bXTs = [[Buf("xt4_0")], [Buf("xt4_1")]]

        def load_x0(tile, ci, dst, bd):
            m = tile.idx
            if m == 4:
                S.dma_sp(lambda q: q.dma_start(out=dst[0:64, :], in_=xs[:, :]), writes=[bd])
                S.dma_sp(lambda q: q.dma_start(out=dst[64:80, :], in_=xp[2032:2048, :]), writes=[bd])
            elif m == 0 and ci == 0:
                S.dma_sp(lambda q: q.dma_start(out=dst[0:16, :], in_=meta[:, :]), writes=[bd])
                S.dma_sp(lambda q: q.dma_start(out=dst[16:128, :], in_=xp[0:112, :]), writes=[bd])
            else:
                r0 = 512 * m + 128 * ci - 16
                S.dma_sp(lambda q: q.dma_start(out=dst[:, :], in_=xp[r0:r0 + 128, :]), writes=[bd])

        def store_out(tile, ci):
            m = tile.idx
            src, bs = tile.Xc[ci], tile.bX[ci]
            if m == 4:
                S.dma_sp(lambda q: q.dma_start(out=ys[:, :], in_=src[0:64, :]), reads=[bs], is_out=True)
                S.dma_sp(lambda q: q.dma_start(out=yp[2032:2048, :], in_=src[64:80, :]), reads=[bs], is_out=True)
            elif m == 0 and ci == 0:
                S.dma_sp(lambda q: q.dma_start(out=yp[0:112, :], in_=src[16:128, :]), reads=[bs], is_out=True)
            else:
                r0 = 512 * m + 128 * ci - 16
                S.dma_sp(lambda q: q.dma_start(out=yp[r0:r0 + 128, :], in_=src[:, :]), reads=[bs], is_out=True)

        def to_xt_cast(src_ap, b_src, nr):
            xbi = rot("xb", 2)
            S.op("act", lambda a: a.copy(out=XB[xbi][:nr, :], in_=src_ap), reads=[b_src], writes=[b_XB[xbi]])
            return xbi

        def to_xt_tr(xbi, nr):
            ptv, bpt = psum_t()

            def em(pe):
                for k in range(8):
                    ins = pe.transpose(ptv[:, k, :nr], XB[xbi][:nr, k * 128:(k + 1) * 128], IDB[:nr, :nr])
                return ins
            S.op("pe", em, reads=[b_XB[xbi]] + CONST, writes=[bpt])
            return ptv, bpt

        def to_xt_copy(ptv, bpt, nr, r0, dxt, dbufs):
            S.op("act", lambda a: a.copy(out=dxt[:, :, r0:r0 + nr], in_=ptv[:, :, :nr]),
                 reads=[bpt], writes=[dbufs[r0 // 128]])

        def to_xt(src_ap, b_src, nr, r0, dxt, dbufs):
            xbi = to_xt_cast(src_ap, b_src, nr)
            ptv, bpt = to_xt_tr(xbi, nr)
            to_xt_copy(ptv, bpt, nr, r0, dxt, dbufs)

        def load_ln(i):
            S.dma_sp(lambda q: q.dma_start(out=LNT[:, 0, :], in_=lng[i].partition_broadcast(128)), writes=[b_LNT])
            S.dma_sp(lambda q: q.dma_start(out=LNT[:, 1, :], in_=lnb[i].partition_broadcast(128)), writes=[b_LNT])

        def early_stats(tile, ci, nr):
            if tile.idx < 4:
                S.op("dve", lambda v: v.bn_stats(out=STAT[ci][:nr, 0, :], in_=tile.Xc[ci][:nr, 0:512]),
                     reads=[tile.bX[ci]], writes=[b_STAT[ci]])

        def layer_norm_tile(tile, xti, store=False, only=None, defer_xt=False):
            CH = [(ci, ch) for ci, ch in enumerate(tile.chunks) if only is None or ci in only]
            for ci, (r0, nr) in CH:
                xa, bx, st = tile.Xc[ci][:nr, :], tile.bX[ci], STAT[ci]
                if tile.idx >= 4:
                    S.op("dve", lambda v: v.bn_stats(out=st[:nr, 0, :], in_=xa[:, 0:512]), reads=[bx], writes=[b_STAT[ci]])
                S.op("dve", lambda v: v.bn_stats(out=st[:nr, 1, :], in_=xa[:, 512:1024]), reads=[bx], writes=[b_STAT[ci]])
            for ci, (r0, nr) in CH:
                S.op("dve", lambda v: v.bn_aggr(out=MV[ci][:nr, 0, :],
                                                in_=STAT[ci][:nr, 0:2, :].rearrange("p a b -> p (a b)")),
                     reads=[b_STAT[ci]], writes=[b_MV[ci]])
            for ci, (r0, nr) in CH:
                S.op("dve", lambda v: v.tensor_scalar(out=RS[ci][:nr, 0:1], in0=MV[ci][:nr, 0, 1:2], scalar1=EPS,
                                                      scalar2=None, op0=ALU.add), reads=[b_MV[ci]], writes=[b_RS[ci]])
            for ci, (r0, nr) in CH:
                S.op("act", lambda a: a.sqrt(out=RS[ci][:nr, 0:1], in_=RS[ci][:nr, 0:1]), reads=[b_RS[ci]],
                     writes=[b_RS[ci]])
            for ci, (r0, nr) in CH:
                S.op("dve", lambda v: v.reciprocal(out=RS[ci][:nr, 0:1], in_=RS[ci][:nr, 0:1]), reads=[b_RS[ci]],
                     writes=[b_RS[ci]])
            for ci, (r0, nr) in CH:
                xa, bx = tile.Xc[ci][:nr, :], tile.bX[ci]
                S.op("dve", lambda v: v.scalar_tensor_tensor(out=xa, in0=xa, scalar=MV[ci][:nr, 0, 0:1],
                                                             in1=LNT[:nr, 0, :], op0=ALU.subtract, op1=ALU.mult),
                     reads=[bx, b_MV[ci], b_LNT], writes=[bx])
                S.op("dve", lambda v: v.scalar_tensor_tensor(out=xa, in0=xa, scalar=RS[ci][:nr, 0:1],
                                                             in1=LNT[:nr, 1, :], op0=ALU.mult, op1=ALU.add),
                     reads=[bx, b_RS[ci], b_LNT], writes=[bx])
                if store:
                    store_out(tile, ci)
            if store:
                return
            if defer_xt:
                return [(to_xt_cast(tile.Xc[ci][:nr, :], tile.bX[ci], nr), nr, r0) for ci, (r0, nr) in CH]
            pend = []
            for ci, (r0, nr) in CH:
                xbi = to_xt_cast(tile.Xc[ci][:nr, :], tile.bX[ci], nr)
                ptv, bpt = to_xt_tr(xbi, nr)
                pend.append((ptv, bpt, nr, r0))
                if len(pend) == 2:
                    to_xt_copy(*pend.pop(0), tile.XTs[xti], tile.bXTs[xti])
            for p_ in pend:
                to_xt_copy(*p_, tile.XTs[xti], tile.bXTs[xti])

        def ffn(TL, fg_hook=None, ln_xti=None):
            def gate_up(tile, xti, fg, ncj, Wg, bWg, Wu, bWu):
                nt = tile.nt
                xt, bxt = tile.XTs[xti], tile.bXTs[xti][:len(tile.chunks)]
                HT, b_HT = tile.HT, tile.bHT
                if 4 * nt <= 512:
                    pg, bpg = psum()
                    pu, bpu = psum()

                    def mmb(pe, W, p):
                        for j in range(ncj):
                            for k in range(8):
                                ins = pe.matmul(p[:, j * nt:(j + 1) * nt], lhsT=W[:, k, j * 128:(j + 1) * 128],
                                                rhs=xt[:, k, :nt], start=(k == 0), stop=(k == 7))
                        return ins
                    S.op("pe", lambda pe: mmb(pe, Wg, pg), reads=[bWg] + bxt, writes=[bpg])
                    S.op("pe", lambda pe: mmb(pe, Wu, pu), reads=[bWu] + bxt, writes=[bpu])
                    ti = rot("tmp", NTMP)
                    w_ = ncj * nt
                    S.op("act", lambda a: a.activation(out=TMP[ti][:, :w_], in_=pg[:, :w_], func=AF.Silu),
                         reads=[bpg], writes=[b_TMP[ti]])
                    S.op("dve", lambda v: v.scalar_tensor_tensor(
                        out=HT[:, fg * 4:fg * 4 + ncj, 0:nt],
                        in0=TMP[ti][:, :w_].rearrange("p (j t) -> p j t", j=ncj), scalar=0.5,
                        in1=pu[:, :w_].rearrange("p (j t) -> p j t", j=ncj), op0=ALU.mult, op1=ALU.mult),
                        reads=[b_TMP[ti], bpu], writes=b_HT[fg * 4:fg * 4 + ncj])
                    return
                for j in range(ncj):
                    f = fg * 4 + j
                    pg, bpg = psum()
                    pu, bpu = psum()

                    def mm(pe, W, p):
                        for k in range(8):
                            ins = pe.matmul(p[:, :nt], lhsT=W[:, k, j * 128:(j + 1) * 128], rhs=xt[:, k, :nt],
                                            start=(k == 0), stop=(k == 7))
                        return ins
                    S.op("pe", lambda pe: mm(pe, Wg, pg), reads=[bWg] + bxt, writes=[bpg])
                    S.op("pe", lambda pe: mm(pe, Wu, pu), reads=[bWu] + bxt, writes=[bpu])
                    ti = rot("tmp", NTMP)
                    S.op("act", lambda a: a.activation(out=TMP[ti][:, :nt], in_=pg[:, :nt], func=AF.Silu),
                         reads=[bpg], writes=[b_TMP[ti]])
                    S.op("dve", lambda v: v.scalar_tensor_tensor(out=HT[:, f, :nt], in0=TMP[ti][:, :nt], scalar=0.5,
                                                                 in1=pu[:, :nt], op0=ALU.mult, op1=ALU.mult),
                         reads=[b_TMP[ti], bpu], writes=[b_HT[f]])

            for fg in range(6):
                nco = 512 if fg < 5 else 256
                Wg, bWg = w_acquire()
                Wu, bWu = w_acquire(pending=1)
                for (tile, xti, res) in TL:
                    gate_up(tile, xti, fg, nco // 128, Wg, bWg, Wu, bWu)
                if fg_hook is not None:
                    fg_hook(fg)
            for dt_ in range(2):
                if dt_ == 1 and ln_xti is not None:
                    Ws = [w_acquire(pending=k) for k in range(3)]
                    cs = slice(512, 1024)
                    pend_xt = None
                    for (tile, xti, res) in TL:
                        HT, b_HT = tile.HT, tile.bHT
                        for ci, (r0, nr) in enumerate(tile.chunks):
                            acc, bacc = psum()

                            def mm3(pe):
                                for si_, (f0, nf) in enumerate(((0, 8), (8, 8), (16, 6))):
                                    for kk in range(nf):
                                        f = f0 + kk
                                        ins = pe.matmul(acc[:nr, :], lhsT=HT[:, f, r0:r0 + nr], rhs=Ws[si_][0][:, kk, :],
                                                        start=(f == 0), stop=(f == NFC - 1))
                                return ins
                            S.op("pe", mm3, reads=[w_[1] for w_ in Ws] + b_HT, writes=[bacc])
                            if pend_xt is not None:
                                (t_, xbi_, nr_, r0_) = pend_xt
                                ptv_, bpt_ = to_xt_tr(xbi_, nr_)
                                to_xt_copy(ptv_, bpt_, nr_, r0_, t_.XTs[ln_xti], t_.bXTs[ln_xti])
                            S.op("dve", lambda v: v.scalar_tensor_tensor(out=tile.Xc[ci][:nr, cs], in0=tile.Xc[ci][:nr, cs],
                                                                         scalar=ALPHA, in1=acc[:nr, :],
                                                                         op0=ALU.mult, op1=ALU.add),
                                 reads=[tile.bX[ci], bacc], writes=[tile.bX[ci]])
                            (xbi, nr2, r02), = layer_norm_tile(tile, ln_xti, only=[ci], defer_xt=True)
                            pend_xt = (tile, xbi, nr2, r02)
                    (t_, xbi_, nr_, r0_) = pend_xt
                    ptv_, bpt_ = to_xt_tr(xbi_, nr_)
                    to_xt_copy(ptv_, bpt_, nr_, r0_, t_.XTs[ln_xti], t_.bXTs[ln_xti])
                    continue
                accs = [[psum() for _ in tile.chunks] for (tile, xti, res) in TL]
                for (f0, nf) in ((0, 8), (8, 8), (16, 6)):
                    Wd, bWd = w_acquire()
                    for ti_, (tile, xti, res) in enumerate(TL):
                        HT, b_HT = tile.HT, tile.bHT
                        for ci, (r0, nr) in enumerate(tile.chunks):
                            acc, bacc = accs[ti_][ci]

                            def mm(pe):
                                for kk in range(nf):
                                    f = f0 + kk
                                    ins = pe.matmul(acc[:nr, :], lhsT=HT[:, f, r0:r0 + nr], rhs=Wd[:, kk, :],
                                                    start=(f == 0), stop=(f == NFC - 1))
                                return ins
                            S.op("pe", mm, reads=[bWd] + b_HT[f0:f0 + nf], writes=[bacc])
                for ti_, (tile, xti, res) in enumerate(TL):
                    for ci, (r0, nr) in enumerate(tile.chunks):
                        acc, bacc = accs[ti_][ci]
                        cs = slice(dt_ * 512, (dt_ + 1) * 512)
                        if res and dt_ == 0:
                            load_x0(tile, ci, tile.Xc[ci], tile.bX[ci])
                        S.op("dve", lambda v: v.scalar_tensor_tensor(out=tile.Xc[ci][:nr, cs], in0=tile.Xc[ci][:nr, cs],
                                                                     scalar=ALPHA, in1=acc[:nr, :],
                                                                     op0=ALU.mult, op1=ALU.add),
                             reads=[tile.bX[ci], bacc], writes=[tile.bX[ci]])
                        if dt_ == 0:
                            early_stats(tile, ci, nr)

        def rope(p, bp, nr, ci, dst, bdst, rope_i):
            pv = p[:nr, :].rearrange("p (h t f) -> p h t f", h=4, t=2)
            cosb = ROPE[rope_i][:nr, ci, 0, :].unsqueeze(1).unsqueeze(1).broadcast_to([nr, 4, 2, 64])
            sinb = ROPE[rope_i][:nr, ci, 1, :].unsqueeze(1).unsqueeze(1).broadcast_to([nr, 4, 2, 64])
            ta = rot("tmp", NTMP)
            A = TMP[ta][:nr, 0:512].rearrange("p (h t f) -> p h t f", h=4, t=2)
            B = TMP[ta][:nr, 512:1024].rearrange("p (h t f) -> p h t f", h=4, t=2)
            S.op("dve", lambda v: v.tensor_tensor(out=A, in0=pv, in1=cosb, op=ALU.mult),
                 reads=[bp, b_ROPE[rope_i]], writes=[b_TMP[ta]])
            S.op("dve", lambda v: v.tensor_tensor(out=B, in0=pv, in1=sinb, op=ALU.mult),
                 reads=[bp, b_ROPE[rope_i]], writes=[b_TMP[ta]])
            dv = dst.rearrange("p (h t f) -> p h t f", h=4, t=2)
            S.op("dve", lambda v: v.tensor_tensor(out=dv[:, :, 0, :], in0=A[:, :, 0, :], in1=B[:, :, 1, :],
                                                  op=ALU.subtract), reads=[b_TMP[ta]], writes=[bdst])
            S.op("dve", lambda v: v.tensor_tensor(out=dv[:, :, 1, :], in0=A[:, :, 1, :], in1=B[:, :, 0, :],
                                                  op=ALU.add), reads=[b_TMP[ta]], writes=[bdst])

        def pool_windows(Xv, bX, n, L, g, dv, bdv, fix=False):
            w = 2 << g
            Lt = 15 + L
            cur = Xv
            bcur = bX
            v0 = 0
            s = 1
            for step in range(g + 1):
                ti = rot("tmp", NTMP)
                new = TMP[ti][:, 0:n * Lt].rearrange("p (n l) -> p n l", n=n)
                lo = v0 + s
                S.op("dve", lambda v, new=new, cur=cur, lo=lo, s=s: v.tensor_tensor(
                    out=new[:, :, lo:Lt], in0=cur[:, :, lo:Lt], in1=cur[:, :, lo - s:Lt - s], op=ALU.add),
                    reads=[bcur], writes=[b_TMP[ti]])
                cur, bcur = new, b_TMP[ti]
                v0 += s
                s *= 2
            S.op("dve", lambda v: v.scalar_tensor_tensor(out=dv, in0=cur[:, :, 15:Lt], scalar=1.0 / w,
                                                         in1=Xv[:, :, 15:Lt], op0=ALU.mult, op1=ALU.subtract),
                 reads=[bcur, bX], writes=[bdv])
            if fix:
                ti = rot("tmp", NTMP)
                tmpv = TMP[ti][:, 0:16]
                S.op("dve", lambda v: v.tensor_tensor(out=tmpv, in0=cur[:, 0, 15:31], in1=INVC[:, g, :], op=ALU.mult),
                     reads=[bcur] + CONST, writes=[b_TMP[ti]])
                S.op("dve", lambda v: v.tensor_tensor(out=dv[:, 0, 0:16], in0=tmpv, in1=Xv[:, 0, 15:31],
                                                      op=ALU.subtract), reads=[b_TMP[ti], bX], writes=[bdv])

        def group_norm_gate_tile(items):
            for (po, bpo, nr, ci) in items:
                pov = po[:, :].rearrange("p (h v) -> p h v", h=4)
                for h in range(4):
                    S.op("dve", lambda v, h=h: v.bn_stats(out=STAT[ci][:nr, h, :], in_=pov[:nr, h, :]), reads=[bpo],
                         writes=[b_STAT[ci]])
            for (po, bpo, nr, ci) in items:
                for h in range(4):
                    S.op("dve", lambda v, h=h: v.bn_aggr(out=MV[ci][:nr, h, :], in_=STAT[ci][:nr, h, :]),
                         reads=[b_STAT[ci]], writes=[b_MV[ci]])
            for (po, bpo, nr, ci) in items:
                S.op("dve", lambda v: v.tensor_scalar(out=RS[ci][:nr, :], in0=MV[ci][:nr, :, 1], scalar1=EPS,
                                                      scalar2=None, op0=ALU.add), reads=[b_MV[ci]], writes=[b_RS[ci]])
            for (po, bpo, nr, ci) in items:
                S.op("act", lambda a: a.sqrt(out=RS[ci][:nr, :], in_=RS[ci][:nr, :]), reads=[b_RS[ci]],
                     writes=[b_RS[ci]])
            for (po, bpo, nr, ci) in items:
                S.op("dve", lambda v: v.reciprocal(out=RS[ci][:nr, :], in_=RS[ci][:nr, :]), reads=[b_RS[ci]],
                     writes=[b_RS[ci]])
            for (po, bpo, nr, ci) in items:
                S.op("dve", lambda v: v.scalar_tensor_tensor(out=NB[ci][:nr, :], in0=MV[ci][:nr, :, 0], scalar=-1.0,
                                                             in1=RS[ci][:nr, :], op0=ALU.mult, op1=ALU.mult),
                     reads=[b_MV[ci], b_RS[ci]], writes=[b_NB[ci]])
            tmps = {}
            for k, (po, bpo, nr, ci) in enumerate(items):
                if k % 2 == 0:
                    tcur = rot("tmp", NTMP)
                tmps[ci] = (tcur, (k % 2) * 512)
                pov = po[:, :].rearrange("p (h v) -> p h v", h=4)
                ti, off = tmps[ci]
                on = TMP[ti][:nr, off:off + 512].rearrange("p (h v) -> p h v", h=4)
                for h in range(4):
                    S.op("act", lambda a, h=h: a.activation(out=on[:, h, :], in_=pov[:nr, h, :], func=AF.Identity,
                                                            bias=NB[ci][:nr, h:h + 1], scale=RS[ci][:nr, h:h + 1]),
                         reads=[bpo, b_NB[ci], b_RS[ci]], writes=[b_TMP[ti]])
            for (po, bpo, nr, ci) in items:
                ti, off = tmps[ci]
                S.op("dve", lambda v: v.tensor_tensor(out=RET[ci][:nr, :], in0=TMP[ti][:nr, off:off + 512],
                                                      in1=SG[:nr, ci, :], op=ALU.mult),
                     reads=[b_TMP[ti], b_SG[ci]], writes=[b_RET[ci]])

        def ret_to_cat(nr, r0, reti, cat, bcat_all):
            bcat = bcat_all[r0 // 128]
            ptv, bpt = psum_t()

            def em(pe):
                for h in range(4):
                    ins = pe.transpose(ptv[:, h, :nr], RET[reti][:nr, h * 128:(h + 1) * 128], IDB[:nr, :nr])
                return ins
            S.op("pe", em, reads=[b_RET[reti]] + CONST, writes=[bpt])
            S.op("act", lambda a: a.copy(out=cat[:, 0:4, r0:r0 + nr], in_=ptv[:, 0:4, :nr]), reads=[bpt], writes=[bcat])

        def qk_transposes(nr, ci):
            ptv, bpt = psum_t()

            def em(pe):
                for h in range(4):
                    pe.transpose(ptv[:, h, :nr], QR[:nr, ci, h * 128:(h + 1) * 128], IDB[:nr, :nr])
                for h in range(4):
                    ins = pe.transpose(ptv[:, 4 + h, :nr], KR[:nr, ci, h * 128:(h + 1) * 128], IDB[:nr, :nr])
                return ins
            S.op("pe", em, reads=[b_QR[ci], b_KR[ci]] + CONST, writes=[bpt])
            qi = ci
            S.op("act", lambda a: a.copy(out=QKT[qi][:, :, :nr], in_=ptv[:, :, :nr]), reads=[bpt], writes=[b_QKT[qi]])
            return ptv, bpt, qi

        mixcut = int(os.environ.get("MIXCUT", "999"))
        mixstage = [0]

        def mc():
            mixstage[0] += 1
            return mixstage[0] >= mixcut

        def mixer(tile, xti, cati, rope_i):
            nt = tile.nt
            m = tile.idx
            xt, bxt = tile.XTs[xti], tile.bXTs[xti][:len(tile.chunks)]
            cat, bcat = tile.XTs[cati], tile.bXTs[cati][:len(tile.chunks)]
            W, bW = w_acquire()
            if m == 4:
                p, bp = psum()

                def mmp(pe):
                    for k in range(8):
                        ins = pe.matmul(p[:80, :], lhsT=xt[:, k, 0:80], rhs=W[:, k, :], start=(k == 0), stop=(k == 7))
                    return ins
                S.op("pe", mmp, reads=[bW] + bxt, writes=[bp])
                tpk = rot("tmp", NTMP)
                PTOK = TMP[tpk]; b_PTOK = b_TMP[tpk]
                S.op("act", lambda a: a.copy(out=PTOK[:80, 0:512], in_=p[:80, :]), reads=[bp], writes=[b_PTOK])
                for i in range(4):
                    srcap = bass.AP(PTOK, i * D, [[4 * D, 16], [1, 512]])
                    S.dma_sp(lambda q, i=i, srcap=srcap: q.dma_start(out=nps[:, 11 + i, :], in_=srcap),
                             reads=[b_PTOK], is_out=True)
                S.dma_sp(lambda q: q.dma_start(out=npp[:, :], in_=PTOK[65:80, 0:512]), reads=[b_PTOK], is_out=True)
            pps = []
            pos_ = []
            for g in range(4):
                pp, bpp = psum()

                def mmq(pe):
                    for k in range(8):
                        ins = pe.matmul(pp[:, :nt], lhsT=W[:, k, g * 128:(g + 1) * 128], rhs=xt[:, k, :nt],
                                        start=(k == 0), stop=(k == 7))
                    return ins
                S.op("pe", mmq, reads=[bW] + bxt, writes=[bpp])
                pps.append((pp, bpp))
            for g in range(4):
                pp, bpp = pps[g]
                di = g
                if m < 4:
                    pi = rot("px", 2)
                    S.op("act", lambda a: a.copy(out=PX[pi][:, 0:15], in_=CARRY[:, g, 0:15]), reads=[b_CARRY[g]],
                         writes=[b_PX[pi]])
                    S.op("act", lambda a: a.copy(out=PX[pi][:, 15:15 + nt], in_=pp[:, :nt]), reads=[bpp],
                         writes=[b_PX[pi]])
                    S.op("act", lambda a: a.copy(out=CARRY[:, g, 0:15], in_=PX[pi][:, nt:nt + 15]), reads=[b_PX[pi]],
                         writes=[b_CARRY[g]])
                    Xv = PX[pi][:, 0:15 + nt].rearrange("p (n l) -> p n l", n=1)
                    dv = DT[di][:, 0:nt].rearrange("p (n l) -> p n l", n=1)
                    pool_windows(Xv, b_PX[pi], 1, nt, g, dv, b_DT[di], fix=(m == 0))
                else:
                    xi = rot("px", 2)
                    xa = XP4[xi][:, 0:16 * 19].rearrange("p (n l) -> p n l", n=16)
                    xb_ = XP4[xi][:, 16 * 19:16 * 19 + 31].rearrange("p (n l) -> p n l", n=1)
                    S.op("act", lambda a: a.copy(out=xa[:, :, 0:15], in_=SPT[:, g, :, :]), reads=[b_SPT],
                         writes=[b_XP4[xi]])
                    S.op("act", lambda a: a.copy(out=xa[:, :, 15:19],
                                                 in_=pp[:, 0:64].rearrange("p (n l) -> p n l", n=16)),
                         reads=[bpp], writes=[b_XP4[xi]])
                    S.op("act", lambda a: a.copy(out=xb_[:, 0, 0:15], in_=CARRY[:, g, 0:15]), reads=[b_CARRY[g]],
                         writes=[b_XP4[xi]])
                    S.op("act", lambda a: a.copy(out=xb_[:, 0, 15:31], in_=pp[:, 64:80]), reads=[bpp],
                         writes=[b_XP4[xi]])
                    dva = DT[di][:, 0:64].rearrange("p (n l) -> p n l", n=16)
                    dvb = DT[di][:, 64:80].rearrange("p (n l) -> p n l", n=1)
                    pool_windows(xa, b_XP4[xi], 16, 4, g, dva, b_DT[di])
                    pool_windows(xb_, b_XP4[xi], 1, 16, g, dvb, b_DT[di])
            for gi in range(4):
                W, bW = w_acquire()
                for ci, (r0, nr) in enumerate(tile.chunks):
                    p, bp = psum()

                    def mm(pe):
                        for k in range(8):
                            ins = pe.matmul(p[:nr, :], lhsT=xt[:, k, r0:r0 + nr], rhs=W[:, k, :],
                                            start=(k == 0), stop=(k == 7))
                        return ins
                    S.op("pe", mm, reads=[bW, bxt[ci]], writes=[bp])
                    if gi == 0:
                        rope(p, bp, nr, ci, QR[:nr, ci, :], b_QR[ci], rope_i)
                    elif gi == 1:
                        rope(p, bp, nr, ci, KR[:nr, ci, :], b_KR[ci], rope_i)
                    elif gi == 2:
                        S.op("act", lambda a: a.copy(out=VV[:nr, ci, :], in_=p[:nr, :]), reads=[bp], writes=[b_VV[ci]])
                    else:
                        S.op("act", lambda a: a.activation(out=SG[:nr, ci, :], in_=p[:nr, :], func=AF.Silu),
                             reads=[bp], writes=[b_SG[ci]])
                if mc():
                    return
            for g in range(4):
                po, bpo = psum()
                S.op("pe", lambda pe: pe.matmul(po[:, :nt], lhsT=POOLW[:, g, :], rhs=DT[g][:, :nt], start=True,
                                                stop=True), reads=[b_DT[g]] + CONST, writes=[bpo])
                S.op("act", lambda a: a.activation(out=cat[:, 4 + g, :nt], in_=po[:, :nt], func=AF.Copy,
                                                   scale=PSC[:, g:g + 1]), reads=[bpo] + CONST, writes=bcat)
            if mc():
                return
            if m < 4:
                CH = list(enumerate(tile.chunks))
                for ci, (r0, nr) in CH:
                    qk_transposes(nr, ci)
                for ci, (r0, nr) in CH:
                    S.op("dve", lambda v: v.tensor_tensor(out=QD[ci][:, :, :nr], in0=QKT[ci][:, 0:4, :nr],
                                                          in1=GT[:, :, :nr], op=ALU.mult),
                         reads=[b_QKT[ci]] + CONST, writes=[b_QD[ci]])
                    for h in range(4):
                        S.op("act", lambda a, h=h: a.activation(
                            out=KD[ci][:nr, h * 128:(h + 1) * 128], in_=KR[:nr, ci, h * 128:(h + 1) * 128],
                            func=AF.Copy, scale=KDEC[:nr, h:h + 1]),
                            reads=[b_KR[ci]] + CONST, writes=[b_KD[ci]])
                for ci, (r0, nr) in CH:
                    ps_s, bps_s = psum()
                    sv = ps_s[:, :].rearrange("p (h i) -> p h i", h=4)

                    def mms(pe):
                        for h in range(4):
                            ins = pe.matmul(sv[:nr, h, :nr], lhsT=QKT[ci][:, 4 + h, :nr], rhs=QKT[ci][:, h, :nr],
                                            start=True, stop=True)
                        return ins
                    S.op("pe", mms, reads=[b_QKT[ci]], writes=[bps_s])
                    S.op("dve", lambda v: v.tensor_tensor(out=MM[ci][:nr, :, :nr], in0=sv[:nr, :, :nr],
                                                          in1=MASK[:nr, :, :nr], op=ALU.mult),
                         reads=[bps_s] + CONST, writes=[b_MM[ci]])
                ps_d_l = {}
                for ci, (r0, nr) in CH:
                    ps_d, bps_d = psum()
                    dvw = ps_d[:, :].rearrange("p (h v) -> p h v", h=4)

                    def mmd(pe):
                        for h in range(4):
                            ins = pe.matmul(dvw[:, h, :], lhsT=KD[ci][:nr, h * 128:(h + 1) * 128],
                                            rhs=VV[:nr, ci, h * 128:(h + 1) * 128], start=True, stop=True)
                        return ins
                    S.op("pe", mmd, reads=[b_KD[ci], b_VV[ci]], writes=[bps_d])
                    ps_d_l[ci] = (dvw, bps_d)
                ps_o_l = {}
                for ci, (r0, nr) in CH:
                    ps_o, bps_o = psum()
                    ov = ps_o[:, :].rearrange("p (h v) -> p h v", h=4)

                    def mmo(pe):
                        for h in range(4):
                            pe.matmul(ov[:nr, h, :], lhsT=MM[ci][:nr, h, :nr], rhs=VV[:nr, ci, h * 128:(h + 1) * 128],
                                      start=True, stop=False)
                            ins = pe.matmul(ov[:nr, h, :], lhsT=QD[ci][:, h, :nr], rhs=SB[ci][:, h, :],
                                            start=False, stop=True)
                        return ins
                    S.op("pe", mmo, reads=[b_MM[ci], b_VV[ci], b_QD[ci], b_SB[ci]], writes=[bps_o])
                    ps_o_l[ci] = (ps_o, bps_o)
                    dvw, bps_d = ps_d_l[ci]
                    for h in range(4):
                        S.op("dve", lambda v, h=h: v.scalar_tensor_tensor(
                            out=ST[:, h, :], in0=ST[:, h, :], scalar=float(GAM[h] ** nr), in1=dvw[:, h, :],
                            op0=ALU.mult, op1=ALU.add), reads=[b_ST, bps_d], writes=[b_ST])
                    nsb = (ci + 1) % 4
                    S.op("act", lambda a: a.copy(out=SB[nsb][:], in_=ST[:]), reads=[b_ST], writes=[b_SB[nsb]])
                if mc():
                    return
                group_norm_gate_tile([(ps_o_l[ci][0], ps_o_l[ci][1], nr, ci) for ci, (r0, nr) in CH])
                for ci, (r0, nr) in CH:
                    ret_to_cat(nr, r0, ci, cat, bcat)
            else:
                nr = 80
                ci = 0
                ptv, bpt, qi = qk_transposes(nr, ci)
                S.op("dve", lambda v: v.tensor_tensor(out=QDA[:, :, :80], in0=QKT[qi][:, 0:4, :80], in1=G4[:, :, :],
                                                      op=ALU.mult), reads=[b_QKT[qi]] + CONST, writes=[b_QDA])
                for h in range(4):
                    S.op("dve", lambda v, h=h: v.tensor_scalar(
                        out=KDA[:80, h * 128:(h + 1) * 128], in0=KR[:80, 0, h * 128:(h + 1) * 128],
                        scalar1=KDEC4[:80, h:h + 1], scalar2=None, op0=ALU.mult),
                        reads=[b_KR[0]] + CONST, writes=[b_KDA])
                ps_s, bps_s = psum()
                sv = ps_s[:, :].rearrange("p (h i) -> p h i", h=4)

                def mms(pe):
                    for h in range(4):
                        ins = pe.matmul(sv[:80, h, :80], lhsT=QKT[qi][:, 4 + h, :80], rhs=QKT[qi][:, h, :80],
                                        start=True, stop=True)
                    return ins
                S.op("pe", mms, reads=[b_QKT[qi]], writes=[bps_s])
                mi = rot("mm", 2)
                S.op("dve", lambda v: v.tensor_tensor(out=MM[mi][:80, :, :80], in0=sv[:80, :, :80],
                                                      in1=MASK4[:80, :, :], op=ALU.mult),
                     reads=[bps_s] + CONST, writes=[b_MM[mi]])
                ps_o, bps_o = psum()
                ov = ps_o[:, :].rearrange("p (h v) -> p h v", h=4)

                def mmo(pe):
                    for h in range(4):
                        ins = pe.matmul(ov[:80, h, :], lhsT=MM[mi][:80, h, :80], rhs=VV[:80, 0, h * 128:(h + 1) * 128],
                                        start=(h == 0), stop=False, skip_group_check=True)
                    return ins
                S.op("pe", mmo, reads=[b_MM[mi], b_VV[0]], writes=[bps_o])
                sbi = 0
                PF = 2

                def load_state(bb):
                    fi_ = bb % 3
                    S.dma_sp(lambda q: q.dma_start(out=SF[fi_][:], in_=sret[bb].rearrange("h d v -> d h v")),
                             writes=[b_SF[fi_]])
                    S.op("act", lambda a: a.copy(out=SBB[fi_][:], in_=SF[fi_][:]), reads=[b_SF[fi_]],
                         writes=[b_SBB[fi_]])
                for bb in range(PF):
                    load_state(bb)

                def blk_prep(b):
                    qb = rot("qdb", 2)
                    S.op("dve", lambda v: v.tensor_tensor(
                        out=QDB[qb][:, :, :80], in0=QDA[:, :, :80],
                        in1=BM[:, b, :].unsqueeze(1).broadcast_to([128, 4, 80]), op=ALU.mult),
                        reads=[b_QDA] + CONST, writes=[b_QDB[qb]])
                    kb = rot("kdb", 2)
                    S.op("dve", lambda v: v.tensor_scalar(
                        out=KDB[kb][:80, :], in0=KDA[:80, :], scalar1=BMT[:80, b:b + 1], scalar2=None, op0=ALU.mult),
                        reads=[b_KDA] + CONST, writes=[b_KDB[kb]])
                    return qb, kb
                preps = {0: blk_prep(0)}
                for b in range(17):
                    cexp = 4 if b < 16 else 16
                    if b + PF < 16:
                        load_state(b + PF)
                    if b + 1 < 17:
                        preps[b + 1] = blk_prep(b + 1)
                    if b < 16:
                        fi = b % 3
                        s_f, b_sf, s_b, b_sb = SF[fi], b_SF[fi], SBB[fi], b_SBB[fi]
                    else:
                        s_f, b_sf, s_b, b_sb = ST, b_ST, SB[sbi], b_SB[sbi]
                    qb, kb = preps[b]

                    def mmo2(pe, b=b, qb=qb, s_b=s_b):
                        for h in range(4):
                            ins = pe.matmul(ov[:80, h, :], lhsT=QDB[qb][:, h, :80], rhs=s_b[:, h, :],
                                            start=False, stop=(b == 16), skip_group_check=True)
                        return ins
                    S.op("pe", mmo2, reads=[b_QDB[qb], b_sb], writes=[bps_o])
                    ps_d, bps_d = psum(exclude=bps_o)
                    dvw = ps_d[:, :].rearrange("p (h v) -> p h v", h=4)

                    def mmd(pe, kb=kb, dvw=dvw):
                        for h in range(4):
                            ins = pe.matmul(dvw[:, h, :], lhsT=KDB[kb][:80, h * 128:(h + 1) * 128],
                                            rhs=VV[:80, 0, h * 128:(h + 1) * 128], start=True, stop=True)
                        return ins
                    S.op("pe", mmd, reads=[b_KDB[kb], b_VV[0]], writes=[bps_d])
                    ni = rot("sn", 2)
                    for h in range(4):
                        S.op("dve", lambda v, h=h, ni=ni, s_f=s_f, dvw=dvw: v.scalar_tensor_tensor(
                            out=SN[ni][:, h, :], in0=s_f[:, h, :], scalar=float(GAM[h] ** cexp), in1=dvw[:, h, :],
                            op0=ALU.mult, op1=ALU.add), reads=[b_sf, bps_d], writes=[b_SN[ni]])
                    if b < 16:
                        S.dma_sp(lambda q, b=b, ni=ni: q.dma_start(out=nrs[b].rearrange("h d v -> d h v"), in_=SN[ni][:]),
                                 reads=[b_SN[ni]], is_out=True)
                    else:
                        S.dma_sp(lambda q, ni=ni: q.dma_start(out=nrp.rearrange("h d v -> d h v"), in_=SN[ni][:]),
                                 reads=[b_SN[ni]], is_out=True)
                group_norm_gate_tile([(ps_o, bps_o, nr, 0)])
                ret_to_cat(nr, 0, 0, cat, bcat)
            for dt_ in range(2):
                W, bW = w_acquire()
                for ci, (r0, nr) in enumerate(tile.chunks):
                    p, bp = psum()

                    def mm(pe):
                        for k in range(8):
                            ins = pe.matmul(p[:nr, :], lhsT=cat[:, k, r0:r0 + nr], rhs=W[:, k, :],
                                            start=(k == 0), stop=(k == 7))
                        return ins
                    S.op("pe", mm, reads=[bW, bcat[ci]], writes=[bp])
                    cs = slice(dt_ * 512, (dt_ + 1) * 512)
                    S.op("dve", lambda v: v.scalar_tensor_tensor(out=tile.Xc[ci][:nr, cs], in0=tile.Xc[ci][:nr, cs],
                                                                 scalar=ALPHA, in1=p[:nr, :], op0=ALU.mult, op1=ALU.add),
                         reads=[tile.bX[ci], bp], writes=[tile.bX[ci]])
                    if dt_ == 0:
                        early_stats(tile, ci, nr)

        def prep_x0(tile, xti):
            for ci, (r0, nr) in enumerate(tile.chunks):
                ti = rot("tmp", NTMP)
                load_x0(tile, ci, TMP[ti], b_TMP[ti])
                to_xt(TMP[ti][:nr, :], b_TMP[ti], nr, r0, tile.XTs[xti], tile.bXTs[xti])

        def t4_prologue():
            S.dma_sp(lambda q: q.dma_start(out=nps[:, 0:11, :], in_=spool[:, 4:15, :]), is_out=True)
            for half in range(2):
                tsp = rot("tmp", NTMP)
                S.dma_sp(lambda q, half=half, tsp=tsp: q.dma_start(
                    out=TMP[tsp][0:120, 0:512], in_=spool[half * 8:(half + 1) * 8].rearrange("s r c -> (s r) c")),
                    writes=[b_TMP[tsp]])
                for g in range(4):
                    p, bp = psum()
                    S.op("pe", lambda pe, tsp=tsp, g=g, p=p: pe.matmul(
                        p[:, 0:120], lhsT=TMP[tsp][0:120, g * 128:(g + 1) * 128], rhs=IDF[0:120, 0:120],
                        start=True, stop=True), reads=[b_TMP[tsp]] + CONST, writes=[bp])
                    S.op("act", lambda a, half=half, g=g, p=p: a.copy(
                        out=SPT[:, g, half * 8:(half + 1) * 8, :],
                        in_=p[:, 0:120].rearrange("p (s r) -> p s r", s=8)), reads=[bp], writes=[b_SPT])

        kstop = int(os.environ.get("KSTOP", "999"))
        stage = [0]

        def reached():
            stage[0] += 1
            return stage[0] >= kstop

        def ln3_hook(prev):
            def hook(fg):
                if prev is not None and fg < len(prev.chunks):
                    layer_norm_tile(prev, None, store=True, only=[fg])
                if fg == 4:
                    load_ln(0)
            return hook

        def tile_round(tile, next_tiles, prev=None):
            m = tile.idx
            rope_i = m % 2
            S.dma_sp(lambda q: q.dma_start(out=ROPE[rope_i][:], in_=t_rope[m]), writes=[b_ROPE[rope_i]])
            if prev is not None:
                load_ln(2)
            ffn([(tile, 0, True)], fg_hook=ln3_hook(prev), ln_xti=1)
            load_ln(1)
            mixer(tile, 1, 0, rope_i)
            for nt_ in next_tiles:
                prep_x0(nt_, 0)
            layer_norm_tile(tile, 1)
            ffn([(tile, 1, False)])

        def last_round(t3, t4, prev):
            S.dma_sp(lambda q: q.dma_start(out=ROPE[1][:], in_=t_rope[3]), writes=[b_ROPE[1]])
            load_ln(2)
            ffn([(t3, 0, True), (t4, 0, True)], fg_hook=ln3_hook(prev), ln_xti=1)
            S.dma_sp(lambda q: q.dma_start(out=ROPE[0][:], in_=t_rope[4]), writes=[b_ROPE[0]])
            t4_prologue()
            load_ln(1)
            mixer(t3, 1, 0, 1)
            layer_norm_tile(t3, 1)
            mixer(t4, 1, 0, 0)
            layer_norm_tile(t4, 1)
            load_ln(2)
            ffn([(t3, 1, False), (t4, 1, False)])
            layer_norm_tile(t3, None, store=True)
            layer_norm_tile(t4, None, store=True)

        def main():
            if kstop == 0:
                return
            prep_x0(tiles[0], 0)
            tile_round(tiles[0], [tiles[1]])
            tile_round(tiles[1], [tiles[2]], prev=tiles[0])
            tile_round(tiles[2], [tiles[3], tiles[4]], prev=tiles[1])
            last_round(tiles[3], tiles[4], tiles[2])

        main()
        if os.environ.get("KDUMP") == "1":
            for ci in range(4):
                store_out(tiles[0], ci)
        for tok in S.out_toks:
            S.wait("sp", tok)
        for i in range(N_SP_SEMS):
            if S.sp_cnt[i] > 0:
                S.wait("sp", Tok(S.sp_sems[i], S.sp_cnt[i]))
        for st_ in wsem + [pwsem]:
            if st_[1] > 0:
                S.wait("sp", Tok(st_[0], st_[1]))
        for e in ("pe", "act", "dve", "pool"):
            if S.cnt[e] > 0:
                S.wait("sp", Tok(S.sem[e], S.cnt[e]))
    return nc


_CACHE = {}


def kernel(x_prompt, x_sample, state_ret, state_pool, meta_tokens,
           ffn1_w_gate, ffn1_w_up, ffn1_w_down, ln1_g, ln1_b, w_in, pool_w, pool_scale, w_out,
           ln2_g, ln2_b, ffn2_w_gate, ffn2_w_up, ffn2_w_down, ln3_g, ln3_b):
    f = lambda a: np.ascontiguousarray(np.asarray(a, dtype=np.float32))
    if "nc" not in _CACHE:
        _CACHE["nc"] = build_program()
        _CACHE["tabs"] = _host_tables()
    nc = _CACHE["nc"]
    tabs = _CACHE["tabs"]
    shared = {
        "meta": f(meta_tokens),
        "w1g": f(ffn1_w_gate[0]), "w1u": f(ffn1_w_up[0]), "w1d": f(ffn1_w_down[0]),
        "w2g": f(ffn2_w_gate[0]), "w2u": f(ffn2_w_up[0]), "w2d": f(ffn2_w_down[0]),
        "win": f(w_in[0]), "wout": f(w_out[0]), "poolw": f(pool_w[0]),
        "pscale": f(np.asarray(pool_scale[0]).reshape(4, 128).T),
        "ln1g": f(ln1_g), "ln1b": f(ln1_b), "ln2g": f(ln2_g), "ln2b": f(ln2_b), "ln3g": f(ln3_g), "ln3b": f(ln3_b),
    }
    shared.update(tabs)
    xpf, xsf, srf, spf = f(x_prompt), f(x_sample), f(state_ret), f(state_pool)
    in_maps = []
    for c in range(8):
        d = dict(shared)
        d["xp"] = xpf[c]
        d["xs"] = np.ascontiguousarray(xsf[16 * c:16 * c + 16].reshape(64, D))
        d["sret"] = np.ascontiguousarray(srf[0, 16 * c:16 * c + 16])
        d["spool"] = np.ascontiguousarray(spf[0, 16 * c:16 * c + 16])
        in_maps.append(d)
    res = run_bass_kernel_spmd(nc, in_maps, core_ids=list(range(8)))
    R = res.results
    y_prompt = np.stack([R[c]["yp"] for c in range(8)], 0).astype(np.float32)
    y_sample = np.concatenate([R[c]["ys"].reshape(16, 4, D) for c in range(8)], 0).astype(np.float32)
    new_ret_p = np.stack([R[c]["nrp"] for c in range(8)], 0)[None].astype(np.float32)
    new_pool_p = np.stack([R[c]["npp"] for c in range(8)], 0)[None].astype(np.float32)
    new_ret_s = np.concatenate([R[c]["nrs"] for c in range(8)], 0)[None].astype(np.float32)
    new_pool_s = np.concatenate([R[c]["nps"] for c in range(8)], 0)[None].astype(np.float32)
    return (y_prompt, y_sample, new_ret_p, new_pool_p, new_ret_s, new_pool_s)
```

```python
import contextlib
import numpy as np
import concourse.bass as bass
import concourse.mybir as mybir
from concourse.bass_utils import run_bass_kernel_spmd

F32 = mybir.dt.float32
BF16 = mybir.dt.bfloat16
AF = mybir.ActivationFunctionType
ALU = mybir.AluOpType

D = 1024
DFF = 2816
NFC = 22
SEQ = 2048
NMETA = 16
PAST = 16384
ALPHA = 2.0 ** 0.25
EPS = 1e-5
GAM = [1.0 - 2.0 ** (-5.0 - h) for h in range(4)]
NSLOT = 4
LOOKAHEAD = NSLOT - 2
N_SP_SEMS = 40
import os as _os
SAME_ENG_WINDOW = 10 ** 9 if _os.environ.get("KSAFE") == "1" else 2
SAME_ENG_MODE = _os.environ.get("KSAMEENG", "raw")


class Tok:
    __slots__ = ("sem", "val")

    def __init__(self, sem, val):
        self.sem = sem
        self.val = val


class Buf:
    def __init__(self, name):
        self.name = name
        self.w = None
        self.r = []


class Sched:
    def __init__(self, nc, es):
        self.nc = nc
        self.eng = {"pe": nc.tensor, "act": nc.scalar, "dve": nc.vector, "pool": nc.gpsimd, "sp": nc.sync}
        self.sem = {k: es.enter_context(nc.semaphore("sem_" + k)) for k in self.eng}
        self.cnt = {k: 0 for k in self.eng}
        self.seen = {k: {} for k in self.eng}
        self.es = es
        self.sp_sems = [es.enter_context(nc.semaphore("spd%d" % i)) for i in range(N_SP_SEMS)]
        self.sp_cnt = [0] * N_SP_SEMS
        self.sp_next = 0
        self.out_toks = []

    def wait(self, e, tok, raw=True):
        if tok is None:
            return
        if tok.sem is self.sem[e]:
            if e == "pe":
                return
            if SAME_ENG_WINDOW < 10 ** 9 and not raw:
                return
            if SAME_ENG_MODE == "window" and tok.val <= self.cnt[e] - SAME_ENG_WINDOW:
                return
        key = id(tok.sem)
        if self.seen[e].get(key, 0) >= tok.val:
            return
        self.eng[e].wait_ge(tok.sem, tok.val)
        self.seen[e][key] = tok.val

    def _deps(self, e, reads, writes):
        for b in reads:
            self.wait(e, b.w, raw=True)
        for b in writes:
            self.wait(e, b.w, raw=False)
            for t in b.r:
                self.wait(e, t, raw=False)

    def _commit(self, tok, reads, writes):
        for b in reads:
            b.r.append(tok)
        for b in writes:
            b.w = tok
            b.r = []

    def op(self, e, emit, reads=(), writes=()):
        self._deps(e, reads, writes)
        ins = emit(self.eng[e])
        self.cnt[e] += 1
        tok = Tok(self.sem[e], self.cnt[e])
        ins.then_inc(tok.sem, 1)
        self._commit(tok, reads, writes)
        return tok

    def dma_sp(self, emit, reads=(), writes=(), is_out=False):
        i = self.sp_next
        self.sp_next = (self.sp_next + 1) % N_SP_SEMS
        sem = self.sp_sems[i]
        if self.sp_cnt[i] > 0:
            self.wait("sp", Tok(sem, self.sp_cnt[i]))
        self._deps("sp", reads, writes)
        ins = emit(self.nc.sync)
        self.sp_cnt[i] += 16
        tok = Tok(sem, self.sp_cnt[i])
        ins.then_inc(sem, 16)
        self._commit(tok, reads, writes)
        if is_out:
            self.out_toks.append(tok)
        return tok

    def dma_on(self, e, sem_state, emit, reads=(), writes=()):
        self._deps(e, reads, writes)
        ins = emit(self.eng[e])
        sem_state[1] += 16
        tok = Tok(sem_state[0], sem_state[1])
        ins.then_inc(sem_state[0], 16)
        self._commit(tok, reads, writes)
        return tok


def _host_tables():
    f32 = np.float32
    lg = np.log(np.array(GAM, dtype=np.float64))
    dk = 128.0 ** -0.5
    inv_freq = (np.float32(10000.0) ** (-np.arange(0, 128, 2, dtype=f32) / f32(128))).astype(f32)

    def rope_tab(pos):
        ang = pos.astype(f32)[:, None] * inv_freq[None, :]
        ang = ang.astype(f32)
        return np.stack([np.cos(ang).astype(f32), np.sin(ang).astype(f32)], axis=1)

    rope = np.zeros((5, 128, 4, 2, 64), f32)
    for m in range(4):
        for c in range(4):
            pos = m * 512 + c * 128 + np.arange(128)
            rope[m, :, c] = rope_tab(pos)
    pos4 = np.concatenate([PAST + (np.arange(64) % 4), 2048 + np.arange(16)])
    rope[4, :80, 0] = rope_tab(pos4)

    i = np.arange(128)
    diff = i[None, :] - i[:, None]
    mask = np.zeros((128, 4, 128), f32)
    g = np.zeros((128, 4, 128), f32)
    kdec = np.zeros((128, 4), f32)
    for h in range(4):
        mask[:, h, :] = np.where(diff >= 0, dk * np.exp(lg[h] * np.maximum(diff, 0)), 0.0)
        g[:, h, :] = (dk * np.exp(lg[h] * (i + 1.0)))[None, :]
        kdec[:, h] = np.exp(lg[h] * (127.0 - i))
    blk = np.concatenate([np.arange(64) // 4, np.full(16, 16)])
    li = np.concatenate([np.arange(64) % 4, np.arange(16)])
    cb = np.concatenate([np.full(64, 4), np.full(16, 16)])
    same = blk[:, None] == blk[None, :]
    d4 = li[None, :] - li[:, None]
    mask4 = np.zeros((128, 4, 80), f32)
    g4 = np.zeros((128, 4, 80), f32)
    kdec4 = np.zeros((128, 4), f32)
    for h in range(4):
        mask4[:80, h, :] = np.where(same & (d4 >= 0), dk * np.exp(lg[h] * np.maximum(d4, 0)), 0.0)
        g4[:, h, :] = (dk * np.exp(lg[h] * (li + 1.0)))[None, :]
        kdec4[:80, h] = np.exp(lg[h] * (cb - 1.0 - li))
    bm = np.zeros((128, 17, 80), f32)
    bmt = np.zeros((128, 17), f32)
    for b in range(17):
        bm[:, b, :] = (blk == b)[None, :]
        bmt[:80, b] = (blk == b)
    invc = np.zeros((128, 4, 16), f32)
    for gi, w in enumerate((2, 4, 8, 16)):
        invc[:, gi, :] = (1.0 / np.minimum(np.arange(16) + 1, w))[None, :]
    ident = np.eye(128, dtype=f32)
    return dict(t_rope=rope, t_mask=mask, t_g=g, t_kdec=kdec, t_mask4=mask4, t_g4=g4, t_kdec4=kdec4,
                t_bm=bm, t_bmt=bmt, t_invc=invc, t_ident=ident)


def build_program():
    nc = bass.Bass("TRN2", target_bir_lowering=False)
    es = contextlib.ExitStack()

    def din(name, shape, dt=F32):
        return nc.dram_tensor(name, list(shape), dt, kind="ExternalInput").ap()

    def dout(name, shape):
        return nc.dram_tensor(name, list(shape), F32, kind="ExternalOutput").ap()

    xp = din("xp", [SEQ, D])
    xs = din("xs", [64, D])
    sret = din("sret", [16, 4, 128, 128])
    spool = din("spool", [16, 15, 512])
    meta = din("meta", [NMETA, D])
    w1g = din("w1g", [D, DFF]); w1u = din("w1u", [D, DFF]); w1d = din("w1d", [DFF, D])
    w2g = din("w2g", [D, DFF]); w2u = din("w2u", [D, DFF]); w2d = din("w2d", [DFF, D])
    win = din("win", [D, 2560]); wout = din("wout", [D, D])
    poolw = din("poolw", [4, 128, 128]); pscale = din("pscale", [128, 4])
    lng = [din("ln%dg" % i, [1, D]) for i in (1, 2, 3)]
    lnb = [din("ln%db" % i, [1, D]) for i in (1, 2, 3)]
    t_rope = din("t_rope", [5, 128, 4, 2, 64])
    t_mask = din("t_mask", [128, 4, 128]); t_g = din("t_g", [128, 4, 128]); t_kdec = din("t_kdec", [128, 4])
    t_mask4 = din("t_mask4", [128, 4, 80]); t_g4 = din("t_g4", [128, 4, 80]); t_kdec4 = din("t_kdec4", [128, 4])
    t_bm = din("t_bm", [128, 17, 80]); t_bmt = din("t_bmt", [128, 17]); t_invc = din("t_invc", [128, 4, 16])
    t_ident = din("t_ident", [128, 128])

    yp = dout("yp", [SEQ, D]); ys = dout("ys", [64, D])
    nrp = dout("nrp", [4, 128, 128]); npp = dout("npp", [15, 512])
    nrs = dout("nrs", [16, 4, 128, 128]); nps = dout("nps", [16, 15, 512])

    import os
    with es:
        S = Sched(nc, es)
        _n = [0]

        def sb(shape, dt, name):
            _n[0] += 1
            return es.enter_context(nc.sbuf_tensor("%s_%d" % (name, _n[0]), list(shape), dt))

        IDB = sb([128, 128], BF16, "idb"); IDF = sb([128, 128], F32, "idf")
        MASK = sb([128, 4, 128], F32, "mask"); GT = sb([128, 4, 128], F32, "gt"); KDEC = sb([128, 4], F32, "kdec")
        MASK4 = sb([128, 4, 80], F32, "mask4"); G4 = sb([128, 4, 80], F32, "g4"); KDEC4 = sb([128, 4], F32, "kdec4")
        BM = sb([128, 17, 80], BF16, "bm"); BMT = sb([128, 17], F32, "bmt"); INVC = sb([128, 4, 16], F32, "invc")
        PSC = sb([128, 4], F32, "psc"); POOLW = sb([128, 4, 128], BF16, "poolw")
        LNT = sb([128, 2, D], F32, "lnt"); b_LNT = Buf("lnt")
        ROPE = [sb([128, 4, 2, 64], F32, "rope") for _ in range(2)]; b_ROPE = [Buf("rope0"), Buf("rope1")]
        X = sb([128, 4, D], F32, "x"); b_X = [Buf("x%d" % c) for c in range(4)]
        XT = [sb([128, 8, 512], BF16, "xt") for _ in range(2)]
        b_XT = [[Buf("xt%d_%d" % (i, c)) for c in range(4)] for i in range(2)]
        XB = [sb([128, D], BF16, "xb") for _ in range(2)]; b_XB = [Buf("xb0"), Buf("xb1")]
        HT = sb([128, NFC, 512], BF16, "ht"); b_HT = [Buf("ht%d" % f) for f in range(NFC)]
        NTMP = 3
        TMP = [sb([128, D], F32, "tmp") for _ in range(NTMP)]; b_TMP = [Buf("tmp%d" % i) for i in range(NTMP)]
        SLOT = [sb([128, 8, 512], BF16, "slot") for _ in range(NSLOT)]; b_SLOT = [Buf("slot%d" % i) for i in range(NSLOT)]
        QR = sb([128, 4, 512], BF16, "qr"); b_QR = [Buf("qr%d" % c) for c in range(4)]
        KR = sb([128, 4, 512], BF16, "kr"); b_KR = [Buf("kr%d" % c) for c in range(4)]
        VV = sb([128, 4, 512], BF16, "vv"); b_VV = [Buf("vv%d" % c) for c in range(4)]
        SG = sb([128, 4, 512], BF16, "sg"); b_SG = [Buf("sg%d" % c) for c in range(4)]
        KD = [sb([128, 512], BF16, "kd") for _ in range(4)]; b_KD = [Buf("kd%d" % i) for i in range(4)]
        QKT = [sb([128, 8, 128], BF16, "qkt") for _ in range(4)]; b_QKT = [Buf("qkt%d" % i) for i in range(4)]
        QD = [sb([128, 4, 128], BF16, "qd") for _ in range(4)]; b_QD = [Buf("qd%d" % i) for i in range(4)]
        MM = [sb([128, 4, 128], BF16, "mm") for _ in range(4)]; b_MM = [Buf("mm%d" % i) for i in range(4)]
        ST = sb([128, 4, 128], F32, "st"); b_ST = Buf("st")
        SB = [sb([128, 4, 128], BF16, "sbb") for _ in range(4)]; b_SB = [Buf("sb%d" % i) for i in range(4)]
        RET = [sb([128, 512], BF16, "ret") for _ in range(4)]; b_RET = [Buf("ret%d" % i) for i in range(4)]
        PX = [sb([128, 528], F32, "px") for _ in range(2)]; b_PX = [Buf("px0"), Buf("px1")]
        DT = [sb([128, 512], BF16, "dt") for _ in range(4)]; b_DT = [Buf("dt%d" % i) for i in range(4)]
        CARRY = sb([128, 4, 16], F32, "carry"); b_CARRY = [Buf("carry%d" % g) for g in range(4)]
        STAT = [sb([128, 4, 6], F32, "stat") for _ in range(4)]; b_STAT = [Buf("stat%d" % i) for i in range(4)]
        MV = [sb([128, 4, 2], F32, "mv") for _ in range(4)]; b_MV = [Buf("mv%d" % i) for i in range(4)]
        RS = [sb([128, 4], F32, "rs") for _ in range(4)]; b_RS = [Buf("rs%d" % i) for i in range(4)]
        NB = [sb([128, 4], F32, "nb") for _ in range(4)]; b_NB = [Buf("nb%d" % i) for i in range(4)]
        SPT = sb([128, 4, 16, 15], F32, "spt"); b_SPT = Buf("spt")
        SF = [sb([128, 4, 128], F32, "sf") for _ in range(3)]; b_SF = [Buf("sf%d" % i) for i in range(3)]
        SBB = [sb([128, 4, 128], BF16, "sbk") for _ in range(3)]; b_SBB = [Buf("sbk%d" % i) for i in range(3)]
        SN = [sb([128, 4, 128], F32, "sn") for _ in range(2)]; b_SN = [Buf("sn0"), Buf("sn1")]
        QDB = [QD[2], QD[3]]; b_QDB = [b_QD[2], b_QD[3]]
        KDB = [KD[2], KD[3]]; b_KDB = [b_KD[2], b_KD[3]]
        QDA = QD[1]; b_QDA = b_QD[1]
        KDA = KD[1]; b_KDA = b_KD[1]
        XP4 = PX; b_XP4 = b_PX

        NPSF = 6
        PSB = [es.enter_context(nc.psum_tensor("ps%d" % i, [128, 512], F32)) for i in range(NPSF)]
        b_PS = [Buf("ps%d" % i) for i in range(NPSF)]
        PST = [es.enter_context(nc.psum_tensor("pst%d" % i, [128, 8, 128], BF16)) for i in range(2)]
        b_PST = [Buf("pst%d" % i) for i in range(2)]
        ps_next = [0, 0]
        if os.environ.get("KMEM") == "1":
            print("SBUF bytes/partition remaining after allocations:", nc.sbuf_bytes_remaining)

        def psum(exclude=None):
            i = ps_next[0]
            if exclude is not None and b_PS[i] is exclude:
                i = (i + 1) % NPSF
            ps_next[0] = (i + 1) % NPSF
            return PSB[i], b_PS[i]

        def psum_t():
            i = ps_next[1]
            ps_next[1] = (i + 1) % 2
            return PST[i], b_PST[i]

        rr = {}

        def rot(key, n):
            i = rr.get(key, 0)
            rr[key] = (i + 1) % n
            return i

        b_const = Buf("const")

        def ld(dst, src):
            S.dma_sp(lambda q: q.dma_start(out=dst, in_=src), writes=[b_const])

        ld(IDF[:], t_ident[:, :]); ld(MASK[:], t_mask[:, :, :]); ld(GT[:], t_g[:, :, :]); ld(KDEC[:], t_kdec[:, :])
        ld(MASK4[:], t_mask4[:, :, :]); ld(G4[:], t_g4[:, :, :]); ld(KDEC4[:], t_kdec4[:, :])
        ld(BMT[:], t_bmt[:, :]); ld(INVC[:], t_invc[:, :, :]); ld(PSC[:], pscale[:, :])
        b_const2 = Buf("const2")
        S.op("dve", lambda v: v.tensor_copy(out=IDB[:], in_=IDF[:]), reads=[b_const], writes=[b_const2])
        S.op("dve", lambda v: v.memset(ST[:], 0.0), writes=[b_ST])
        S.op("dve", lambda v: v.memset(SB[0][:], 0.0), writes=[b_SB[0]])
        S.op("dve", lambda v: v.memset(CARRY[:], 0.0), writes=b_CARRY)
        wsem = [[es.enter_context(nc.semaphore("wsem%d" % i)), 0] for i in range(NSLOT)]
        pwsem = [es.enter_context(nc.semaphore("pwsem")), 0]
        S.dma_on("pool", pwsem, lambda q: q.dma_start(out=POOLW[:], in_=poolw.rearrange("g c d -> c g d")),
                 writes=[b_const2])
        S.dma_on("pool", pwsem, lambda q: q.dma_start(out=BM[:], in_=t_bm[:, :, :]), writes=[b_const2])
        CONST = [b_const, b_const2]

        wlist = []

        def add_ffn(wg, wu, wd):
            for fg in range(6):
                nco = 512 if fg < 5 else 256
                wlist.append((wg, 0, 8, fg * 512, nco))
                wlist.append((wu, 0, 8, fg * 512, nco))
            for dt_ in range(2):
                for (f0, nf) in ((0, 8), (8, 8), (16, 6)):
                    wlist.append((wd, f0 * 128, nf, dt_ * 512, 512))

        def add_mixer():
            for gi in (4, 0, 1, 2, 3):
                wlist.append((win, 0, 8, gi * 512, 512))
            for dt_ in range(2):
                wlist.append((wout, 0, 8, dt_ * 512, 512))

        for _tile in range(3):
            add_ffn(w1g, w1u, w1d)
            add_mixer()
            add_ffn(w2g, w2u, w2d)
        add_ffn(w1g, w1u, w1d)
        add_mixer()
        add_mixer()
        add_ffn(w2g, w2u, w2d)
        wstate = {"emitted": 0, "next": 0}

        NW = 43
        wkey = lambda ent: (id(ent[0]), ent[1], ent[3])
        img_idx = {}
        for ent in wlist[:NW]:
            img_idx[wkey(ent)] = len(img_idx)
        assert len(img_idx) == NW and all(wkey(ent) in img_idx for ent in wlist)
        WSC = []
        for e_ in range(NW):
            (_w, _r0, nch_, _c0, nco_) = wlist[e_]
            WSC.append(nc.dram_tensor("wsc%d" % e_, [128, nch_ * nco_], BF16, kind="Internal").ap())
        b_WSC = [Buf("wsc%d" % e_) for e_ in range(NW)]
        w_uses = [0] * NW
        w_total = [0] * NW
        for ent in wlist:
            w_total[img_idx[wkey(ent)]] += 1

        def w_emit(n):
            (w, r0, nch, c0, nco) = wlist[n]
            si = n % NSLOT
            e_ = img_idx[wkey(wlist[n])]
            img = WSC[e_].rearrange("p (k n) -> p k n", k=nch)
            p_ = w_uses[e_]
            w_uses[e_] += 1
            wb_pass = 1 + (e_ % 3)
            if p_ <= wb_pass:
                src = w[r0:r0 + nch * 128, c0:c0 + nco].rearrange("(k p) n -> p k n", p=128)
                S.dma_on("pool", wsem[si], lambda q: q.dma_start(out=SLOT[si][:, 0:nch, 0:nco], in_=src),
                         writes=[b_SLOT[si]])
                if p_ == wb_pass and p_ < w_total[e_] - 1:
                    S.dma_sp(lambda q: q.dma_start(out=img, in_=SLOT[si][:, 0:nch, 0:nco]),
                             reads=[b_SLOT[si]], writes=[b_WSC[e_]])
            else:
                S.dma_on("pool", wsem[si], lambda q: q.dma_start(out=SLOT[si][:, 0:nch, 0:nco], in_=img),
                         reads=[b_WSC[e_]], writes=[b_SLOT[si]])

        def w_acquire(pending=0):
            n = wstate["next"]
            wstate["next"] += 1
            while wstate["emitted"] <= min(n + NSLOT - 1 - pending, len(wlist) - 1):
                w_emit(wstate["emitted"])
                wstate["emitted"] += 1
            si = n % NSLOT
            return SLOT[si], b_SLOT[si]

        class Tile:
            pass

        tiles = []
        for m in range(4):
            t = Tile(); t.idx = m; t.nt = 512; t.chunks = [(c * 128, 128) for c in range(4)]; tiles.append(t)
        t = Tile(); t.idx = 4; t.nt = 80; t.chunks = [(0, 80)]; tiles.append(t)
        for t in tiles[:4]:
            t.Xc = [X[:, c, :] for c in range(4)]; t.bX = b_X
            t.HT = HT; t.bHT = b_HT; t.XTs = XT; t.bXTs = b_XT
        X4 = sb([128, D], F32, "x4"); HT4 = sb([128, NFC, 80], BF16, "ht4")
        XT4 = [sb([128, 8, 80], BF16, "xt4") for _ in range(2)]
        t = tiles[4]
        t.Xc = [X4[:, :]]; t.bX = [Buf("x4")]
        t.HT = HT4; t.bHT = [Buf("ht4_%d" % f) for f in range(NFC)]
        t.XTs = XT4; t.bXTs = [[Buf("xt4_0")], [Buf("xt4_1")]]

        def load_x0(tile, ci, dst, bd):
            m = tile.idx
            if m == 4:
                S.dma_sp(lambda q: q.dma_start(out=dst[0:64, :], in_=xs[:, :]), writes=[bd])
                S.dma_sp(lambda q: q.dma_start(out=dst[64:80, :], in_=xp[2032:2048, :]), writes=[bd])
            elif m == 0 and ci == 0:
                S.dma_sp(lambda q: q.dma_start(out=dst[0:16, :], in_=meta[:, :]), writes=[bd])
                S.dma_sp(lambda q: q.dma_start(out=dst[16:128, :], in_=xp[0:112, :]), writes=[bd])
            else:
                r0 = 512 * m + 128 * ci - 16
                S.dma_sp(lambda q: q.dma_start(out=dst[:, :], in_=xp[r0:r0 + 128, :]), writes=[bd])

        def store_out(tile, ci):
            m = tile.idx
            src, bs = tile.Xc[ci], tile.bX[ci]
            if m == 4:
                S.dma_sp(lambda q: q.dma_start(out=ys[:, :], in_=src[0:64, :]), reads=[bs], is_out=True)
                S.dma_sp(lambda q: q.dma_start(out=yp[2032:2048, :], in_=src[64:80, :]), reads=[bs], is_out=True)
            elif m == 0 and ci == 0:
                S.dma_sp(lambda q: q.dma_start(out=yp[0:112, :], in_=src[16:128, :]), reads=[bs], is_out=True)
            else:
                r0 = 512 * m + 128 * ci - 16
                S.dma_sp(lambda q: q.dma_start(out=yp[r0:r0 + 128, :], in_=src[:, :]), reads=[bs], is_out=True)

        def to_xt_cast(src_ap, b_src, nr):
            xbi = rot("xb", 2)
            S.op("act", lambda a: a.copy(out=XB[xbi][:nr, :], in_=src_ap), reads=[b_src], writes=[b_XB[xbi]])
            return xbi

        def to_xt_tr(xbi, nr):
            ptv, bpt = psum_t()

            def em(pe):
                for k in range(8):
                    ins = pe.transpose(ptv[:, k, :nr], XB[xbi][:nr, k * 128:(k + 1) * 128], IDB[:nr, :nr])
                return ins
            S.op("pe", em, reads=[b_XB[xbi]] + CONST, writes=[bpt])
            return ptv, bpt

        def to_xt_copy(ptv, bpt, nr, r0, dxt, dbufs):
            S.op("act", lambda a: a.copy(out=dxt[:, :, r0:r0 + nr], in_=ptv[:, :, :nr]),
                 reads=[bpt], writes=[dbufs[r0 // 128]])

        def to_xt(src_ap, b_src, nr, r0, dxt, dbufs):
            xbi = to_xt_cast(src_ap, b_src, nr)
            ptv, bpt = to_xt_tr(xbi, nr)
            to_xt_copy(ptv, bpt, nr, r0, dxt, dbufs)

        def load_ln(i):
            S.dma_sp(lambda q: q.dma_start(out=LNT[:, 0, :], in_=lng[i].partition_broadcast(128)), writes=[b_LNT])
            S.dma_sp(lambda q: q.dma_start(out=LNT[:, 1, :], in_=lnb[i].partition_broadcast(128)), writes=[b_LNT])

        def early_stats(tile, ci, nr):
            if tile.idx < 4:
                S.op("dve", lambda v: v.bn_stats(out=STAT[ci][:nr, 0, :], in_=tile.Xc[ci][:nr, 0:512]),
                     reads=[tile.bX[ci]], writes=[b_STAT[ci]])

        def layer_norm_tile(tile, xti, store=False, only=None):
            CH = [(ci, ch) for ci, ch in enumerate(tile.chunks) if only is None or ci in only]
            for ci, (r0, nr) in CH:
                xa, bx, st = tile.Xc[ci][:nr, :], tile.bX[ci], STAT[ci]
                if tile.idx >= 4:
                    S.op("dve", lambda v: v.bn_stats(out=st[:nr, 0, :], in_=xa[:, 0:512]), reads=[bx], writes=[b_STAT[ci]])
                S.op("dve", lambda v: v.bn_stats(out=st[:nr, 1, :], in_=xa[:, 512:1024]), reads=[bx], writes=[b_STAT[ci]])
            for ci, (r0, nr) in CH:
                S.op("dve", lambda v: v.bn_aggr(out=MV[ci][:nr, 0, :],
                                                in_=STAT[ci][:nr, 0:2, :].rearrange("p a b -> p (a b)")),
                     reads=[b_STAT[ci]], writes=[b_MV[ci]])
            for ci, (r0, nr) in CH:
                S.op("dve", lambda v: v.tensor_scalar(out=RS[ci][:nr, 0:1], in0=MV[ci][:nr, 0, 1:2], scalar1=EPS,
                                                      scalar2=None, op0=ALU.add), reads=[b_MV[ci]], writes=[b_RS[ci]])
            for ci, (r0, nr) in CH:
                S.op("act", lambda a: a.sqrt(out=RS[ci][:nr, 0:1], in_=RS[ci][:nr, 0:1]), reads=[b_RS[ci]],
                     writes=[b_RS[ci]])
            for ci, (r0, nr) in CH:
                S.op("dve", lambda v: v.reciprocal(out=RS[ci][:nr, 0:1], in_=RS[ci][:nr, 0:1]), reads=[b_RS[ci]],
                     writes=[b_RS[ci]])
            for ci, (r0, nr) in CH:
                xa, bx = tile.Xc[ci][:nr, :], tile.bX[ci]
                S.op("dve", lambda v: v.scalar_tensor_tensor(out=xa, in0=xa, scalar=MV[ci][:nr, 0, 0:1],
                                                             in1=LNT[:nr, 0, :], op0=ALU.subtract, op1=ALU.mult),
                     reads=[bx, b_MV[ci], b_LNT], writes=[bx])
                S.op("dve", lambda v: v.scalar_tensor_tensor(out=xa, in0=xa, scalar=RS[ci][:nr, 0:1],
                                                             in1=LNT[:nr, 1, :], op0=ALU.mult, op1=ALU.add),
                     reads=[bx, b_RS[ci], b_LNT], writes=[bx])
                if store:
                    store_out(tile, ci)
            if store:
                return
            pend = []
            for ci, (r0, nr) in CH:
                xbi = to_xt_cast(tile.Xc[ci][:nr, :], tile.bX[ci], nr)
                ptv, bpt = to_xt_tr(xbi, nr)
                pend.append((ptv, bpt, nr, r0))
                if len(pend) == 2:
                    to_xt_copy(*pend.pop(0), tile.XTs[xti], tile.bXTs[xti])
            for p_ in pend:
                to_xt_copy(*p_, tile.XTs[xti], tile.bXTs[xti])

        def ffn(TL, fg_hook=None):
            def gate_up(tile, xti, fg, ncj, Wg, bWg, Wu, bWu):
                nt = tile.nt
                xt, bxt = tile.XTs[xti], tile.bXTs[xti][:len(tile.chunks)]
                HT, b_HT = tile.HT, tile.bHT
                if 4 * nt <= 512:
                    pg, bpg = psum()
                    pu, bpu = psum()

                    def mmb(pe, W, p):
                        for j in range(ncj):
                            for k in range(8):
                                ins = pe.matmul(p[:, j * nt:(j + 1) * nt], lhsT=W[:, k, j * 128:(j + 1) * 128],
                                                rhs=xt[:, k, :nt], start=(k == 0), stop=(k == 7))
                        return ins
                    S.op("pe", lambda pe: mmb(pe, Wg, pg), reads=[bWg] + bxt, writes=[bpg])
                    S.op("pe", lambda pe: mmb(pe, Wu, pu), reads=[bWu] + bxt, writes=[bpu])
                    ti = rot("tmp", NTMP)
                    w_ = ncj * nt
                    S.op("act", lambda a: a.activation(out=TMP[ti][:, :w_], in_=pg[:, :w_], func=AF.Silu),
                         reads=[bpg], writes=[b_TMP[ti]])
                    S.op("dve", lambda v: v.scalar_tensor_tensor(
                        out=HT[:, fg * 4:fg * 4 + ncj, 0:nt],
                        in0=TMP[ti][:, :w_].rearrange("p (j t) -> p j t", j=ncj), scalar=0.5,
                        in1=pu[:, :w_].rearrange("p (j t) -> p j t", j=ncj), op0=ALU.mult, op1=ALU.mult),
                        reads=[b_TMP[ti], bpu], writes=b_HT[fg * 4:fg * 4 + ncj])
                    return
                for j in range(ncj):
                    f = fg * 4 + j
                    pg, bpg = psum()
                    pu, bpu = psum()

                    def mm(pe, W, p):
                        for k in range(8):
                            ins = pe.matmul(p[:, :nt], lhsT=W[:, k, j * 128:(j + 1) * 128], rhs=xt[:, k, :nt],
                                            start=(k == 0), stop=(k == 7))
                        return ins
                    S.op("pe", lambda pe: mm(pe, Wg, pg), reads=[bWg] + bxt, writes=[bpg])
                    S.op("pe", lambda pe: mm(pe, Wu, pu), reads=[bWu] + bxt, writes=[bpu])
                    ti = rot("tmp", NTMP)
                    S.op("act", lambda a: a.activation(out=TMP[ti][:, :nt], in_=pg[:, :nt], func=AF.Silu),
                         reads=[bpg], writes=[b_TMP[ti]])
                    S.op("dve", lambda v: v.scalar_tensor_tensor(out=HT[:, f, :nt], in0=TMP[ti][:, :nt], scalar=0.5,
                                                                 in1=pu[:, :nt], op0=ALU.mult, op1=ALU.mult),
                         reads=[b_TMP[ti], bpu], writes=[b_HT[f]])

            for fg in range(6):
                nco = 512 if fg < 5 else 256
                Wg, bWg = w_acquire()
                Wu, bWu = w_acquire(pending=1)
                for (tile, xti, res) in TL:
                    gate_up(tile, xti, fg, nco // 128, Wg, bWg, Wu, bWu)
                if fg_hook is not None:
                    fg_hook(fg)
            for dt_ in range(2):
                accs = [[psum() for _ in tile.chunks] for (tile, xti, res) in TL]
                for (f0, nf) in ((0, 8), (8, 8), (16, 6)):
                    Wd, bWd = w_acquire()
                    for ti_, (tile, xti, res) in enumerate(TL):
                        HT, b_HT = tile.HT, tile.bHT
                        for ci, (r0, nr) in enumerate(tile.chunks):
                            acc, bacc = accs[ti_][ci]

                            def mm(pe):
                                for kk in range(nf):
                                    f = f0 + kk
                                    ins = pe.matmul(acc[:nr, :], lhsT=HT[:, f, r0:r0 + nr], rhs=Wd[:, kk, :],
                                                    start=(f == 0), stop=(f == NFC - 1))
                                return ins
                            S.op("pe", mm, reads=[bWd] + b_HT[f0:f0 + nf], writes=[bacc])
                for ti_, (tile, xti, res) in enumerate(TL):
                    for ci, (r0, nr) in enumerate(tile.chunks):
                        acc, bacc = accs[ti_][ci]
                        cs = slice(dt_ * 512, (dt_ + 1) * 512)
                        if res and dt_ == 0:
                            load_x0(tile, ci, tile.Xc[ci], tile.bX[ci])
                        S.op("dve", lambda v: v.scalar_tensor_tensor(out=tile.Xc[ci][:nr, cs], in0=tile.Xc[ci][:nr, cs],
                                                                     scalar=ALPHA, in1=acc[:nr, :],
                                                                     op0=ALU.mult, op1=ALU.add),
                             reads=[tile.bX[ci], bacc], writes=[tile.bX[ci]])
                        if dt_ == 0:
                            early_stats(tile, ci, nr)

        def rope(p, bp, nr, ci, dst, bdst, rope_i):
            pv = p[:nr, :].rearrange("p (h t f) -> p h t f", h=4, t=2)
            cosb = ROPE[rope_i][:nr, ci, 0, :].unsqueeze(1).unsqueeze(1).broadcast_to([nr, 4, 2, 64])
            sinb = ROPE[rope_i][:nr, ci, 1, :].unsqueeze(1).unsqueeze(1).broadcast_to([nr, 4, 2, 64])
            ta = rot("tmp", NTMP)
            A = TMP[ta][:nr, 0:512].rearrange("p (h t f) -> p h t f", h=4, t=2)
            B = TMP[ta][:nr, 512:1024].rearrange("p (h t f) -> p h t f", h=4, t=2)
            S.op("dve", lambda v: v.tensor_tensor(out=A, in0=pv, in1=cosb, op=ALU.mult),
                 reads=[bp, b_ROPE[rope_i]], writes=[b_TMP[ta]])
            S.op("dve", lambda v: v.tensor_tensor(out=B, in0=pv, in1=sinb, op=ALU.mult),
                 reads=[bp, b_ROPE[rope_i]], writes=[b_TMP[ta]])
            dv = dst.rearrange("p (h t f) -> p h t f", h=4, t=2)
            S.op("dve", lambda v: v.tensor_tensor(out=dv[:, :, 0, :], in0=A[:, :, 0, :], in1=B[:, :, 1, :],
                                                  op=ALU.subtract), reads=[b_TMP[ta]], writes=[bdst])
            S.op("dve", lambda v: v.tensor_tensor(out=dv[:, :, 1, :], in0=A[:, :, 1, :], in1=B[:, :, 0, :],
                                                  op=ALU.add), reads=[b_TMP[ta]], writes=[bdst])

        def pool_windows(Xv, bX, n, L, g, dv, bdv, fix=False):
            w = 2 << g
            Lt = 15 + L
            cur = Xv
            bcur = bX
            v0 = 0
            s = 1
            for step in range(g + 1):
                ti = rot("tmp", NTMP)
                new = TMP[ti][:, 0:n * Lt].rearrange("p (n l) -> p n l", n=n)
                lo = v0 + s
                S.op("dve", lambda v, new=new, cur=cur, lo=lo, s=s: v.tensor_tensor(
                    out=new[:, :, lo:Lt], in0=cur[:, :, lo:Lt], in1=cur[:, :, lo - s:Lt - s], op=ALU.add),
                    reads=[bcur], writes=[b_TMP[ti]])
                cur, bcur = new, b_TMP[ti]
                v0 += s
                s *= 2
            S.op("dve", lambda v: v.scalar_tensor_tensor(out=dv, in0=cur[:, :, 15:Lt], scalar=1.0 / w,
                                                         in1=Xv[:, :, 15:Lt], op0=ALU.mult, op1=ALU.subtract),
                 reads=[bcur, bX], writes=[bdv])
            if fix:
                ti = rot("tmp", NTMP)
                tmpv = TMP[ti][:, 0:16]
                S.op("dve", lambda v: v.tensor_tensor(out=tmpv, in0=cur[:, 0, 15:31], in1=INVC[:, g, :], op=ALU.mult),
                     reads=[bcur] + CONST, writes=[b_TMP[ti]])
                S.op("dve", lambda v: v.tensor_tensor(out=dv[:, 0, 0:16], in0=tmpv, in1=Xv[:, 0, 15:31],
                                                      op=ALU.subtract), reads=[b_TMP[ti], bX], writes=[bdv])

        def group_norm_gate_tile(items):
            for (po, bpo, nr, ci) in items:
                pov = po[:, :].rearrange("p (h v) -> p h v", h=4)
                for h in range(4):
                    S.op("dve", lambda v, h=h: v.bn_stats(out=STAT[ci][:nr, h, :], in_=pov[:nr, h, :]), reads=[bpo],
                         writes=[b_STAT[ci]])
            for (po, bpo, nr, ci) in items:
                for h in range(4):
                    S.op("dve", lambda v, h=h: v.bn_aggr(out=MV[ci][:nr, h, :], in_=STAT[ci][:nr, h, :]),
                         reads=[b_STAT[ci]], writes=[b_MV[ci]])
            for (po, bpo, nr, ci) in items:
                S.op("dve", lambda v: v.tensor_scalar(out=RS[ci][:nr, :], in0=MV[ci][:nr, :, 1], scalar1=EPS,
                                                      scalar2=None, op0=ALU.add), reads=[b_MV[ci]], writes=[b_RS[ci]])
            for (po, bpo, nr, ci) in items:
                S.op("act", lambda a: a.sqrt(out=RS[ci][:nr, :], in_=RS[ci][:nr, :]), reads=[b_RS[ci]],
                     writes=[b_RS[ci]])
            for (po, bpo, nr, ci) in items:
                S.op("dve", lambda v: v.reciprocal(out=RS[ci][:nr, :], in_=RS[ci][:nr, :]), reads=[b_RS[ci]],
                     writes=[b_RS[ci]])
            for (po, bpo, nr, ci) in items:
                S.op("dve", lambda v: v.scalar_tensor_tensor(out=NB[ci][:nr, :], in0=MV[ci][:nr, :, 0], scalar=-1.0,
                                                             in1=RS[ci][:nr, :], op0=ALU.mult, op1=ALU.mult),
                     reads=[b_MV[ci], b_RS[ci]], writes=[b_NB[ci]])
            tmps = {}
            for k, (po, bpo, nr, ci) in enumerate(items):
                if k % 2 == 0:
                    tcur = rot("tmp", NTMP)
                tmps[ci] = (tcur, (k % 2) * 512)
                pov = po[:, :].rearrange("p (h v) -> p h v", h=4)
                ti, off = tmps[ci]
                on = TMP[ti][:nr, off:off + 512].rearrange("p (h v) -> p h v", h=4)
                for h in range(4):
                    S.op("act", lambda a, h=h: a.activation(out=on[:, h, :], in_=pov[:nr, h, :], func=AF.Identity,
                                                            bias=NB[ci][:nr, h:h + 1], scale=RS[ci][:nr, h:h + 1]),
                         reads=[bpo, b_NB[ci], b_RS[ci]], writes=[b_TMP[ti]])
            for (po, bpo, nr, ci) in items:
                ti, off = tmps[ci]
                S.op("dve", lambda v: v.tensor_tensor(out=RET[ci][:nr, :], in0=TMP[ti][:nr, off:off + 512],
                                                      in1=SG[:nr, ci, :], op=ALU.mult),
                     reads=[b_TMP[ti], b_SG[ci]], writes=[b_RET[ci]])

        def ret_to_cat(nr, r0, reti, cat, bcat_all):
            bcat = bcat_all[r0 // 128]
            ptv, bpt = psum_t()

            def em(pe):
                for h in range(4):
                    ins = pe.transpose(ptv[:, h, :nr], RET[reti][:nr, h * 128:(h + 1) * 128], IDB[:nr, :nr])
                return ins
            S.op("pe", em, reads=[b_RET[reti]] + CONST, writes=[bpt])
            S.op("act", lambda a: a.copy(out=cat[:, 0:4, r0:r0 + nr], in_=ptv[:, 0:4, :nr]), reads=[bpt], writes=[bcat])

        def qk_transposes(nr, ci):
            ptv, bpt = psum_t()

            def em(pe):
                for h in range(4):
                    pe.transpose(ptv[:, h, :nr], QR[:nr, ci, h * 128:(h + 1) * 128], IDB[:nr, :nr])
                for h in range(4):
                    ins = pe.transpose(ptv[:, 4 + h, :nr], KR[:nr, ci, h * 128:(h + 1) * 128], IDB[:nr, :nr])
                return ins
            S.op("pe", em, reads=[b_QR[ci], b_KR[ci]] + CONST, writes=[bpt])
            qi = ci
            S.op("act", lambda a: a.copy(out=QKT[qi][:, :, :nr], in_=ptv[:, :, :nr]), reads=[bpt], writes=[b_QKT[qi]])
            return ptv, bpt, qi

        mixcut = int(os.environ.get("MIXCUT", "999"))
        mixstage = [0]

        def mc():
            mixstage[0] += 1
            return mixstage[0] >= mixcut

        def mixer(tile, xti, cati, rope_i):
            nt = tile.nt
            m = tile.idx
            xt, bxt = tile.XTs[xti], tile.bXTs[xti][:len(tile.chunks)]
            cat, bcat = tile.XTs[cati], tile.bXTs[cati][:len(tile.chunks)]
            W, bW = w_acquire()
            if m == 4:
                p, bp = psum()

                def mmp(pe):
                    for k in range(8):
                        ins = pe.matmul(p[:80, :], lhsT=xt[:, k, 0:80], rhs=W[:, k, :], start=(k == 0), stop=(k == 7))
                    return ins
                S.op("pe", mmp, reads=[bW] + bxt, writes=[bp])
                tpk = rot("tmp", NTMP)
                PTOK = TMP[tpk]; b_PTOK = b_TMP[tpk]
                S.op("act", lambda a: a.copy(out=PTOK[:80, 0:512], in_=p[:80, :]), reads=[bp], writes=[b_PTOK])
                for i in range(4):
                    srcap = bass.AP(PTOK, i * D, [[4 * D, 16], [1, 512]])
                    S.dma_sp(lambda q, i=i, srcap=srcap: q.dma_start(out=nps[:, 11 + i, :], in_=srcap),
                             reads=[b_PTOK], is_out=True)
                S.dma_sp(lambda q: q.dma_start(out=npp[:, :], in_=PTOK[65:80, 0:512]), reads=[b_PTOK], is_out=True)
            pps = []
            pos_ = []
            for g in range(4):
                pp, bpp = psum()

                def mmq(pe):
                    for k in range(8):
                        ins = pe.matmul(pp[:, :nt], lhsT=W[:, k, g * 128:(g + 1) * 128], rhs=xt[:, k, :nt],
                                        start=(k == 0), stop=(k == 7))
                    return ins
                S.op("pe", mmq, reads=[bW] + bxt, writes=[bpp])
                pps.append((pp, bpp))
            for g in range(4):
                pp, bpp = pps[g]
                di = g
                if m < 4:
                    pi = rot("px", 2)
                    S.op("act", lambda a: a.copy(out=PX[pi][:, 0:15], in_=CARRY[:, g, 0:15]), reads=[b_CARRY[g]],
                         writes=[b_PX[pi]])
                    S.op("act", lambda a: a.copy(out=PX[pi][:, 15:15 + nt], in_=pp[:, :nt]), reads=[bpp],
                         writes=[b_PX[pi]])
                    S.op("act", lambda a: a.copy(out=CARRY[:, g, 0:15], in_=PX[pi][:, nt:nt + 15]), reads=[b_PX[pi]],
                         writes=[b_CARRY[g]])
                    Xv = PX[pi][:, 0:15 + nt].rearrange("p (n l) -> p n l", n=1)
                    dv = DT[di][:, 0:nt].rearrange("p (n l) -> p n l", n=1)
                    pool_windows(Xv, b_PX[pi], 1, nt, g, dv, b_DT[di], fix=(m == 0))
                else:
                    xi = rot("px", 2)
                    xa = XP4[xi][:, 0:16 * 19].rearrange("p (n l) -> p n l", n=16)
                    xb_ = XP4[xi][:, 16 * 19:16 * 19 + 31].rearrange("p (n l) -> p n l", n=1)
                    S.op("act", lambda a: a.copy(out=xa[:, :, 0:15], in_=SPT[:, g, :, :]), reads=[b_SPT],
                         writes=[b_XP4[xi]])
                    S.op("act", lambda a: a.copy(out=xa[:, :, 15:19],
                                                 in_=pp[:, 0:64].rearrange("p (n l) -> p n l", n=16)),
                         reads=[bpp], writes=[b_XP4[xi]])
                    S.op("act", lambda a: a.copy(out=xb_[:, 0, 0:15], in_=CARRY[:, g, 0:15]), reads=[b_CARRY[g]],
                         writes=[b_XP4[xi]])
                    S.op("act", lambda a: a.copy(out=xb_[:, 0, 15:31], in_=pp[:, 64:80]), reads=[bpp],
                         writes=[b_XP4[xi]])
                    dva = DT[di][:, 0:64].rearrange("p (n l) -> p n l", n=16)
                    dvb = DT[di][:, 64:80].rearrange("p (n l) -> p n l", n=1)
                    pool_windows(xa, b_XP4[xi], 16, 4, g, dva, b_DT[di])
                    pool_windows(xb_, b_XP4[xi], 1, 16, g, dvb, b_DT[di])
            for gi in range(4):
                W, bW = w_acquire()
                for ci, (r0, nr) in enumerate(tile.chunks):
                    p, bp = psum()

                    def mm(pe):
                        for k in range(8):
                            ins = pe.matmul(p[:nr, :], lhsT=xt[:, k, r0:r0 + nr], rhs=W[:, k, :],
                                            start=(k == 0), stop=(k == 7))
                        return ins
                    S.op("pe", mm, reads=[bW, bxt[ci]], writes=[bp])
                    if gi == 0:
                        rope(p, bp, nr, ci, QR[:nr, ci, :], b_QR[ci], rope_i)
                    elif gi == 1:
                        rope(p, bp, nr, ci, KR[:nr, ci, :], b_KR[ci], rope_i)
                    elif gi == 2:
                        S.op("act", lambda a: a.copy(out=VV[:nr, ci, :], in_=p[:nr, :]), reads=[bp], writes=[b_VV[ci]])
                    else:
                        S.op("act", lambda a: a.activation(out=SG[:nr, ci, :], in_=p[:nr, :], func=AF.Silu),
                             reads=[bp], writes=[b_SG[ci]])
                if mc():
                    return
            for g in range(4):
                po, bpo = psum()
                S.op("pe", lambda pe: pe.matmul(po[:, :nt], lhsT=POOLW[:, g, :], rhs=DT[g][:, :nt], start=True,
                                                stop=True), reads=[b_DT[g]] + CONST, writes=[bpo])
                S.op("act", lambda a: a.activation(out=cat[:, 4 + g, :nt], in_=po[:, :nt], func=AF.Copy,
                                                   scale=PSC[:, g:g + 1]), reads=[bpo] + CONST, writes=bcat)
            if mc():
                return
            if m < 4:
                CH = list(enumerate(tile.chunks))
                for ci, (r0, nr) in CH:
                    qk_transposes(nr, ci)
                for ci, (r0, nr) in CH:
                    S.op("dve", lambda v: v.tensor_tensor(out=QD[ci][:, :, :nr], in0=QKT[ci][:, 0:4, :nr],
                                                          in1=GT[:, :, :nr], op=ALU.mult),
                         reads=[b_QKT[ci]] + CONST, writes=[b_QD[ci]])
                    for h in range(4):
                        S.op("act", lambda a, h=h: a.activation(
                            out=KD[ci][:nr, h * 128:(h + 1) * 128], in_=KR[:nr, ci, h * 128:(h + 1) * 128],
                            func=AF.Copy, scale=KDEC[:nr, h:h + 1]),
                            reads=[b_KR[ci]] + CONST, writes=[b_KD[ci]])
                for ci, (r0, nr) in CH:
                    ps_s, bps_s = psum()
                    sv = ps_s[:, :].rearrange("p (h i) -> p h i", h=4)

                    def mms(pe):
                        for h in range(4):
                            ins = pe.matmul(sv[:nr, h, :nr], lhsT=QKT[ci][:, 4 + h, :nr], rhs=QKT[ci][:, h, :nr],
                                            start=True, stop=True)
                        return ins
                    S.op("pe", mms, reads=[b_QKT[ci]], writes=[bps_s])
                    S.op("dve", lambda v: v.tensor_tensor(out=MM[ci][:nr, :, :nr], in0=sv[:nr, :, :nr],
                                                          in1=MASK[:nr, :, :nr], op=ALU.mult),
                         reads=[bps_s] + CONST, writes=[b_MM[ci]])
                ps_d_l = {}
                for ci, (r0, nr) in CH:
                    ps_d, bps_d = psum()
                    dvw = ps_d[:, :].rearrange("p (h v) -> p h v", h=4)

                    def mmd(pe):
                        for h in range(4):
                            ins = pe.matmul(dvw[:, h, :], lhsT=KD[ci][:nr, h * 128:(h + 1) * 128],
                                            rhs=VV[:nr, ci, h * 128:(h + 1) * 128], start=True, stop=True)
                        return ins
                    S.op("pe", mmd, reads=[b_KD[ci], b_VV[ci]], writes=[bps_d])
                    ps_d_l[ci] = (dvw, bps_d)
                ps_o_l = {}
                for ci, (r0, nr) in CH:
                    ps_o, bps_o = psum()
                    ov = ps_o[:, :].rearrange("p (h v) -> p h v", h=4)

                    def mmo(pe):
                        for h in range(4):
                            pe.matmul(ov[:nr, h, :], lhsT=MM[ci][:nr, h, :nr], rhs=VV[:nr, ci, h * 128:(h + 1) * 128],
                                      start=True, stop=False)
                            ins = pe.matmul(ov[:nr, h, :], lhsT=QD[ci][:, h, :nr], rhs=SB[ci][:, h, :],
                                            start=False, stop=True)
                        return ins
                    S.op("pe", mmo, reads=[b_MM[ci], b_VV[ci], b_QD[ci], b_SB[ci]], writes=[bps_o])
                    ps_o_l[ci] = (ps_o, bps_o)
                    dvw, bps_d = ps_d_l[ci]
                    for h in range(4):
                        S.op("dve", lambda v, h=h: v.scalar_tensor_tensor(
                            out=ST[:, h, :], in0=ST[:, h, :], scalar=float(GAM[h] ** nr), in1=dvw[:, h, :],
                            op0=ALU.mult, op1=ALU.add), reads=[b_ST, bps_d], writes=[b_ST])
                    nsb = (ci + 1) % 4
                    S.op("act", lambda a: a.copy(out=SB[nsb][:], in_=ST[:]), reads=[b_ST], writes=[b_SB[nsb]])
                if mc():
                    return
                group_norm_gate_tile([(ps_o_l[ci][0], ps_o_l[ci][1], nr, ci) for ci, (r0, nr) in CH])
                for ci, (r0, nr) in CH:
                    ret_to_cat(nr, r0, ci, cat, bcat)
            else:
                nr = 80
                ci = 0
                ptv, bpt, qi = qk_transposes(nr, ci)
                S.op("dve", lambda v: v.tensor_tensor(out=QDA[:, :, :80], in0=QKT[qi][:, 0:4, :80], in1=G4[:, :, :],
                                                      op=ALU.mult), reads=[b_QKT[qi]] + CONST, writes=[b_QDA])
                for h in range(4):
                    S.op("dve", lambda v, h=h: v.tensor_scalar(
                        out=KDA[:80, h * 128:(h + 1) * 128], in0=KR[:80, 0, h * 128:(h + 1) * 128],
                        scalar1=KDEC4[:80, h:h + 1], scalar2=None, op0=ALU.mult),
                        reads=[b_KR[0]] + CONST, writes=[b_KDA])
                ps_s, bps_s = psum()
                sv = ps_s[:, :].rearrange("p (h i) -> p h i", h=4)

                def mms(pe):
                    for h in range(4):
                        ins = pe.matmul(sv[:80, h, :80], lhsT=QKT[qi][:, 4 + h, :80], rhs=QKT[qi][:, h, :80],
                                        start=True, stop=True)
                    return ins
                S.op("pe", mms, reads=[b_QKT[qi]], writes=[bps_s])
                mi = rot("mm", 2)
                S.op("dve", lambda v: v.tensor_tensor(out=MM[mi][:80, :, :80], in0=sv[:80, :, :80],
                                                      in1=MASK4[:80, :, :], op=ALU.mult),
                     reads=[bps_s] + CONST, writes=[b_MM[mi]])
                ps_o, bps_o = psum()
                ov = ps_o[:, :].rearrange("p (h v) -> p h v", h=4)

                def mmo(pe):
                    for h in range(4):
                        ins = pe.matmul(ov[:80, h, :], lhsT=MM[mi][:80, h, :80], rhs=VV[:80, 0, h * 128:(h + 1) * 128],
                                        start=(h == 0), stop=False, skip_group_check=True)
                    return ins
                S.op("pe", mmo, reads=[b_MM[mi], b_VV[0]], writes=[bps_o])
                sbi = 0
                PF = 2

                def load_state(bb):
                    fi_ = bb % 3
                    S.dma_sp(lambda q: q.dma_start(out=SF[fi_][:], in_=sret[bb].rearrange("h d v -> d h v")),
                             writes=[b_SF[fi_]])
                    S.op("act", lambda a: a.copy(out=SBB[fi_][:], in_=SF[fi_][:]), reads=[b_SF[fi_]],
                         writes=[b_SBB[fi_]])
                for bb in range(PF):
                    load_state(bb)

                def blk_prep(b):
                    qb = rot("qdb", 2)
                    S.op("dve", lambda v: v.tensor_tensor(
                        out=QDB[qb][:, :, :80], in0=QDA[:, :, :80],
                        in1=BM[:, b, :].unsqueeze(1).broadcast_to([128, 4, 80]), op=ALU.mult),
                        reads=[b_QDA] + CONST, writes=[b_QDB[qb]])
                    kb = rot("kdb", 2)
                    S.op("dve", lambda v: v.tensor_scalar(
                        out=KDB[kb][:80, :], in0=KDA[:80, :], scalar1=BMT[:80, b:b + 1], scalar2=None, op0=ALU.mult),
                        reads=[b_KDA] + CONST, writes=[b_KDB[kb]])
                    return qb, kb
                preps = {0: blk_prep(0)}
                for b in range(17):
                    cexp = 4 if b < 16 else 16
                    if b + PF < 16:
                        load_state(b + PF)
                    if b + 1 < 17:
                        preps[b + 1] = blk_prep(b + 1)
                    if b < 16:
                        fi = b % 3
                        s_f, b_sf, s_b, b_sb = SF[fi], b_SF[fi], SBB[fi], b_SBB[fi]
                    else:
                        s_f, b_sf, s_b, b_sb = ST, b_ST, SB[sbi], b_SB[sbi]
                    qb, kb = preps[b]

                    def mmo2(pe, b=b, qb=qb, s_b=s_b):
                        for h in range(4):
                            ins = pe.matmul(ov[:80, h, :], lhsT=QDB[qb][:, h, :80], rhs=s_b[:, h, :],
                                            start=False, stop=(b == 16), skip_group_check=True)
                        return ins
                    S.op("pe", mmo2, reads=[b_QDB[qb], b_sb], writes=[bps_o])
                    ps_d, bps_d = psum(exclude=bps_o)
                    dvw = ps_d[:, :].rearrange("p (h v) -> p h v", h=4)

                    def mmd(pe, kb=kb, dvw=dvw):
                        for h in range(4):
                            ins = pe.matmul(dvw[:, h, :], lhsT=KDB[kb][:80, h * 128:(h + 1) * 128],
                                            rhs=VV[:80, 0, h * 128:(h + 1) * 128], start=True, stop=True)
                        return ins
                    S.op("pe", mmd, reads=[b_KDB[kb], b_VV[0]], writes=[bps_d])
                    ni = rot("sn", 2)
                    for h in range(4):
                        S.op("dve", lambda v, h=h, ni=ni, s_f=s_f, dvw=dvw: v.scalar_tensor_tensor(
                            out=SN[ni][:, h, :], in0=s_f[:, h, :], scalar=float(GAM[h] ** cexp), in1=dvw[:, h, :],
                            op0=ALU.mult, op1=ALU.add), reads=[b_sf, bps_d], writes=[b_SN[ni]])
                    if b < 16:
                        S.dma_sp(lambda q, b=b, ni=ni: q.dma_start(out=nrs[b].rearrange("h d v -> d h v"), in_=SN[ni][:]),
                                 reads=[b_SN[ni]], is_out=True)
                    else:
                        S.dma_sp(lambda q, ni=ni: q.dma_start(out=nrp.rearrange("h d v -> d h v"), in_=SN[ni][:]),
                                 reads=[b_SN[ni]], is_out=True)
                group_norm_gate_tile([(ps_o, bps_o, nr, 0)])
                ret_to_cat(nr, 0, 0, cat, bcat)
            for dt_ in range(2):
                W, bW = w_acquire()
                for ci, (r0, nr) in enumerate(tile.chunks):
                    p, bp = psum()

                    def mm(pe):
                        for k in range(8):
                            ins = pe.matmul(p[:nr, :], lhsT=cat[:, k, r0:r0 + nr], rhs=W[:, k, :],
                                            start=(k == 0), stop=(k == 7))
                        return ins
                    S.op("pe", mm, reads=[bW, bcat[ci]], writes=[bp])
                    cs = slice(dt_ * 512, (dt_ + 1) * 512)
                    S.op("dve", lambda v: v.scalar_tensor_tensor(out=tile.Xc[ci][:nr, cs], in0=tile.Xc[ci][:nr, cs],
                                                                 scalar=ALPHA, in1=p[:nr, :], op0=ALU.mult, op1=ALU.add),
                         reads=[tile.bX[ci], bp], writes=[tile.bX[ci]])
                    if dt_ == 0:
                        early_stats(tile, ci, nr)

        def prep_x0(tile, xti):
            for ci, (r0, nr) in enumerate(tile.chunks):
                ti = rot("tmp", NTMP)
                load_x0(tile, ci, TMP[ti], b_TMP[ti])
                to_xt(TMP[ti][:nr, :], b_TMP[ti], nr, r0, tile.XTs[xti], tile.bXTs[xti])

        def t4_prologue():
            S.dma_sp(lambda q: q.dma_start(out=nps[:, 0:11, :], in_=spool[:, 4:15, :]), is_out=True)
            for half in range(2):
                tsp = rot("tmp", NTMP)
                S.dma_sp(lambda q, half=half, tsp=tsp: q.dma_start(
                    out=TMP[tsp][0:120, 0:512], in_=spool[half * 8:(half + 1) * 8].rearrange("s r c -> (s r) c")),
                    writes=[b_TMP[tsp]])
                for g in range(4):
                    p, bp = psum()
                    S.op("pe", lambda pe, tsp=tsp, g=g, p=p: pe.matmul(
                        p[:, 0:120], lhsT=TMP[tsp][0:120, g * 128:(g + 1) * 128], rhs=IDF[0:120, 0:120],
                        start=True, stop=True), reads=[b_TMP[tsp]] + CONST, writes=[bp])
                    S.op("act", lambda a, half=half, g=g, p=p: a.copy(
                        out=SPT[:, g, half * 8:(half + 1) * 8, :],
                        in_=p[:, 0:120].rearrange("p (s r) -> p s r", s=8)), reads=[bp], writes=[b_SPT])

        kstop = int(os.environ.get("KSTOP", "999"))
        stage = [0]

        def reached():
            stage[0] += 1
            return stage[0] >= kstop

        def ln3_hook(prev):
            def hook(fg):
                if prev is not None and fg < len(prev.chunks):
                    layer_norm_tile(prev, None, store=True, only=[fg])
                if fg == 4:
                    load_ln(0)
            return hook

        def tile_round(tile, next_tiles, prev=None):
            m = tile.idx
            rope_i = m % 2
            S.dma_sp(lambda q: q.dma_start(out=ROPE[rope_i][:], in_=t_rope[m]), writes=[b_ROPE[rope_i]])
            if prev is not None:
                load_ln(2)
            ffn([(tile, 0, True)], fg_hook=ln3_hook(prev))
            layer_norm_tile(tile, 1)
            load_ln(1)
            mixer(tile, 1, 0, rope_i)
            for nt_ in next_tiles:
                prep_x0(nt_, 0)
            layer_norm_tile(tile, 1)
            ffn([(tile, 1, False)])

        def last_round(t3, t4, prev):
            S.dma_sp(lambda q: q.dma_start(out=ROPE[1][:], in_=t_rope[3]), writes=[b_ROPE[1]])
            load_ln(2)
            ffn([(t3, 0, True), (t4, 0, True)], fg_hook=ln3_hook(prev))
            layer_norm_tile(t3, 1)
            layer_norm_tile(t4, 1)
            S.dma_sp(lambda q: q.dma_start(out=ROPE[0][:], in_=t_rope[4]), writes=[b_ROPE[0]])
            t4_prologue()
            load_ln(1)
            mixer(t3, 1, 0, 1)
            layer_norm_tile(t3, 1)
            mixer(t4, 1, 0, 0)
            layer_norm_tile(t4, 1)
            load_ln(2)
            ffn([(t3, 1, False), (t4, 1, False)])
            layer_norm_tile(t3, None, store=True)
            layer_norm_tile(t4, None, store=True)

        def main():
            if kstop == 0:
                return
            prep_x0(tiles[0], 0)
            tile_round(tiles[0], [tiles[1]])
            tile_round(tiles[1], [tiles[2]], prev=tiles[0])
            tile_round(tiles[2], [tiles[3], tiles[4]], prev=tiles[1])
            last_round(tiles[3], tiles[4], tiles[2])

        main()
        if os.environ.get("KDUMP") == "1":
            for ci in range(4):
                store_out(tiles[0], ci)
        for tok in S.out_toks:
            S.wait("sp", tok)
        for i in range(N_SP_SEMS):
            if S.sp_cnt[i] > 0:
                S.wait("sp", Tok(S.sp_sems[i], S.sp_cnt[i]))
        for st_ in wsem + [pwsem]:
            if st_[1] > 0:
                S.wait("sp", Tok(st_[0], st_[1]))
        for e in ("pe", "act", "dve", "pool"):
            if S.cnt[e] > 0:
                S.wait("sp", Tok(S.sem[e], S.cnt[e]))
    return nc


_CACHE = {}


def kernel(x_prompt, x_sample, state_ret, state_pool, meta_tokens,
           ffn1_w_gate, ffn1_w_up, ffn1_w_down, ln1_g, ln1_b, w_in, pool_w, pool_scale, w_out,
           ln2_g, ln2_b, ffn2_w_gate, ffn2_w_up, ffn2_w_down, ln3_g, ln3_b):
    f = lambda a: np.ascontiguousarray(np.asarray(a, dtype=np.float32))
    if "nc" not in _CACHE:
        _CACHE["nc"] = build_program()
        _CACHE["tabs"] = _host_tables()
    nc = _CACHE["nc"]
    tabs = _CACHE["tabs"]
    shared = {
        "meta": f(meta_tokens),
        "w1g": f(ffn1_w_gate[0]), "w1u": f(ffn1_w_up[0]), "w1d": f(ffn1_w_down[0]),
        "w2g": f(ffn2_w_gate[0]), "w2u": f(ffn2_w_up[0]), "w2d": f(ffn2_w_down[0]),
        "win": f(w_in[0]), "wout": f(w_out[0]), "poolw": f(pool_w[0]),
        "pscale": f(np.asarray(pool_scale[0]).reshape(4, 128).T),
        "ln1g": f(ln1_g), "ln1b": f(ln1_b), "ln2g": f(ln2_g), "ln2b": f(ln2_b), "ln3g": f(ln3_g), "ln3b": f(ln3_b),
    }
    shared.update(tabs)
    xpf, xsf, srf, spf = f(x_prompt), f(x_sample), f(state_ret), f(state_pool)
    in_maps = []
    for c in range(8):
        d = dict(shared)
        d["xp"] = xpf[c]
        d["xs"] = np.ascontiguousarray(xsf[16 * c:16 * c + 16].reshape(64, D))
        d["sret"] = np.ascontiguousarray(srf[0, 16 * c:16 * c + 16])
        d["spool"] = np.ascontiguousarray(spf[0, 16 * c:16 * c + 16])
        in_maps.append(d)
    res = run_bass_kernel_spmd(nc, in_maps, core_ids=list(range(8)))
    R = res.results
    y_prompt = np.stack([R[c]["yp"] for c in range(8)], 0).astype(np.float32)
    y_sample = np.concatenate([R[c]["ys"].reshape(16, 4, D) for c in range(8)], 0).astype(np.float32)
    new_ret_p = np.stack([R[c]["nrp"] for c in range(8)], 0)[None].astype(np.float32)
    new_pool_p = np.stack([R[c]["npp"] for c in range(8)], 0)[None].astype(np.float32)
    new_ret_s = np.concatenate([R[c]["nrs"] for c in range(8)], 0)[None].astype(np.float32)
    new_pool_s = np.concatenate([R[c]["nps"] for c in range(8)], 0)[None].astype(np.float32)
    return (y_prompt, y_sample, new_ret_p, new_pool_p, new_ret_s, new_pool_s)
```

```python
import contextlib
import numpy as np
import concourse.bass as bass
import concourse.mybir as mybir
from concourse.bass_utils import run_bass_kernel_spmd

F32 = mybir.dt.float32
BF16 = mybir.dt.bfloat16
AF = mybir.ActivationFunctionType
ALU = mybir.AluOpType

D = 1024
DFF = 2816
NFC = 22
SEQ = 2048
NMETA = 16
PAST = 16384
ALPHA = 2.0 ** 0.25
EPS = 1e-5
GAM = [1.0 - 2.0 ** (-5.0 - h) for h in range(4)]
NSLOT = 4
LOOKAHEAD = NSLOT - 2
N_SP_SEMS = 40
import os as _os
SAME_ENG_WINDOW = 10 ** 9 if _os.environ.get("KSAFE") == "1" else 2
SAME_ENG_MODE = _os.environ.get("KSAMEENG", "raw")


class Tok:
    __slots__ = ("sem", "val")

    def __init__(self, sem, val):
        self.sem = sem
        self.val = val


class Buf:
    def __init__(self, name):
        self.name = name
        self.w = None
        self.r = []


class Sched:
    def __init__(self, nc, es):
        self.nc = nc
        self.eng = {"pe": nc.tensor, "act": nc.scalar, "dve": nc.vector, "pool": nc.gpsimd, "sp": nc.sync}
        self.sem = {k: es.enter_context(nc.semaphore("sem_" + k)) for k in self.eng}
        self.cnt = {k: 0 for k in self.eng}
        self.seen = {k: {} for k in self.eng}
        self.es = es
        self.sp_sems = [es.enter_context(nc.semaphore("spd%d" % i)) for i in range(N_SP_SEMS)]
        self.sp_cnt = [0] * N_SP_SEMS
        self.sp_next = 0
        self.out_toks = []

    def wait(self, e, tok, raw=True):
        if tok is None:
            return
        if tok.sem is self.sem[e]:
            if e == "pe":
                return
            if SAME_ENG_WINDOW < 10 ** 9 and not raw:
                return
            if SAME_ENG_MODE == "window" and tok.val <= self.cnt[e] - SAME_ENG_WINDOW:
                return
        key = id(tok.sem)
        if self.seen[e].get(key, 0) >= tok.val:
            return
        self.eng[e].wait_ge(tok.sem, tok.val)
        self.seen[e][key] = tok.val

    def _deps(self, e, reads, writes):
        for b in reads:
            self.wait(e, b.w, raw=True)
        for b in writes:
            self.wait(e, b.w, raw=False)
            for t in b.r:
                self.wait(e, t, raw=False)

    def _commit(self, tok, reads, writes):
        for b in reads:
            b.r.append(tok)
        for b in writes:
            b.w = tok
            b.r = []

    def op(self, e, emit, reads=(), writes=()):
        self._deps(e, reads, writes)
        ins = emit(self.eng[e])
        self.cnt[e] += 1
        tok = Tok(self.sem[e], self.cnt[e])
        ins.then_inc(tok.sem, 1)
        self._commit(tok, reads, writes)
        return tok

    def dma_sp(self, emit, reads=(), writes=(), is_out=False):
        i = self.sp_next
        self.sp_next = (self.sp_next + 1) % N_SP_SEMS
        sem = self.sp_sems[i]
        if self.sp_cnt[i] > 0:
            self.wait("sp", Tok(sem, self.sp_cnt[i]))
        self._deps("sp", reads, writes)
        ins = emit(self.nc.sync)
        self.sp_cnt[i] += 16
        tok = Tok(sem, self.sp_cnt[i])
        ins.then_inc(sem, 16)
        self._commit(tok, reads, writes)
        if is_out:
            self.out_toks.append(tok)
        return tok

    def dma_on(self, e, sem_state, emit, reads=(), writes=()):
        self._deps(e, reads, writes)
        ins = emit(self.eng[e])
        sem_state[1] += 16
        tok = Tok(sem_state[0], sem_state[1])
        ins.then_inc(sem_state[0], 16)
        self._commit(tok, reads, writes)
        return tok


def _host_tables():
    f32 = np.float32
    lg = np.log(np.array(GAM, dtype=np.float64))
    dk = 128.0 ** -0.5
    inv_freq = (np.float32(10000.0) ** (-np.arange(0, 128, 2, dtype=f32) / f32(128))).astype(f32)

    def rope_tab(pos):
        ang = pos.astype(f32)[:, None] * inv_freq[None, :]
        ang = ang.astype(f32)
        return np.stack([np.cos(ang).astype(f32), np.sin(ang).astype(f32)], axis=1)

    rope = np.zeros((5, 128, 4, 2, 64), f32)
    for m in range(4):
        for c in range(4):
            pos = m * 512 + c * 128 + np.arange(128)
            rope[m, :, c] = rope_tab(pos)
    pos4 = np.concatenate([PAST + (np.arange(64) % 4), 2048 + np.arange(16)])
    rope[4, :80, 0] = rope_tab(pos4)

    i = np.arange(128)
    diff = i[None, :] - i[:, None]
    mask = np.zeros((128, 4, 128), f32)
    g = np.zeros((128, 4, 128), f32)
    kdec = np.zeros((128, 4), f32)
    for h in range(4):
        mask[:, h, :] = np.where(diff >= 0, dk * np.exp(lg[h] * np.maximum(diff, 0)), 0.0)
        g[:, h, :] = (dk * np.exp(lg[h] * (i + 1.0)))[None, :]
        kdec[:, h] = np.exp(lg[h] * (127.0 - i))
    blk = np.concatenate([np.arange(64) // 4, np.full(16, 16)])
    li = np.concatenate([np.arange(64) % 4, np.arange(16)])
    cb = np.concatenate([np.full(64, 4), np.full(16, 16)])
    same = blk[:, None] == blk[None, :]
    d4 = li[None, :] - li[:, None]
    mask4 = np.zeros((128, 4, 80), f32)
    g4 = np.zeros((128, 4, 80), f32)
    kdec4 = np.zeros((128, 4), f32)
    for h in range(4):
        mask4[:80, h, :] = np.where(same & (d4 >= 0), dk * np.exp(lg[h] * np.maximum(d4, 0)), 0.0)
        g4[:, h, :] = (dk * np.exp(lg[h] * (li + 1.0)))[None, :]
        kdec4[:80, h] = np.exp(lg[h] * (cb - 1.0 - li))
    bm = np.zeros((128, 17, 80), f32)
    bmt = np.zeros((128, 17), f32)
    for b in range(17):
        bm[:, b, :] = (blk == b)[None, :]
        bmt[:80, b] = (blk == b)
    invc = np.zeros((128, 4, 16), f32)
    for gi, w in enumerate((2, 4, 8, 16)):
        invc[:, gi, :] = (1.0 / np.minimum(np.arange(16) + 1, w))[None, :]
    ident = np.eye(128, dtype=f32)
    return dict(t_rope=rope, t_mask=mask, t_g=g, t_kdec=kdec, t_mask4=mask4, t_g4=g4, t_kdec4=kdec4,
                t_bm=bm, t_bmt=bmt, t_invc=invc, t_ident=ident)


def build_program():
    nc = bass.Bass("TRN2", target_bir_lowering=False)
    es = contextlib.ExitStack()

    def din(name, shape, dt=F32):
        return nc.dram_tensor(name, list(shape), dt, kind="ExternalInput").ap()

    def dout(name, shape):
        return nc.dram_tensor(name, list(shape), F32, kind="ExternalOutput").ap()

    xp = din("xp", [SEQ, D])
    xs = din("xs", [64, D])
    sret = din("sret", [16, 4, 128, 128])
    spool = din("spool", [16, 15, 512])
    meta = din("meta", [NMETA, D])
    w1g = din("w1g", [D, DFF]); w1u = din("w1u", [D, DFF]); w1d = din("w1d", [DFF, D])
    w2g = din("w2g", [D, DFF]); w2u = din("w2u", [D, DFF]); w2d = din("w2d", [DFF, D])
    win = din("win", [D, 2560]); wout = din("wout", [D, D])
    poolw = din("poolw", [4, 128, 128]); pscale = din("pscale", [128, 4])
    lng = [din("ln%dg" % i, [1, D]) for i in (1, 2, 3)]
    lnb = [din("ln%db" % i, [1, D]) for i in (1, 2, 3)]
    t_rope = din("t_rope", [5, 128, 4, 2, 64])
    t_mask = din("t_mask", [128, 4, 128]); t_g = din("t_g", [128, 4, 128]); t_kdec = din("t_kdec", [128, 4])
    t_mask4 = din("t_mask4", [128, 4, 80]); t_g4 = din("t_g4", [128, 4, 80]); t_kdec4 = din("t_kdec4", [128, 4])
    t_bm = din("t_bm", [128, 17, 80]); t_bmt = din("t_bmt", [128, 17]); t_invc = din("t_invc", [128, 4, 16])
    t_ident = din("t_ident", [128, 128])

    yp = dout("yp", [SEQ, D]); ys = dout("ys", [64, D])
    nrp = dout("nrp", [4, 128, 128]); npp = dout("npp", [15, 512])
    nrs = dout("nrs", [16, 4, 128, 128]); nps = dout("nps", [16, 15, 512])

    import os
    with es:
        S = Sched(nc, es)
        _n = [0]

        def sb(shape, dt, name):
            _n[0] += 1
            return es.enter_context(nc.sbuf_tensor("%s_%d" % (name, _n[0]), list(shape), dt))

        IDB = sb([128, 128], BF16, "idb"); IDF = sb([128, 128], F32, "idf")
        MASK = sb([128, 4, 128], F32, "mask"); GT = sb([128, 4, 128], F32, "gt"); KDEC = sb([128, 4], F32, "kdec")
        MASK4 = sb([128, 4, 80], F32, "mask4"); G4 = sb([128, 4, 80], F32, "g4"); KDEC4 = sb([128, 4], F32, "kdec4")
        BM = sb([128, 17, 80], BF16, "bm"); BMT = sb([128, 17], F32, "bmt"); INVC = sb([128, 4, 16], F32, "invc")
        PSC = sb([128, 4], F32, "psc"); POOLW = sb([128, 4, 128], BF16, "poolw")
        LNT = sb([128, 2, D], F32, "lnt"); b_LNT = Buf("lnt")
        ROPE = [sb([128, 4, 2, 64], F32, "rope") for _ in range(2)]; b_ROPE = [Buf("rope0"), Buf("rope1")]
        X = sb([128, 4, D], F32, "x"); b_X = [Buf("x%d" % c) for c in range(4)]
        XT = [sb([128, 8, 512], BF16, "xt") for _ in range(2)]
        b_XT = [[Buf("xt%d_%d" % (i, c)) for c in range(4)] for i in range(2)]
        XB = [sb([128, D], BF16, "xb") for _ in range(2)]; b_XB = [Buf("xb0"), Buf("xb1")]
        HT = sb([128, NFC, 512], BF16, "ht"); b_HT = [Buf("ht%d" % f) for f in range(NFC)]
        NTMP = 3
        TMP = [sb([128, D], F32, "tmp") for _ in range(NTMP)]; b_TMP = [Buf("tmp%d" % i) for i in range(NTMP)]
        SLOT = [sb([128, 8, 512], BF16, "slot") for _ in range(NSLOT)]; b_SLOT = [Buf("slot%d" % i) for i in range(NSLOT)]
        QR = sb([128, 4, 512], BF16, "qr"); b_QR = [Buf("qr%d" % c) for c in range(4)]
        KR = sb([128, 4, 512], BF16, "kr"); b_KR = [Buf("kr%d" % c) for c in range(4)]
        VV = sb([128, 4, 512], BF16, "vv"); b_VV = [Buf("vv%d" % c) for c in range(4)]
        SG = sb([128, 4, 512], BF16, "sg"); b_SG = [Buf("sg%d" % c) for c in range(4)]
        KD = [sb([128, 512], BF16, "kd") for _ in range(4)]; b_KD = [Buf("kd%d" % i) for i in range(4)]
        QKT = [sb([128, 8, 128], BF16, "qkt") for _ in range(4)]; b_QKT = [Buf("qkt%d" % i) for i in range(4)]
        QD = [sb([128, 4, 128], BF16, "qd") for _ in range(4)]; b_QD = [Buf("qd%d" % i) for i in range(4)]
        MM = [sb([128, 4, 128], BF16, "mm") for _ in range(4)]; b_MM = [Buf("mm%d" % i) for i in range(4)]
        ST = sb([128, 4, 128], F32, "st"); b_ST = Buf("st")
        SB = [sb([128, 4, 128], BF16, "sbb") for _ in range(4)]; b_SB = [Buf("sb%d" % i) for i in range(4)]
        RET = [sb([128, 512], BF16, "ret") for _ in range(4)]; b_RET = [Buf("ret%d" % i) for i in range(4)]
        PX = [sb([128, 528], F32, "px") for _ in range(2)]; b_PX = [Buf("px0"), Buf("px1")]
        DT = [sb([128, 512], BF16, "dt") for _ in range(4)]; b_DT = [Buf("dt%d" % i) for i in range(4)]
        CARRY = sb([128, 4, 16], F32, "carry"); b_CARRY = [Buf("carry%d" % g) for g in range(4)]
        STAT = [sb([128, 4, 6], F32, "stat") for _ in range(4)]; b_STAT = [Buf("stat%d" % i) for i in range(4)]
        MV = [sb([128, 4, 2], F32, "mv") for _ in range(4)]; b_MV = [Buf("mv%d" % i) for i in range(4)]
        RS = [sb([128, 4], F32, "rs") for _ in range(4)]; b_RS = [Buf("rs%d" % i) for i in range(4)]
        NB = [sb([128, 4], F32, "nb") for _ in range(4)]; b_NB = [Buf("nb%d" % i) for i in range(4)]
        SPT = sb([128, 4, 16, 15], F32, "spt"); b_SPT = Buf("spt")
        SF = [sb([128, 4, 128], F32, "sf") for _ in range(3)]; b_SF = [Buf("sf%d" % i) for i in range(3)]
        SBB = [sb([128, 4, 128], BF16, "sbk") for _ in range(3)]; b_SBB = [Buf("sbk%d" % i) for i in range(3)]
        SN = [sb([128, 4, 128], F32, "sn") for _ in range(2)]; b_SN = [Buf("sn0"), Buf("sn1")]
        QDB = [QD[2], QD[3]]; b_QDB = [b_QD[2], b_QD[3]]
        KDB = [KD[2], KD[3]]; b_KDB = [b_KD[2], b_KD[3]]
        QDA = QD[1]; b_QDA = b_QD[1]
        KDA = KD[1]; b_KDA = b_KD[1]
        XP4 = PX; b_XP4 = b_PX

        NPSF = 6
        PSB = [es.enter_context(nc.psum_tensor("ps%d" % i, [128, 512], F32)) for i in range(NPSF)]
        b_PS = [Buf("ps%d" % i) for i in range(NPSF)]
        PST = [es.enter_context(nc.psum_tensor("pst%d" % i, [128, 8, 128], BF16)) for i in range(2)]
        b_PST = [Buf("pst%d" % i) for i in range(2)]
        ps_next = [0, 0]
        if os.environ.get("KMEM") == "1":
            print("SBUF bytes/partition remaining after allocations:", nc.sbuf_bytes_remaining)

        def psum(exclude=None):
            i = ps_next[0]
            if exclude is not None and b_PS[i] is exclude:
                i = (i + 1) % NPSF
            ps_next[0] = (i + 1) % NPSF
            return PSB[i], b_PS[i]

        def psum_t():
            i = ps_next[1]
            ps_next[1] = (i + 1) % 2
            return PST[i], b_PST[i]

        rr = {}

        def rot(key, n):
            i = rr.get(key, 0)
            rr[key] = (i + 1) % n
            return i

        b_const = Buf("const")

        def ld(dst, src):
            S.dma_sp(lambda q: q.dma_start(out=dst, in_=src), writes=[b_const])

        ld(IDF[:], t_ident[:, :]); ld(MASK[:], t_mask[:, :, :]); ld(GT[:], t_g[:, :, :]); ld(KDEC[:], t_kdec[:, :])
        ld(MASK4[:], t_mask4[:, :, :]); ld(G4[:], t_g4[:, :, :]); ld(KDEC4[:], t_kdec4[:, :])
        ld(BMT[:], t_bmt[:, :]); ld(INVC[:], t_invc[:, :, :]); ld(PSC[:], pscale[:, :])
        b_const2 = Buf("const2")
        S.op("dve", lambda v: v.tensor_copy(out=IDB[:], in_=IDF[:]), reads=[b_const], writes=[b_const2])
        S.op("dve", lambda v: v.memset(ST[:], 0.0), writes=[b_ST])
        S.op("dve", lambda v: v.memset(SB[0][:], 0.0), writes=[b_SB[0]])
        S.op("dve", lambda v: v.memset(CARRY[:], 0.0), writes=b_CARRY)
        wsem = [[es.enter_context(nc.semaphore("wsem%d" % i)), 0] for i in range(NSLOT)]
        pwsem = [es.enter_context(nc.semaphore("pwsem")), 0]
        S.dma_on("pool", pwsem, lambda q: q.dma_start(out=POOLW[:], in_=poolw.rearrange("g c d -> c g d")),
                 writes=[b_const2])
        S.dma_on("pool", pwsem, lambda q: q.dma_start(out=BM[:], in_=t_bm[:, :, :]), writes=[b_const2])
        CONST = [b_const, b_const2]

        wlist = []

        def add_ffn(wg, wu, wd):
            for fg in range(6):
                nco = 512 if fg < 5 else 256
                wlist.append((wg, 0, 8, fg * 512, nco))
                wlist.append((wu, 0, 8, fg * 512, nco))
            for dt_ in range(2):
                for (f0, nf) in ((0, 8), (8, 8), (16, 6)):
                    wlist.append((wd, f0 * 128, nf, dt_ * 512, 512))

        def add_mixer():
            for gi in (0, 4, 1, 2, 3):
                wlist.append((win, 0, 8, gi * 512, 512))
            for dt_ in range(2):
                wlist.append((wout, 0, 8, dt_ * 512, 512))

        for _tile in range(3):
            add_ffn(w1g, w1u, w1d)
            add_mixer()
            add_ffn(w2g, w2u, w2d)
        add_ffn(w1g, w1u, w1d)
        add_mixer()
        add_mixer()
        add_ffn(w2g, w2u, w2d)
        wstate = {"emitted": 0, "next": 0}

        NW = 43
        wkey = lambda ent: (id(ent[0]), ent[1], ent[3])
        img_idx = {}
        for ent in wlist[:NW]:
            img_idx[wkey(ent)] = len(img_idx)
        assert len(img_idx) == NW and all(wkey(ent) in img_idx for ent in wlist)
        WSC = []
        for e_ in range(NW):
            (_w, _r0, nch_, _c0, nco_) = wlist[e_]
            WSC.append(nc.dram_tensor("wsc%d" % e_, [128, nch_ * nco_], BF16, kind="Internal").ap())
        b_WSC = [Buf("wsc%d" % e_) for e_ in range(NW)]
        w_uses = [0] * NW

        def w_emit(n):
            (w, r0, nch, c0, nco) = wlist[n]
            si = n % NSLOT
            e_ = img_idx[wkey(wlist[n])]
            img = WSC[e_].rearrange("p (k n) -> p k n", k=nch)
            p_ = w_uses[e_]
            w_uses[e_] += 1
            wb_pass = e_ % 3
            if p_ <= wb_pass:
                src = w[r0:r0 + nch * 128, c0:c0 + nco].rearrange("(k p) n -> p k n", p=128)
                S.dma_on("pool", wsem[si], lambda q: q.dma_start(out=SLOT[si][:, 0:nch, 0:nco], in_=src),
                         writes=[b_SLOT[si]])
                if p_ == wb_pass:
                    S.dma_sp(lambda q: q.dma_start(out=img, in_=SLOT[si][:, 0:nch, 0:nco]),
                             reads=[b_SLOT[si]], writes=[b_WSC[e_]])
            else:
                S.dma_on("pool", wsem[si], lambda q: q.dma_start(out=SLOT[si][:, 0:nch, 0:nco], in_=img),
                         reads=[b_WSC[e_]], writes=[b_SLOT[si]])

        def w_acquire(pending=0):
            n = wstate["next"]
            wstate["next"] += 1
            while wstate["emitted"] <= min(n + NSLOT - 1 - pending, len(wlist) - 1):
                w_emit(wstate["emitted"])
                wstate["emitted"] += 1
            si = n % NSLOT
            return SLOT[si], b_SLOT[si]

        class Tile:
            pass

        tiles = []
        for m in range(4):
            t = Tile(); t.idx = m; t.nt = 512; t.chunks = [(c * 128, 128) for c in range(4)]; tiles.append(t)
        t = Tile(); t.idx = 4; t.nt = 80; t.chunks = [(0, 80)]; tiles.append(t)
        for t in tiles[:4]:
            t.Xc = [X[:, c, :] for c in range(4)]; t.bX = b_X
            t.HT = HT; t.bHT = b_HT; t.XTs = XT; t.bXTs = b_XT
        X4 = sb([128, D], F32, "x4"); HT4 = sb([128, NFC, 80], BF16, "ht4")
        XT4 = [sb([128, 8, 80], BF16, "xt4") for _ in range(2)]
        t = tiles[4]
        t.Xc = [X4[:, :]]; t.bX = [Buf("x4")]
        t.HT = HT4; t.bHT = [Buf("ht4_%d" % f) for f in range(NFC)]
        t.XTs = XT4; t.bXTs = [[Buf("xt4_0")], [Buf("xt4_1")]]

        def load_x0(tile, ci, dst, bd):
            m = tile.idx
            if m == 4:
                S.dma_sp(lambda q: q.dma_start(out=dst[0:64, :], in_=xs[:, :]), writes=[bd])
                S.dma_sp(lambda q: q.dma_start(out=dst[64:80, :], in_=xp[2032:2048, :]), writes=[bd])
            elif m == 0 and ci == 0:
                S.dma_sp(lambda q: q.dma_start(out=dst[0:16, :], in_=meta[:, :]), writes=[bd])
                S.dma_sp(lambda q: q.dma_start(out=dst[16:128, :], in_=xp[0:112, :]), writes=[bd])
            else:
                r0 = 512 * m + 128 * ci - 16
                S.dma_sp(lambda q: q.dma_start(out=dst[:, :], in_=xp[r0:r0 + 128, :]), writes=[bd])

        def store_out(tile, ci):
            m = tile.idx
            src, bs = tile.Xc[ci], tile.bX[ci]
            if m == 4:
                S.dma_sp(lambda q: q.dma_start(out=ys[:, :], in_=src[0:64, :]), reads=[bs], is_out=True)
                S.dma_sp(lambda q: q.dma_start(out=yp[2032:2048, :], in_=src[64:80, :]), reads=[bs], is_out=True)
            elif m == 0 and ci == 0:
                S.dma_sp(lambda q: q.dma_start(out=yp[0:112, :], in_=src[16:128, :]), reads=[bs], is_out=True)
            else:
                r0 = 512 * m + 128 * ci - 16
                S.dma_sp(lambda q: q.dma_start(out=yp[r0:r0 + 128, :], in_=src[:, :]), reads=[bs], is_out=True)

        def to_xt_cast(src_ap, b_src, nr):
            xbi = rot("xb", 2)
            S.op("act", lambda a: a.copy(out=XB[xbi][:nr, :], in_=src_ap), reads=[b_src], writes=[b_XB[xbi]])
            return xbi

        def to_xt_tr(xbi, nr):
            ptv, bpt = psum_t()

            def em(pe):
                for k in range(8):
                    ins = pe.transpose(ptv[:, k, :nr], XB[xbi][:nr, k * 128:(k + 1) * 128], IDB[:nr, :nr])
                return ins
            S.op("pe", em, reads=[b_XB[xbi]] + CONST, writes=[bpt])
            return ptv, bpt

        def to_xt_copy(ptv, bpt, nr, r0, dxt, dbufs):
            S.op("act", lambda a: a.copy(out=dxt[:, :, r0:r0 + nr], in_=ptv[:, :, :nr]),
                 reads=[bpt], writes=[dbufs[r0 // 128]])

        def to_xt(src_ap, b_src, nr, r0, dxt, dbufs):
            xbi = to_xt_cast(src_ap, b_src, nr)
            ptv, bpt = to_xt_tr(xbi, nr)
            to_xt_copy(ptv, bpt, nr, r0, dxt, dbufs)

        def load_ln(i):
            S.dma_sp(lambda q: q.dma_start(out=LNT[:, 0, :], in_=lng[i].partition_broadcast(128)), writes=[b_LNT])
            S.dma_sp(lambda q: q.dma_start(out=LNT[:, 1, :], in_=lnb[i].partition_broadcast(128)), writes=[b_LNT])

        def early_stats(tile, ci, nr):
            if tile.idx < 4:
                S.op("dve", lambda v: v.bn_stats(out=STAT[ci][:nr, 0, :], in_=tile.Xc[ci][:nr, 0:512]),
                     reads=[tile.bX[ci]], writes=[b_STAT[ci]])

        def layer_norm_tile(tile, xti, store=False, only=None):
            CH = [(ci, ch) for ci, ch in enumerate(tile.chunks) if only is None or ci in only]
            for ci, (r0, nr) in CH:
                xa, bx, st = tile.Xc[ci][:nr, :], tile.bX[ci], STAT[ci]
                if tile.idx >= 4:
                    S.op("dve", lambda v: v.bn_stats(out=st[:nr, 0, :], in_=xa[:, 0:512]), reads=[bx], writes=[b_STAT[ci]])
                S.op("dve", lambda v: v.bn_stats(out=st[:nr, 1, :], in_=xa[:, 512:1024]), reads=[bx], writes=[b_STAT[ci]])
            for ci, (r0, nr) in CH:
                S.op("dve", lambda v: v.bn_aggr(out=MV[ci][:nr, 0, :],
                                                in_=STAT[ci][:nr, 0:2, :].rearrange("p a b -> p (a b)")),
                     reads=[b_STAT[ci]], writes=[b_MV[ci]])
            for ci, (r0, nr) in CH:
                S.op("dve", lambda v: v.tensor_scalar(out=RS[ci][:nr, 0:1], in0=MV[ci][:nr, 0, 1:2], scalar1=EPS,
                                                      scalar2=None, op0=ALU.add), reads=[b_MV[ci]], writes=[b_RS[ci]])
            for ci, (r0, nr) in CH:
                S.op("act", lambda a: a.sqrt(out=RS[ci][:nr, 0:1], in_=RS[ci][:nr, 0:1]), reads=[b_RS[ci]],
                     writes=[b_RS[ci]])
            for ci, (r0, nr) in CH:
                S.op("dve", lambda v: v.reciprocal(out=RS[ci][:nr, 0:1], in_=RS[ci][:nr, 0:1]), reads=[b_RS[ci]],
                     writes=[b_RS[ci]])
            for ci, (r0, nr) in CH:
                xa, bx = tile.Xc[ci][:nr, :], tile.bX[ci]
                S.op("dve", lambda v: v.scalar_tensor_tensor(out=xa, in0=xa, scalar=MV[ci][:nr, 0, 0:1],
                                                             in1=LNT[:nr, 0, :], op0=ALU.subtract, op1=ALU.mult),
                     reads=[bx, b_MV[ci], b_LNT], writes=[bx])
                S.op("dve", lambda v: v.scalar_tensor_tensor(out=xa, in0=xa, scalar=RS[ci][:nr, 0:1],
                                                             in1=LNT[:nr, 1, :], op0=ALU.mult, op1=ALU.add),
                     reads=[bx, b_RS[ci], b_LNT], writes=[bx])
                if store:
                    store_out(tile, ci)
            if store:
                return
            pend = []
            for ci, (r0, nr) in CH:
                xbi = to_xt_cast(tile.Xc[ci][:nr, :], tile.bX[ci], nr)
                ptv, bpt = to_xt_tr(xbi, nr)
                pend.append((ptv, bpt, nr, r0))
                if len(pend) == 2:
                    to_xt_copy(*pend.pop(0), tile.XTs[xti], tile.bXTs[xti])
            for p_ in pend:
                to_xt_copy(*p_, tile.XTs[xti], tile.bXTs[xti])

        def ffn(TL, fg_hook=None):
            def gate_up(tile, xti, fg, ncj, Wg, bWg, Wu, bWu):
                nt = tile.nt
                xt, bxt = tile.XTs[xti], tile.bXTs[xti][:len(tile.chunks)]
                HT, b_HT = tile.HT, tile.bHT
                if 4 * nt <= 512:
                    pg, bpg = psum()
                    pu, bpu = psum()

                    def mmb(pe, W, p):
                        for j in range(ncj):
                            for k in range(8):
                                ins = pe.matmul(p[:, j * nt:(j + 1) * nt], lhsT=W[:, k, j * 128:(j + 1) * 128],
                                                rhs=xt[:, k, :nt], start=(k == 0), stop=(k == 7))
                        return ins
                    S.op("pe", lambda pe: mmb(pe, Wg, pg), reads=[bWg] + bxt, writes=[bpg])
                    S.op("pe", lambda pe: mmb(pe, Wu, pu), reads=[bWu] + bxt, writes=[bpu])
                    ti = rot("tmp", NTMP)
                    w_ = ncj * nt
                    S.op("act", lambda a: a.activation(out=TMP[ti][:, :w_], in_=pg[:, :w_], func=AF.Silu),
                         reads=[bpg], writes=[b_TMP[ti]])
                    S.op("dve", lambda v: v.scalar_tensor_tensor(
                        out=HT[:, fg * 4:fg * 4 + ncj, 0:nt],
                        in0=TMP[ti][:, :w_].rearrange("p (j t) -> p j t", j=ncj), scalar=0.5,
                        in1=pu[:, :w_].rearrange("p (j t) -> p j t", j=ncj), op0=ALU.mult, op1=ALU.mult),
                        reads=[b_TMP[ti], bpu], writes=b_HT[fg * 4:fg * 4 + ncj])
                    return
                for j in range(ncj):
                    f = fg * 4 + j
                    pg, bpg = psum()
                    pu, bpu = psum()

                    def mm(pe, W, p):
                        for k in range(8):
                            ins = pe.matmul(p[:, :nt], lhsT=W[:, k, j * 128:(j + 1) * 128], rhs=xt[:, k, :nt],
                                            start=(k == 0), stop=(k == 7))
                        return ins
                    S.op("pe", lambda pe: mm(pe, Wg, pg), reads=[bWg] + bxt, writes=[bpg])
                    S.op("pe", lambda pe: mm(pe, Wu, pu), reads=[bWu] + bxt, writes=[bpu])
                    ti = rot("tmp", NTMP)
                    S.op("act", lambda a: a.activation(out=TMP[ti][:, :nt], in_=pg[:, :nt], func=AF.Silu),
                         reads=[bpg], writes=[b_TMP[ti]])
                    S.op("dve", lambda v: v.scalar_tensor_tensor(out=HT[:, f, :nt], in0=TMP[ti][:, :nt], scalar=0.5,
                                                                 in1=pu[:, :nt], op0=ALU.mult, op1=ALU.mult),
                         reads=[b_TMP[ti], bpu], writes=[b_HT[f]])

            for fg in range(6):
                nco = 512 if fg < 5 else 256
                Wg, bWg = w_acquire()
                Wu, bWu = w_acquire(pending=1)
                for (tile, xti, res) in TL:
                    gate_up(tile, xti, fg, nco // 128, Wg, bWg, Wu, bWu)
                if fg_hook is not None:
                    fg_hook(fg)
            for dt_ in range(2):
                accs = [[psum() for _ in tile.chunks] for (tile, xti, res) in TL]
                for (f0, nf) in ((0, 8), (8, 8), (16, 6)):
                    Wd, bWd = w_acquire()
                    for ti_, (tile, xti, res) in enumerate(TL):
                        HT, b_HT = tile.HT, tile.bHT
                        for ci, (r0, nr) in enumerate(tile.chunks):
                            acc, bacc = accs[ti_][ci]

                            def mm(pe):
                                for kk in range(nf):
                                    f = f0 + kk
                                    ins = pe.matmul(acc[:nr, :], lhsT=HT[:, f, r0:r0 + nr], rhs=Wd[:, kk, :],
                                                    start=(f == 0), stop=(f == NFC - 1))
                                return ins
                            S.op("pe", mm, reads=[bWd] + b_HT[f0:f0 + nf], writes=[bacc])
                for ti_, (tile, xti, res) in enumerate(TL):
                    for ci, (r0, nr) in enumerate(tile.chunks):
                        acc, bacc = accs[ti_][ci]
                        cs = slice(dt_ * 512, (dt_ + 1) * 512)
                        if res and dt_ == 0:
                            load_x0(tile, ci, tile.Xc[ci], tile.bX[ci])
                        S.op("dve", lambda v: v.scalar_tensor_tensor(out=tile.Xc[ci][:nr, cs], in0=tile.Xc[ci][:nr, cs],
                                                                     scalar=ALPHA, in1=acc[:nr, :],
                                                                     op0=ALU.mult, op1=ALU.add),
                             reads=[tile.bX[ci], bacc], writes=[tile.bX[ci]])
                        if dt_ == 0:
                            early_stats(tile, ci, nr)

        def rope(p, bp, nr, ci, dst, bdst, rope_i):
            pv = p[:nr, :].rearrange("p (h t f) -> p h t f", h=4, t=2)
            cosb = ROPE[rope_i][:nr, ci, 0, :].unsqueeze(1).unsqueeze(1).broadcast_to([nr, 4, 2, 64])
            sinb = ROPE[rope_i][:nr, ci, 1, :].unsqueeze(1).unsqueeze(1).broadcast_to([nr, 4, 2, 64])
            ta = rot("tmp", NTMP)
            A = TMP[ta][:nr, 0:512].rearrange("p (h t f) -> p h t f", h=4, t=2)
            B = TMP[ta][:nr, 512:1024].rearrange("p (h t f) -> p h t f", h=4, t=2)
            S.op("dve", lambda v: v.tensor_tensor(out=A, in0=pv, in1=cosb, op=ALU.mult),
                 reads=[bp, b_ROPE[rope_i]], writes=[b_TMP[ta]])
            S.op("dve", lambda v: v.tensor_tensor(out=B, in0=pv, in1=sinb, op=ALU.mult),
                 reads=[bp, b_ROPE[rope_i]], writes=[b_TMP[ta]])
            dv = dst.rearrange("p (h t f) -> p h t f", h=4, t=2)
            S.op("dve", lambda v: v.tensor_tensor(out=dv[:, :, 0, :], in0=A[:, :, 0, :], in1=B[:, :, 1, :],
                                                  op=ALU.subtract), reads=[b_TMP[ta]], writes=[bdst])
            S.op("dve", lambda v: v.tensor_tensor(out=dv[:, :, 1, :], in0=A[:, :, 1, :], in1=B[:, :, 0, :],
                                                  op=ALU.add), reads=[b_TMP[ta]], writes=[bdst])

        def pool_windows(Xv, bX, n, L, g, dv, bdv, fix=False):
            w = 2 << g
            Lt = 15 + L
            cur = Xv
            bcur = bX
            v0 = 0
            s = 1
            for step in range(g + 1):
                ti = rot("tmp", NTMP)
                new = TMP[ti][:, 0:n * Lt].rearrange("p (n l) -> p n l", n=n)
                lo = v0 + s
                S.op("dve", lambda v, new=new, cur=cur, lo=lo, s=s: v.tensor_tensor(
                    out=new[:, :, lo:Lt], in0=cur[:, :, lo:Lt], in1=cur[:, :, lo - s:Lt - s], op=ALU.add),
                    reads=[bcur], writes=[b_TMP[ti]])
                cur, bcur = new, b_TMP[ti]
                v0 += s
                s *= 2
            S.op("dve", lambda v: v.scalar_tensor_tensor(out=dv, in0=cur[:, :, 15:Lt], scalar=1.0 / w,
                                                         in1=Xv[:, :, 15:Lt], op0=ALU.mult, op1=ALU.subtract),
                 reads=[bcur, bX], writes=[bdv])
            if fix:
                ti = rot("tmp", NTMP)
                tmpv = TMP[ti][:, 0:16]
                S.op("dve", lambda v: v.tensor_tensor(out=tmpv, in0=cur[:, 0, 15:31], in1=INVC[:, g, :], op=ALU.mult),
                     reads=[bcur] + CONST, writes=[b_TMP[ti]])
                S.op("dve", lambda v: v.tensor_tensor(out=dv[:, 0, 0:16], in0=tmpv, in1=Xv[:, 0, 15:31],
                                                      op=ALU.subtract), reads=[b_TMP[ti], bX], writes=[bdv])

        def group_norm_gate_tile(items):
            for (po, bpo, nr, ci) in items:
                pov = po[:, :].rearrange("p (h v) -> p h v", h=4)
                for h in range(4):
                    S.op("dve", lambda v, h=h: v.bn_stats(out=STAT[ci][:nr, h, :], in_=pov[:nr, h, :]), reads=[bpo],
                         writes=[b_STAT[ci]])
            for (po, bpo, nr, ci) in items:
                for h in range(4):
                    S.op("dve", lambda v, h=h: v.bn_aggr(out=MV[ci][:nr, h, :], in_=STAT[ci][:nr, h, :]),
                         reads=[b_STAT[ci]], writes=[b_MV[ci]])
            for (po, bpo, nr, ci) in items:
                S.op("dve", lambda v: v.tensor_scalar(out=RS[ci][:nr, :], in0=MV[ci][:nr, :, 1], scalar1=EPS,
                                                      scalar2=None, op0=ALU.add), reads=[b_MV[ci]], writes=[b_RS[ci]])
            for (po, bpo, nr, ci) in items:
                S.op("act", lambda a: a.sqrt(out=RS[ci][:nr, :], in_=RS[ci][:nr, :]), reads=[b_RS[ci]],
                     writes=[b_RS[ci]])
            for (po, bpo, nr, ci) in items:
                S.op("dve", lambda v: v.reciprocal(out=RS[ci][:nr, :], in_=RS[ci][:nr, :]), reads=[b_RS[ci]],
                     writes=[b_RS[ci]])
            for (po, bpo, nr, ci) in items:
                S.op("dve", lambda v: v.scalar_tensor_tensor(out=NB[ci][:nr, :], in0=MV[ci][:nr, :, 0], scalar=-1.0,
                                                             in1=RS[ci][:nr, :], op0=ALU.mult, op1=ALU.mult),
                     reads=[b_MV[ci], b_RS[ci]], writes=[b_NB[ci]])
            tmps = {}
            for k, (po, bpo, nr, ci) in enumerate(items):
                if k % 2 == 0:
                    tcur = rot("tmp", NTMP)
                tmps[ci] = (tcur, (k % 2) * 512)
                pov = po[:, :].rearrange("p (h v) -> p h v", h=4)
                ti, off = tmps[ci]
                on = TMP[ti][:nr, off:off + 512].rearrange("p (h v) -> p h v", h=4)
                for h in range(4):
                    S.op("act", lambda a, h=h: a.activation(out=on[:, h, :], in_=pov[:nr, h, :], func=AF.Identity,
                                                            bias=NB[ci][:nr, h:h + 1], scale=RS[ci][:nr, h:h + 1]),
                         reads=[bpo, b_NB[ci], b_RS[ci]], writes=[b_TMP[ti]])
            for (po, bpo, nr, ci) in items:
                ti, off = tmps[ci]
                S.op("dve", lambda v: v.tensor_tensor(out=RET[ci][:nr, :], in0=TMP[ti][:nr, off:off + 512],
                                                      in1=SG[:nr, ci, :], op=ALU.mult),
                     reads=[b_TMP[ti], b_SG[ci]], writes=[b_RET[ci]])

        def ret_to_cat(nr, r0, reti, cat, bcat_all):
            bcat = bcat_all[r0 // 128]
            ptv, bpt = psum_t()

            def em(pe):
                for h in range(4):
                    ins = pe.transpose(ptv[:, h, :nr], RET[reti][:nr, h * 128:(h + 1) * 128], IDB[:nr, :nr])
                return ins
            S.op("pe", em, reads=[b_RET[reti]] + CONST, writes=[bpt])
            S.op("act", lambda a: a.copy(out=cat[:, 0:4, r0:r0 + nr], in_=ptv[:, 0:4, :nr]), reads=[bpt], writes=[bcat])

        def qk_transposes(nr, ci):
            ptv, bpt = psum_t()

            def em(pe):
                for h in range(4):
                    pe.transpose(ptv[:, h, :nr], QR[:nr, ci, h * 128:(h + 1) * 128], IDB[:nr, :nr])
                for h in range(4):
                    ins = pe.transpose(ptv[:, 4 + h, :nr], KR[:nr, ci, h * 128:(h + 1) * 128], IDB[:nr, :nr])
                return ins
            S.op("pe", em, reads=[b_QR[ci], b_KR[ci]] + CONST, writes=[bpt])
            qi = ci
            S.op("act", lambda a: a.copy(out=QKT[qi][:, :, :nr], in_=ptv[:, :, :nr]), reads=[bpt], writes=[b_QKT[qi]])
            return ptv, bpt, qi

        mixcut = int(os.environ.get("MIXCUT", "999"))
        mixstage = [0]

        def mc():
            mixstage[0] += 1
            return mixstage[0] >= mixcut

        def mixer(tile, xti, cati, rope_i):
            nt = tile.nt
            m = tile.idx
            xt, bxt = tile.XTs[xti], tile.bXTs[xti][:len(tile.chunks)]
            cat, bcat = tile.XTs[cati], tile.bXTs[cati][:len(tile.chunks)]
            def proj_group(gi):
                W, bW = w_acquire()
                for ci, (r0, nr) in enumerate(tile.chunks):
                    p, bp = psum()

                    def mm(pe):
                        for k in range(8):
                            ins = pe.matmul(p[:nr, :], lhsT=xt[:, k, r0:r0 + nr], rhs=W[:, k, :],
                                            start=(k == 0), stop=(k == 7))
                        return ins
                    S.op("pe", mm, reads=[bW, bxt[ci]], writes=[bp])
                    if gi == 0:
                        rope(p, bp, nr, ci, QR[:nr, ci, :], b_QR[ci], rope_i)
                    elif gi == 1:
                        rope(p, bp, nr, ci, KR[:nr, ci, :], b_KR[ci], rope_i)
                    elif gi == 2:
                        S.op("act", lambda a: a.copy(out=VV[:nr, ci, :], in_=p[:nr, :]), reads=[bp], writes=[b_VV[ci]])
                    else:
                        S.op("act", lambda a: a.activation(out=SG[:nr, ci, :], in_=p[:nr, :], func=AF.Silu),
                             reads=[bp], writes=[b_SG[ci]])
            proj_group(0)
            W, bW = w_acquire()
            if m == 4:
                p, bp = psum()

                def mmp(pe):
                    for k in range(8):
                        ins = pe.matmul(p[:80, :], lhsT=xt[:, k, 0:80], rhs=W[:, k, :], start=(k == 0), stop=(k == 7))
                    return ins
                S.op("pe", mmp, reads=[bW] + bxt, writes=[bp])
                tpk = rot("tmp", NTMP)
                PTOK = TMP[tpk]; b_PTOK = b_TMP[tpk]
                S.op("act", lambda a: a.copy(out=PTOK[:80, 0:512], in_=p[:80, :]), reads=[bp], writes=[b_PTOK])
                for i in range(4):
                    srcap = bass.AP(PTOK, i * D, [[4 * D, 16], [1, 512]])
                    S.dma_sp(lambda q, i=i, srcap=srcap: q.dma_start(out=nps[:, 11 + i, :], in_=srcap),
                             reads=[b_PTOK], is_out=True)
                S.dma_sp(lambda q: q.dma_start(out=npp[:, :], in_=PTOK[65:80, 0:512]), reads=[b_PTOK], is_out=True)
            pps = []
            pos_ = []
            for g in range(4):
                pp, bpp = psum()

                def mmq(pe):
                    for k in range(8):
                        ins = pe.matmul(pp[:, :nt], lhsT=W[:, k, g * 128:(g + 1) * 128], rhs=xt[:, k, :nt],
                                        start=(k == 0), stop=(k == 7))
                    return ins
                S.op("pe", mmq, reads=[bW] + bxt, writes=[bpp])
                pps.append((pp, bpp))
            for g in range(4):
                pp, bpp = pps[g]
                di = g
                if m < 4:
                    pi = rot("px", 2)
                    S.op("act", lambda a: a.copy(out=PX[pi][:, 0:15], in_=CARRY[:, g, 0:15]), reads=[b_CARRY[g]],
                         writes=[b_PX[pi]])
                    S.op("act", lambda a: a.copy(out=PX[pi][:, 15:15 + nt], in_=pp[:, :nt]), reads=[bpp],
                         writes=[b_PX[pi]])
                    S.op("act", lambda a: a.copy(out=CARRY[:, g, 0:15], in_=PX[pi][:, nt:nt + 15]), reads=[b_PX[pi]],
                         writes=[b_CARRY[g]])
                    Xv = PX[pi][:, 0:15 + nt].rearrange("p (n l) -> p n l", n=1)
                    dv = DT[di][:, 0:nt].rearrange("p (n l) -> p n l", n=1)
                    pool_windows(Xv, b_PX[pi], 1, nt, g, dv, b_DT[di], fix=(m == 0))
                else:
                    xi = rot("px", 2)
                    xa = XP4[xi][:, 0:16 * 19].rearrange("p (n l) -> p n l", n=16)
                    xb_ = XP4[xi][:, 16 * 19:16 * 19 + 31].rearrange("p (n l) -> p n l", n=1)
                    S.op("act", lambda a: a.copy(out=xa[:, :, 0:15], in_=SPT[:, g, :, :]), reads=[b_SPT],
                         writes=[b_XP4[xi]])
                    S.op("act", lambda a: a.copy(out=xa[:, :, 15:19],
                                                 in_=pp[:, 0:64].rearrange("p (n l) -> p n l", n=16)),
                         reads=[bpp], writes=[b_XP4[xi]])
                    S.op("act", lambda a: a.copy(out=xb_[:, 0, 0:15], in_=CARRY[:, g, 0:15]), reads=[b_CARRY[g]],
                         writes=[b_XP4[xi]])
                    S.op("act", lambda a: a.copy(out=xb_[:, 0, 15:31], in_=pp[:, 64:80]), reads=[bpp],
                         writes=[b_XP4[xi]])
                    dva = DT[di][:, 0:64].rearrange("p (n l) -> p n l", n=16)
                    dvb = DT[di][:, 64:80].rearrange("p (n l) -> p n l", n=1)
                    pool_windows(xa, b_XP4[xi], 16, 4, g, dva, b_DT[di])
                    pool_windows(xb_, b_XP4[xi], 1, 16, g, dvb, b_DT[di])
            for gi_ in (1, 2, 3):
                proj_group(gi_)
            for g in range(4):
                po, bpo = psum()
                S.op("pe", lambda pe: pe.matmul(po[:, :nt], lhsT=POOLW[:, g, :], rhs=DT[g][:, :nt], start=True,
                                                stop=True), reads=[b_DT[g]] + CONST, writes=[bpo])
                S.op("act", lambda a: a.activation(out=cat[:, 4 + g, :nt], in_=po[:, :nt], func=AF.Copy,
                                                   scale=PSC[:, g:g + 1]), reads=[bpo] + CONST, writes=bcat)
            if mc():
                return
            if m < 4:
                CH = list(enumerate(tile.chunks))
                for ci, (r0, nr) in CH:
                    qk_transposes(nr, ci)
                for ci, (r0, nr) in CH:
                    S.op("dve", lambda v: v.tensor_tensor(out=QD[ci][:, :, :nr], in0=QKT[ci][:, 0:4, :nr],
                                                          in1=GT[:, :, :nr], op=ALU.mult),
                         reads=[b_QKT[ci]] + CONST, writes=[b_QD[ci]])
                    for h in range(4):
                        S.op("act", lambda a, h=h: a.activation(
                            out=KD[ci][:nr, h * 128:(h + 1) * 128], in_=KR[:nr, ci, h * 128:(h + 1) * 128],
                            func=AF.Copy, scale=KDEC[:nr, h:h + 1]),
                            reads=[b_KR[ci]] + CONST, writes=[b_KD[ci]])
                for ci, (r0, nr) in CH:
                    ps_s, bps_s = psum()
                    sv = ps_s[:, :].rearrange("p (h i) -> p h i", h=4)

                    def mms(pe):
                        for h in range(4):
                            ins = pe.matmul(sv[:nr, h, :nr], lhsT=QKT[ci][:, 4 + h, :nr], rhs=QKT[ci][:, h, :nr],
                                            start=True, stop=True)
                        return ins
                    S.op("pe", mms, reads=[b_QKT[ci]], writes=[bps_s])
                    S.op("dve", lambda v: v.tensor_tensor(out=MM[ci][:nr, :, :nr], in0=sv[:nr, :, :nr],
                                                          in1=MASK[:nr, :, :nr], op=ALU.mult),
                         reads=[bps_s] + CONST, writes=[b_MM[ci]])
                ps_d_l = {}
                for ci, (r0, nr) in CH:
                    ps_d, bps_d = psum()
                    dvw = ps_d[:, :].rearrange("p (h v) -> p h v", h=4)

                    def mmd(pe):
                        for h in range(4):
                            ins = pe.matmul(dvw[:, h, :], lhsT=KD[ci][:nr, h * 128:(h + 1) * 128],
                                            rhs=VV[:nr, ci, h * 128:(h + 1) * 128], start=True, stop=True)
                        return ins
                    S.op("pe", mmd, reads=[b_KD[ci], b_VV[ci]], writes=[bps_d])
                    ps_d_l[ci] = (dvw, bps_d)
                ps_o_l = {}
                for ci, (r0, nr) in CH:
                    ps_o, bps_o = psum()
                    ov = ps_o[:, :].rearrange("p (h v) -> p h v", h=4)

                    def mmo(pe):
                        for h in range(4):
                            pe.matmul(ov[:nr, h, :], lhsT=MM[ci][:nr, h, :nr], rhs=VV[:nr, ci, h * 128:(h + 1) * 128],
                                      start=True, stop=False)
                            ins = pe.matmul(ov[:nr, h, :], lhsT=QD[ci][:, h, :nr], rhs=SB[ci][:, h, :],
                                            start=False, stop=True)
                        return ins
                    S.op("pe", mmo, reads=[b_MM[ci], b_VV[ci], b_QD[ci], b_SB[ci]], writes=[bps_o])
                    ps_o_l[ci] = (ps_o, bps_o)
                    dvw, bps_d = ps_d_l[ci]
                    for h in range(4):
                        S.op("dve", lambda v, h=h: v.scalar_tensor_tensor(
                            out=ST[:, h, :], in0=ST[:, h, :], scalar=float(GAM[h] ** nr), in1=dvw[:, h, :],
                            op0=ALU.mult, op1=ALU.add), reads=[b_ST, bps_d], writes=[b_ST])
                    nsb = (ci + 1) % 4
                    S.op("act", lambda a: a.copy(out=SB[nsb][:], in_=ST[:]), reads=[b_ST], writes=[b_SB[nsb]])
                if mc():
                    return
                group_norm_gate_tile([(ps_o_l[ci][0], ps_o_l[ci][1], nr, ci) for ci, (r0, nr) in CH])
                for ci, (r0, nr) in CH:
                    ret_to_cat(nr, r0, ci, cat, bcat)
            else:
                nr = 80
                ci = 0
                ptv, bpt, qi = qk_transposes(nr, ci)
                S.op("dve", lambda v: v.tensor_tensor(out=QDA[:, :, :80], in0=QKT[qi][:, 0:4, :80], in1=G4[:, :, :],
                                                      op=ALU.mult), reads=[b_QKT[qi]] + CONST, writes=[b_QDA])
                for h in range(4):
                    S.op("dve", lambda v, h=h: v.tensor_scalar(
                        out=KDA[:80, h * 128:(h + 1) * 128], in0=KR[:80, 0, h * 128:(h + 1) * 128],
                        scalar1=KDEC4[:80, h:h + 1], scalar2=None, op0=ALU.mult),
                        reads=[b_KR[0]] + CONST, writes=[b_KDA])
                ps_s, bps_s = psum()
                sv = ps_s[:, :].rearrange("p (h i) -> p h i", h=4)

                def mms(pe):
                    for h in range(4):
                        ins = pe.matmul(sv[:80, h, :80], lhsT=QKT[qi][:, 4 + h, :80], rhs=QKT[qi][:, h, :80],
                                        start=True, stop=True)
                    return ins
                S.op("pe", mms, reads=[b_QKT[qi]], writes=[bps_s])
                mi = rot("mm", 2)
                S.op("dve", lambda v: v.tensor_tensor(out=MM[mi][:80, :, :80], in0=sv[:80, :, :80],
                                                      in1=MASK4[:80, :, :], op=ALU.mult),
                     reads=[bps_s] + CONST, writes=[b_MM[mi]])
                ps_o, bps_o = psum()
                ov = ps_o[:, :].rearrange("p (h v) -> p h v", h=4)

                def mmo(pe):
                    for h in range(4):
                        ins = pe.matmul(ov[:80, h, :], lhsT=MM[mi][:80, h, :80], rhs=VV[:80, 0, h * 128:(h + 1) * 128],
                                        start=(h == 0), stop=False, skip_group_check=True)
                    return ins
                S.op("pe", mmo, reads=[b_MM[mi], b_VV[0]], writes=[bps_o])
                sbi = 0
                PF = 2

                def load_state(bb):
                    fi_ = bb % 3
                    S.dma_sp(lambda q: q.dma_start(out=SF[fi_][:], in_=sret[bb].rearrange("h d v -> d h v")),
                             writes=[b_SF[fi_]])
                    S.op("act", lambda a: a.copy(out=SBB[fi_][:], in_=SF[fi_][:]), reads=[b_SF[fi_]],
                         writes=[b_SBB[fi_]])
                for bb in range(PF):
                    load_state(bb)

                def blk_prep(b):
                    qb = rot("qdb", 2)
                    S.op("dve", lambda v: v.tensor_tensor(
                        out=QDB[qb][:, :, :80], in0=QDA[:, :, :80],
                        in1=BM[:, b, :].unsqueeze(1).broadcast_to([128, 4, 80]), op=ALU.mult),
                        reads=[b_QDA] + CONST, writes=[b_QDB[qb]])
                    kb = rot("kdb", 2)
                    S.op("dve", lambda v: v.tensor_scalar(
                        out=KDB[kb][:80, :], in0=KDA[:80, :], scalar1=BMT[:80, b:b + 1], scalar2=None, op0=ALU.mult),
                        reads=[b_KDA] + CONST, writes=[b_KDB[kb]])
                    return qb, kb
                preps = {0: blk_prep(0)}
                for b in range(17):
                    cexp = 4 if b < 16 else 16
                    if b + PF < 16:
                        load_state(b + PF)
                    if b + 1 < 17:
                        preps[b + 1] = blk_prep(b + 1)
                    if b < 16:
                        fi = b % 3
                        s_f, b_sf, s_b, b_sb = SF[fi], b_SF[fi], SBB[fi], b_SBB[fi]
                    else:
                        s_f, b_sf, s_b, b_sb = ST, b_ST, SB[sbi], b_SB[sbi]
                    qb, kb = preps[b]

                    def mmo2(pe, b=b, qb=qb, s_b=s_b):
                        for h in range(4):
                            ins = pe.matmul(ov[:80, h, :], lhsT=QDB[qb][:, h, :80], rhs=s_b[:, h, :],
                                            start=False, stop=(b == 16), skip_group_check=True)
                        return ins
                    S.op("pe", mmo2, reads=[b_QDB[qb], b_sb], writes=[bps_o])
                    ps_d, bps_d = psum(exclude=bps_o)
                    dvw = ps_d[:, :].rearrange("p (h v) -> p h v", h=4)

                    def mmd(pe, kb=kb, dvw=dvw):
                        for h in range(4):
                            ins = pe.matmul(dvw[:, h, :], lhsT=KDB[kb][:80, h * 128:(h + 1) * 128],
                                            rhs=VV[:80, 0, h * 128:(h + 1) * 128], start=True, stop=True)
                        return ins
                    S.op("pe", mmd, reads=[b_KDB[kb], b_VV[0]], writes=[bps_d])
                    ni = rot("sn", 2)
                    for h in range(4):
                        S.op("dve", lambda v, h=h, ni=ni, s_f=s_f, dvw=dvw: v.scalar_tensor_tensor(
                            out=SN[ni][:, h, :], in0=s_f[:, h, :], scalar=float(GAM[h] ** cexp), in1=dvw[:, h, :],
                            op0=ALU.mult, op1=ALU.add), reads=[b_sf, bps_d], writes=[b_SN[ni]])
                    if b < 16:
                        S.dma_sp(lambda q, b=b, ni=ni: q.dma_start(out=nrs[b].rearrange("h d v -> d h v"), in_=SN[ni][:]),
                                 reads=[b_SN[ni]], is_out=True)
                    else:
                        S.dma_sp(lambda q, ni=ni: q.dma_start(out=nrp.rearrange("h d v -> d h v"), in_=SN[ni][:]),
                                 reads=[b_SN[ni]], is_out=True)
                group_norm_gate_tile([(ps_o, bps_o, nr, 0)])
                ret_to_cat(nr, 0, 0, cat, bcat)
            for dt_ in range(2):
                W, bW = w_acquire()
                for ci, (r0, nr) in enumerate(tile.chunks):
                    p, bp = psum()

                    def mm(pe):
                        for k in range(8):
                            ins = pe.matmul(p[:nr, :], lhsT=cat[:, k, r0:r0 + nr], rhs=W[:, k, :],
                                            start=(k == 0), stop=(k == 7))
                        return ins
                    S.op("pe", mm, reads=[bW, bcat[ci]], writes=[bp])
                    cs = slice(dt_ * 512, (dt_ + 1) * 512)
                    S.op("dve", lambda v: v.scalar_tensor_tensor(out=tile.Xc[ci][:nr, cs], in0=tile.Xc[ci][:nr, cs],
                                                                 scalar=ALPHA, in1=p[:nr, :], op0=ALU.mult, op1=ALU.add),
                         reads=[tile.bX[ci], bp], writes=[tile.bX[ci]])
                    if dt_ == 0:
                        early_stats(tile, ci, nr)

        def prep_x0(tile, xti):
            for ci, (r0, nr) in enumerate(tile.chunks):
                ti = rot("tmp", NTMP)
                load_x0(tile, ci, TMP[ti], b_TMP[ti])
                to_xt(TMP[ti][:nr, :], b_TMP[ti], nr, r0, tile.XTs[xti], tile.bXTs[xti])

        def t4_prologue():
            S.dma_sp(lambda q: q.dma_start(out=nps[:, 0:11, :], in_=spool[:, 4:15, :]), is_out=True)
            for half in range(2):
                tsp = rot("tmp", NTMP)
                S.dma_sp(lambda q, half=half, tsp=tsp: q.dma_start(
                    out=TMP[tsp][0:120, 0:512], in_=spool[half * 8:(half + 1) * 8].rearrange("s r c -> (s r) c")),
                    writes=[b_TMP[tsp]])
                for g in range(4):
                    p, bp = psum()
                    S.op("pe", lambda pe, tsp=tsp, g=g, p=p: pe.matmul(
                        p[:, 0:120], lhsT=TMP[tsp][0:120, g * 128:(g + 1) * 128], rhs=IDF[0:120, 0:120],
                        start=True, stop=True), reads=[b_TMP[tsp]] + CONST, writes=[bp])
                    S.op("act", lambda a, half=half, g=g, p=p: a.copy(
                        out=SPT[:, g, half * 8:(half + 1) * 8, :],
                        in_=p[:, 0:120].rearrange("p (s r) -> p s r", s=8)), reads=[bp], writes=[b_SPT])

        kstop = int(os.environ.get("KSTOP", "999"))
        stage = [0]

        def reached():
            stage[0] += 1
            return stage[0] >= kstop

        def ln3_hook(prev):
            def hook(fg):
                if prev is not None and fg < len(prev.chunks):
                    layer_norm_tile(prev, None, store=True, only=[fg])
                if fg == 4:
                    load_ln(0)
            return hook

        def tile_round(tile, next_tiles, prev=None):
            m = tile.idx
            rope_i = m % 2
            S.dma_sp(lambda q: q.dma_start(out=ROPE[rope_i][:], in_=t_rope[m]), writes=[b_ROPE[rope_i]])
            if prev is not None:
                load_ln(2)
            ffn([(tile, 0, True)], fg_hook=ln3_hook(prev))
            layer_norm_tile(tile, 1)
            load_ln(1)
            mixer(tile, 1, 0, rope_i)
            for nt_ in next_tiles:
                prep_x0(nt_, 0)
            layer_norm_tile(tile, 1)
            ffn([(tile, 1, False)])

        def last_round(t3, t4, prev):
            S.dma_sp(lambda q: q.dma_start(out=ROPE[1][:], in_=t_rope[3]), writes=[b_ROPE[1]])
            load_ln(2)
            ffn([(t3, 0, True), (t4, 0, True)], fg_hook=ln3_hook(prev))
            layer_norm_tile(t3, 1)
            layer_norm_tile(t4, 1)
            S.dma_sp(lambda q: q.dma_start(out=ROPE[0][:], in_=t_rope[4]), writes=[b_ROPE[0]])
            t4_prologue()
            load_ln(1)
            mixer(t3, 1, 0, 1)
            layer_norm_tile(t3, 1)
            mixer(t4, 1, 0, 0)
            layer_norm_tile(t4, 1)
            load_ln(2)
            ffn([(t3, 1, False), (t4, 1, False)])
            layer_norm_tile(t3, None, store=True)
            layer_norm_tile(t4, None, store=True)

        def main():
            if kstop == 0:
                return
            prep_x0(tiles[0], 0)
            tile_round(tiles[0], [tiles[1]])
            tile_round(tiles[1], [tiles[2]], prev=tiles[0])
            tile_round(tiles[2], [tiles[3], tiles[4]], prev=tiles[1])
            last_round(tiles[3], tiles[4], tiles[2])

        main()
        if os.environ.get("KDUMP") == "1":
            for ci in range(4):
                store_out(tiles[0], ci)
        for tok in S.out_toks:
            S.wait("sp", tok)
        for i in range(N_SP_SEMS):
            if S.sp_cnt[i] > 0:
                S.wait("sp", Tok(S.sp_sems[i], S.sp_cnt[i]))
        for st_ in wsem + [pwsem]:
            if st_[1] > 0:
                S.wait("sp", Tok(st_[0], st_[1]))
        for e in ("pe", "act", "dve", "pool"):
            if S.cnt[e] > 0:
                S.wait("sp", Tok(S.sem[e], S.cnt[e]))
    return nc


_CACHE = {}


def kernel(x_prompt, x_sample, state_ret, state_pool, meta_tokens,
           ffn1_w_gate, ffn1_w_up, ffn1_w_down, ln1_g, ln1_b, w_in, pool_w, pool_scale, w_out,
           ln2_g, ln2_b, ffn2_w_gate, ffn2_w_up, ffn2_w_down, ln3_g, ln3_b):
    f = lambda a: np.ascontiguousarray(np.asarray(a, dtype=np.float32))
    if "nc" not in _CACHE:
        _CACHE["nc"] = build_program()
        _CACHE["tabs"] = _host_tables()
    nc = _CACHE["nc"]
    tabs = _CACHE["tabs"]
    shared = {
        "meta": f(meta_tokens),
        "w1g": f(ffn1_w_gate[0]), "w1u": f(ffn1_w_up[0]), "w1d": f(ffn1_w_down[0]),
        "w2g": f(ffn2_w_gate[0]), "w2u": f(ffn2_w_up[0]), "w2d": f(ffn2_w_down[0]),
        "win": f(w_in[0]), "wout": f(w_out[0]), "poolw": f(pool_w[0]),
        "pscale": f(np.asarray(pool_scale[0]).reshape(4, 128).T),
        "ln1g": f(ln1_g), "ln1b": f(ln1_b), "ln2g": f(ln2_g), "ln2b": f(ln2_b), "ln3g": f(ln3_g), "ln3b": f(ln3_b),
    }
    shared.update(tabs)
    xpf, xsf, srf, spf = f(x_prompt), f(x_sample), f(state_ret), f(state_pool)
    in_maps = []
    for c in range(8):
        d = dict(shared)
        d["xp"] = xpf[c]
        d["xs"] = np.ascontiguousarray(xsf[16 * c:16 * c + 16].reshape(64, D))
        d["sret"] = np.ascontiguousarray(srf[0, 16 * c:16 * c + 16])
        d["spool"] = np.ascontiguousarray(spf[0, 16 * c:16 * c + 16])
        in_maps.append(d)
    res = run_bass_kernel_spmd(nc, in_maps, core_ids=list(range(8)))
    R = res.results
    y_prompt = np.stack([R[c]["yp"] for c in range(8)], 0).astype(np.float32)
    y_sample = np.concatenate([R[c]["ys"].reshape(16, 4, D) for c in range(8)], 0).astype(np.float32)
    new_ret_p = np.stack([R[c]["nrp"] for c in range(8)], 0)[None].astype(np.float32)
    new_pool_p = np.stack([R[c]["npp"] for c in range(8)], 0)[None].astype(np.float32)
    new_ret_s = np.concatenate([R[c]["nrs"] for c in range(8)], 0)[None].astype(np.float32)
    new_pool_s = np.concatenate([R[c]["nps"] for c in range(8)], 0)[None].astype(np.float32)
    return (y_prompt, y_sample, new_ret_p, new_pool_p, new_ret_s, new_pool_s)
```
